# Optimizing a Trainium2 kernel written in Bass

```python
import math
import jax, jax.numpy as jnp
from jax import lax
import numpy as np

D_MODEL = 2048
BATCH = 1
SEQ = 8192
DEPTH = 4

N_A = DEPTH // 2
N_B = DEPTH - N_A

HEAD_DIM = 128
DN_HEADS = 12
DN_WIDTH = DN_HEADS * HEAD_DIM
CONV_K = 4
CHUNK = 64
DT_MIN = 0.001
DT_MAX = 0.1

MEM_LEN = 256
MEM_HEADS = 4
MEM_WIDTH = MEM_HEADS * HEAD_DIM

DIL_GROUPS = ((128, 1), (512, 4), (2048, 16))
DIL_HEADS_PER_GROUP = 4
DIL_HEADS = DIL_HEADS_PER_GROUP * len(DIL_GROUPS)
DIL_WIDTH = DIL_HEADS * HEAD_DIM
DIL_OUT = DIL_HEADS_PER_GROUP * HEAD_DIM
DIL_BLOCK = 128

D_FF = ((8 * D_MODEL // 3 + 127) // 128) * 128
A_IN = 4 * DN_WIDTH + 2 * DN_HEADS + MEM_WIDTH
B_IN = DIL_WIDTH + MEM_WIDTH
NORM_EPS = 1e-6

kernel_name = 'yoco_deltanet_dilated_hybrid'


def rmsnorm(x, gain):
    x32 = x.astype(jnp.float32)
    y = x32 * lax.rsqrt(jnp.mean(x32 * x32, axis=-1, keepdims=True) + NORM_EPS)
    return (y * gain.astype(jnp.float32)).astype(x.dtype)


def l2norm(x):
    return x * lax.rsqrt(jnp.sum(x * x, axis=-1, keepdims=True) + NORM_EPS)


def swiglu(x, w_in, w_out):
    gate, up = jnp.split(x @ w_in, 2, axis=-1)
    return (jax.nn.silu(gate) * up) @ w_out


def causal_depthwise_conv(x, w):
    c = x.shape[-1]
    return lax.conv_general_dilated(
        x, w[:, None, :].astype(x.dtype), window_strides=(1,), padding=[(CONV_K - 1, 0)],
        dimension_numbers=('NWC', 'WIO', 'NWC'), feature_group_count=c)


def memory_kv(mem, gain, w):
    b, m, _ = mem.shape
    k, v = jnp.split(rmsnorm(mem, gain) @ w, 2, axis=-1)
    return k.reshape(b, m, MEM_HEADS, HEAD_DIM), v.reshape(b, m, MEM_HEADS, HEAD_DIM)


def memory_attention(q, mem_k, mem_v):
    s = jnp.einsum('bshd,bmhd->bhsm', q.astype(jnp.float32), mem_k.astype(jnp.float32)) * (HEAD_DIM ** -0.5)
    p = jax.nn.softmax(s, axis=-1)
    return jnp.einsum('bhsm,bmhd->bshd', p, mem_v.astype(jnp.float32)).astype(q.dtype)


def chunk_gated_delta_rule(q, k, v, g, beta):
    b_, s_, h_, d = q.shape
    n = s_ // CHUNK

    def to_chunks(t):
        t = t.reshape((b_, n, CHUNK) + t.shape[2:])
        return jnp.moveaxis(t, 2, 3)

    q, k, v = to_chunks(q), to_chunks(k), to_chunks(v)
    g, beta = to_chunks(g), to_chunks(beta)
    g = jnp.cumsum(g, axis=-1)
    idx = jnp.arange(CHUNK)
    causal = idx[:, None] >= idx[None, :]
    strict = idx[:, None] > idx[None, :]
    decay = jnp.exp(jnp.where(causal, g[..., :, None] - g[..., None, :], -jnp.inf))
    k_beta = k * beta[..., None]
    lower = jnp.where(strict, jnp.einsum('bnhid,bnhjd->bnhij', k_beta, k) * decay, 0.0)
    eye = jnp.eye(CHUNK, dtype=q.dtype)
    rhs = jnp.concatenate([v * beta[..., None], k_beta * jnp.exp(g)[..., None]], axis=-1)
    sol = lax.linalg.triangular_solve(eye + lower, rhs, left_side=True, lower=True)
    u, w = sol[..., :d], sol[..., d:]
    attn = jnp.einsum('bnhid,bnhjd->bnhij', q, k) * decay
    q_decay = q * jnp.exp(g)[..., None]
    k_decay = k * jnp.exp(g[..., -1:] - g)[..., None]
    g_last = jnp.exp(g[..., -1])
    xs = tuple(jnp.moveaxis(t, 1, 0) for t in (q_decay, k_decay, u, w, attn, g_last))

    def step(state, inp):
        qd, kd, u_c, w_c, a_c, gl = inp
        v_new = u_c - jnp.einsum('bhcd,bhde->bhce', w_c, state)
        o = jnp.einsum('bhcd,bhde->bhce', qd, state) + jnp.einsum('bhij,bhje->bhie', a_c, v_new)
        state = state * gl[..., None, None] + jnp.einsum('bhcd,bhce->bhde', kd, v_new)
        return state, o

    state0 = jnp.zeros((b_, h_, d, d), q.dtype)
    _, o = lax.scan(step, state0, xs)
    return jnp.moveaxis(o, 0, 1).transpose(0, 1, 3, 2, 4).reshape(b_, s_, h_, d)


def deltanet_mixer(h, mem_k, mem_v, w_in, conv_w, a_log, dt_bias, o_norm, w_out):
    b_, s_, _ = h.shape
    f32 = jnp.float32
    splits = np.cumsum([3 * DN_WIDTH, DN_WIDTH, DN_HEADS, DN_HEADS]).tolist()
    qkv, z, a, bb, mq = jnp.split(h @ w_in, splits, axis=-1)
    qkv = jax.nn.silu(causal_depthwise_conv(qkv, conv_w))
    q, k, v = [t.reshape(b_, s_, DN_HEADS, HEAD_DIM).astype(f32) for t in jnp.split(qkv, 3, axis=-1)]
    q = l2norm(q) * (HEAD_DIM ** -0.5)
    k = l2norm(k)
    beta = jax.nn.sigmoid(bb.astype(f32))
    g = -jnp.exp(a_log.astype(f32)) * jax.nn.softplus(a.astype(f32) + dt_bias.astype(f32))
    o = chunk_gated_delta_rule(q, k, v, g, beta)
    o = rmsnorm(o, o_norm) * jax.nn.silu(z.reshape(b_, s_, DN_HEADS, HEAD_DIM).astype(f32))
    o_mem = memory_attention(mq.reshape(b_, s_, MEM_HEADS, HEAD_DIM), mem_k, mem_v)
    y = jnp.concatenate([o.reshape(b_, s_, DN_WIDTH).astype(h.dtype), o_mem.reshape(b_, s_, MEM_WIDTH)], axis=-1)
    return y @ w_out


def alibi_slopes(n_heads):
    return jnp.exp2(-8.0 * jnp.arange(1, n_heads + 1, dtype=jnp.float32) / n_heads)


def dilated_group_attention(q, k, v, dil, steps, slopes):
    b_, s_, h_, d = q.shape
    f32 = jnp.float32
    L = s_ // dil
    nb = -(-L // DIL_BLOCK)
    lp = nb * DIL_BLOCK

    def streams(t):
        t = t.astype(f32).reshape(b_, L, dil, h_, d).transpose(0, 2, 1, 3, 4)
        t = jnp.pad(t, ((0, 0), (0, 0), (0, lp - L), (0, 0), (0, 0)))
        return t.reshape(b_, dil, nb, DIL_BLOCK, h_, d)

    def with_prev(t):
        prev = jnp.pad(t, ((0, 0), (0, 0), (1, 0), (0, 0), (0, 0), (0, 0)))[:, :, :nb]
        return jnp.concatenate([prev, t], axis=3)

    qb = streams(q)
    kc, vc = with_prev(streams(k)), with_prev(streams(v))
    s = jnp.einsum('bcnqhd,bcnkhd->bcnhqk', qb, kc) * (d ** -0.5)
    qi = jnp.arange(DIL_BLOCK)[:, None] + DIL_BLOCK
    kj = jnp.arange(2 * DIL_BLOCK)[None, :]
    rel = qi - kj
    key_ok = (jnp.arange(nb)[:, None] * DIL_BLOCK - DIL_BLOCK + kj) >= 0
    valid = ((rel >= 0) & (rel <= steps))[None] & key_ok[:, None, :]
    bias = -slopes[:, None, None] * (rel * dil).astype(f32)
    s = jnp.where(valid[None, None, :, None], s + bias, -jnp.inf)
    m = jnp.max(s, axis=-1, keepdims=True)
    e = jnp.exp(s - m)
    den = jnp.sum(e, axis=-1, keepdims=True)
    o = jnp.einsum('bcnhqk,bcnkhd->bcnqhd', e / den, vc)
    lse = (m + jnp.log(den))[..., 0]
    o = o.reshape(b_, dil, lp, h_, d)[:, :, :L].transpose(0, 2, 1, 3, 4).reshape(b_, s_, h_, d)
    lse = lse.transpose(0, 1, 2, 4, 3).reshape(b_, dil, lp, h_)[:, :, :L].transpose(0, 2, 1, 3).reshape(b_, s_, h_)
    return o, lse


def shared_kv(x, gain, w_kv):
    b_, s_, _ = x.shape
    k, v = jnp.split(rmsnorm(x, gain) @ w_kv, 2, axis=-1)
    return k.reshape(b_, s_, DIL_HEADS, HEAD_DIM), v.reshape(b_, s_, DIL_HEADS, HEAD_DIM)


def dilated_mixer(h, k_sh, v_sh, mem_k, mem_v, w_in, w_out):
    b_, s_, _ = h.shape
    q, mq = jnp.split(h @ w_in, [DIL_WIDTH], axis=-1)
    q = q.reshape(b_, s_, DIL_HEADS, HEAD_DIM)
    slopes = alibi_slopes(DIL_HEADS)
    outs, lses = [], []
    for gi, (win, dil) in enumerate(DIL_GROUPS):
        hs = slice(gi * DIL_HEADS_PER_GROUP, (gi + 1) * DIL_HEADS_PER_GROUP)
        o, lse = dilated_group_attention(q[:, :, hs], k_sh[:, :, hs], v_sh[:, :, hs], dil, win // dil, slopes[hs])
        outs.append(o)
        lses.append(lse)
    wts = jax.nn.softmax(jnp.stack(lses), axis=0)
    o = jnp.einsum('gbsh,gbshd->bshd', wts, jnp.stack(outs)).astype(h.dtype)
    o_mem = memory_attention(mq.reshape(b_, s_, MEM_HEADS, HEAD_DIM), mem_k, mem_v)
    y = jnp.concatenate([o.reshape(b_, s_, DIL_OUT), o_mem.reshape(b_, s_, MEM_WIDTH)], axis=-1)
    return y @ w_out


def setup_inputs(seed: int = 0) -> dict:
    key = jax.random.key(seed)
    ks = jax.random.split(key, 20)
    f32 = jnp.float32

    def normal(k, shape, scale):
        return jax.random.normal(k, shape, f32) * scale

    def gain(k, shape):
        return 1.0 + 0.02 * jax.random.normal(k, shape, f32)

    u = jax.random.uniform(ks[9], (N_A, DN_HEADS), f32)
    dt = jnp.exp(u * (math.log(DT_MAX) - math.log(DT_MIN)) + math.log(DT_MIN))
    return {
        'x': normal(ks[0], (BATCH, SEQ, D_MODEL), 1.0),
        'mem': normal(ks[1], (BATCH, MEM_LEN, D_MODEL), 1.0),
        'norm_gains': gain(ks[2], (DEPTH, 6, D_MODEL)),
        'ffn_w_in': normal(ks[3], (DEPTH, 2, D_MODEL, 2 * D_FF), D_MODEL ** -0.5),
        'ffn_w_out': normal(ks[4], (DEPTH, 2, D_FF, D_MODEL), D_FF ** -0.5),
        'mem_norm_gain': gain(ks[5], (DEPTH, D_MODEL)),
        'w_mem_kv': normal(ks[6], (DEPTH, D_MODEL, 2 * MEM_WIDTH), D_MODEL ** -0.5),
        'dn_w_in': normal(ks[7], (N_A, D_MODEL, A_IN), D_MODEL ** -0.5),
        'dn_conv': normal(ks[8], (N_A, CONV_K, 3 * DN_WIDTH), CONV_K ** -0.5),
        'dn_a_log': jnp.log(jax.random.uniform(ks[10], (N_A, DN_HEADS), f32, 1.0, 16.0)),
        'dn_dt_bias': dt + jnp.log(-jnp.expm1(-dt)),
        'dn_o_norm': gain(ks[11], (N_A, HEAD_DIM)),
        'dn_w_out': normal(ks[12], (N_A, DN_WIDTH + MEM_WIDTH, D_MODEL), (DN_WIDTH + MEM_WIDTH) ** -0.5),
        'kv_norm_gain': gain(ks[13], (D_MODEL,)),
        'w_kv': normal(ks[14], (D_MODEL, 2 * DIL_WIDTH), D_MODEL ** -0.5),
        'dil_w_in': normal(ks[15], (N_B, D_MODEL, B_IN), D_MODEL ** -0.5),
        'dil_w_out': normal(ks[16], (N_B, DIL_OUT + MEM_WIDTH, D_MODEL), (DIL_OUT + MEM_WIDTH) ** -0.5),
    }


def reference(x, mem, norm_gains, ffn_w_in, ffn_w_out, mem_norm_gain, w_mem_kv,
              dn_w_in, dn_conv, dn_a_log, dn_dt_bias, dn_o_norm, dn_w_out,
              kv_norm_gain, w_kv, dil_w_in, dil_w_out):
    k_sh, v_sh = None, None
    for l in range(DEPTH):
        gains = norm_gains[l]
        if l == N_A:
            k_sh, v_sh = shared_kv(x, kv_norm_gain, w_kv)
        x = x + 0.5 * rmsnorm(swiglu(rmsnorm(x, gains[0]), ffn_w_in[l, 0], ffn_w_out[l, 0]), gains[1])
        mem_k, mem_v = memory_kv(mem, mem_norm_gain[l], w_mem_kv[l])
        h = rmsnorm(x, gains[2])
        if l < N_A:
            y = deltanet_mixer(h, mem_k, mem_v, dn_w_in[l], dn_conv[l], dn_a_log[l],
                               dn_dt_bias[l], dn_o_norm[l], dn_w_out[l])
        else:
            i = l - N_A
            y = dilated_mixer(h, k_sh, v_sh, mem_k, mem_v, dil_w_in[i], dil_w_out[i])
        x = x + rmsnorm(y, gains[3])
        x = x + 0.5 * rmsnorm(swiglu(rmsnorm(x, gains[4]), ffn_w_in[l, 1], ffn_w_out[l, 1]), gains[5])
    return x
```

```python
import math
import numpy as np
import concourse.bass as bass
import concourse.mybir as mybir
from concourse.bass_utils import run_bass_kernel_spmd

F32 = mybir.dt.float32
BF16 = mybir.dt.bfloat16
AF = mybir.ActivationFunctionType
ALU = mybir.AluOpType

D = 2048
KT = D // 128
SEQ = 8192
NCORES = 8
TOK = SEQ // NCORES
DFF = 5504
FT = DFF // 128
EPS = 1e-6


class Buf:
    __slots__ = ("name", "last_w", "readers")

    def __init__(self, name):
        self.name = name
        self.last_w = None
        self.readers = []


class Instr:
    __slots__ = ("eng", "fn", "dma", "deps", "signals", "sig_idx", "sem", "semval", "idx")

    def __init__(self, eng, fn, dma):
        self.eng = eng
        self.fn = fn
        self.dma = dma
        self.deps = []
        self.signals = False
        self.sig_idx = 0
        self.sem = None
        self.semval = 0
        self.idx = 0


ENGS = ("tensor", "vector", "scalar", "gpsimd", "sync")
N_DMA_SEMS = 6


class Prog:
    def __init__(self, nc):
        self.nc = nc
        self.streams = {e: [] for e in ENGS}
        self.n = 0

    def op(self, eng, fn, reads=(), writes=(), dma=False):
        ins = Instr(eng, fn, dma)
        ins.idx = self.n
        self.n += 1
        deps = {}
        for b in reads:
            if b.last_w is not None:
                deps[id(b.last_w)] = b.last_w
        for b in writes:
            if b.last_w is not None:
                deps[id(b.last_w)] = b.last_w
            for r in b.readers:
                deps[id(r)] = r
        deps.pop(id(ins), None)
        ins.deps = list(deps.values())
        for b in reads:
            if not dma:
                b.readers = [r for r in b.readers if r.dma or r.eng != eng]
            b.readers.append(ins)
        for b in writes:
            b.last_w = ins
            b.readers = []
        self.streams[eng].append(ins)
        return ins

    def mm(self, reads, writes, out, lhsT, rhs, start=True, stop=True):
        return self.op("tensor", ("matmul", dict(out=out, lhsT=lhsT, rhs=rhs, start=start, stop=stop)), reads, writes)

    def tr(self, reads, writes, out, in_, identity):
        return self.op("tensor", ("transpose", dict(out=out, in_=in_, identity=identity)), reads, writes)

    def dve(self, meth, reads, writes, **kw):
        return self.op("vector", (meth, kw), reads, writes)

    def act(self, reads, writes, out, in_, func, **kw):
        return self.op("scalar", ("activation", dict(out=out, in_=in_, func=func, **kw)), reads, writes)

    def pool(self, meth, reads, writes, **kw):
        return self.op("gpsimd", (meth, kw), reads, writes)

    def dma(self, eng, reads, writes, out, in_):
        return self.op(eng, ("dma_start", dict(out=out, in_=in_)), reads, writes, dma=True)

    def emit(self, final_wait=()):
        nc = self.nc
        def need_sync(ins, d):
            if d.dma:
                return True
            if d.eng == ins.eng:
                if ins.eng == "tensor" and not ins.dma:
                    return False
                return True
            return True

        for e in ENGS:
            for ins in self.streams[e]:
                for d in ins.deps:
                    if not d.dma and need_sync(ins, d):
                        d.signals = True
        import contextlib
        with contextlib.ExitStack() as st:
            esem = {e: st.enter_context(nc.semaphore("s_" + e)) for e in ENGS}
            dsem = {e: [st.enter_context(nc.semaphore("d_%s%d" % (e, i))) for i in range(N_DMA_SEMS)]
                    for e in ("gpsimd", "sync", "scalar")}
            for e in ENGS:
                c = 0
                k = 0
                for ins in self.streams[e]:
                    if ins.dma:
                        ins.sem = dsem[e][k % N_DMA_SEMS]
                        ins.semval = 16 * (k // N_DMA_SEMS + 1)
                        k += 1
                    elif ins.signals:
                        c += 1
                        ins.sig_idx = c
            block = st.enter_context(nc.Block())
            streams = self.streams

            def run(e, eng):
                seen = {}
                k = 0
                prev_on_sem = {}
                for ins in streams[e]:
                    waits = {}
                    for d in ins.deps:
                        if d.dma:
                            key = ("d", id(d.sem))
                            if waits.get(key, (None, 0))[1] < d.semval:
                                waits[key] = (d.sem, d.semval)
                        elif need_sync(ins, d):
                            key = ("e", d.eng)
                            if waits.get(key, (None, 0))[1] < d.sig_idx:
                                waits[key] = (esem[d.eng], d.sig_idx)
                    if ins.dma:
                        key = ("d", id(ins.sem))
                        pv = ins.semval - 16
                        if pv > 0 and waits.get(key, (None, 0))[1] < pv:
                            waits[key] = (ins.sem, pv)
                    for key, (sem, val) in waits.items():
                        if seen.get(key, 0) >= val:
                            continue
                        seen[key] = val
                        eng.wait_ge(sem, val)
                    r = getattr(eng, ins.fn[0])(**ins.fn[1])
                    if ins.dma:
                        r.then_inc(ins.sem, 16)
                    elif ins.signals:
                        r.then_inc(esem[e], 1)
                if e == "sync":
                    for d in final_wait:
                        eng.wait_ge(d.sem, d.semval)

            @block.tensor
            def _(eng):
                run("tensor", eng)

            @block.vector
            def _(eng):
                run("vector", eng)

            @block.scalar
            def _(eng):
                run("scalar", eng)

            @block.gpsimd
            def _(eng):
                run("gpsimd", eng)

            @block.sync
            def _(eng):
                run("sync", eng)


class Pool:
    def __init__(self, tiles):
        self.tiles = tiles
        self.bufs = [Buf("pool") for _ in tiles]
        self.i = 0

    def next(self):
        i = self.i % len(self.tiles)
        self.i += 1
        return self.tiles[i], self.bufs[i]


MEM = 256
DN_W = 1536
A_IN = 4 * DN_W + 24 + 512
SCALE = 128 ** -0.5


class Env:
    pass


class StopBuild(Exception):
    pass


CUT = [0]


def cut(n):
    if CUT[0] == n:
        raise StopBuild()


class DT:
    def __init__(self, ap, nk, name="dt"):
        self.ap = ap
        self.nk = nk
        self.v = ap.rearrange("(k p) t -> p k t", p=128)
        self.b = [Buf("%s%d" % (name, k)) for k in range(nk)]

    def tile(self, k):
        return self.v[:, k, :]


def make_env(nc, st, P, T=TOK):
    E = Env()
    E.nc, E.P, E.T, E.st = nc, P, T, st
    E.NH = T // 512
    sb = lambda name, shape, dt: st.enter_context(nc.sbuf_tensor(name, shape, dt))
    E.sb = sb
    E.hT = sb("hT", [128, KT, T], BF16)
    E.hb = [Buf("h%d" % k) for k in range(KT)]
    E.actT = sb("actT", [128, FT, T], BF16)
    E.ab = [Buf("a%d" % f) for f in range(FT)]
    NW = 6
    wt = sb("wts", [128, NW, 2048], BF16)
    E.wpool = Pool([wt[:, i, :] for i in range(NW)])
    rs = sb("rstd", [128, 2, T], F32)
    E.rstds = [rs[:, 0, :], rs[:, 1, :]]
    E.rstd_bs = [Buf("rstd0"), Buf("rstd1")]
    E.ri = 0
    sq = sb("sq", [128, 2, T], BF16)
    E.sqpool = Pool([sq[:, i, :] for i in range(2)])
    tmp = sb("tmpf", [128, 5, 512], F32)
    E.tmppool = Pool([tmp[:, i, :] for i in range(5)])
    xin = sb("xin", [128, 2, T], F32)
    E.xpool = Pool([xin[:, i, :] for i in range(2)])
    E.ones = sb("ones", [128, 128], BF16)
    E.ones_b = Buf("ones")
    E.onef = sb("onef", [1, 1], F32)
    E.rcol = sb("rcol", [128, T // 128], F32)
    E.rcol_b = Buf("rcol")
    E.gains = sb("gains_sb", [128, 14 * KT], F32)
    E.gains_b = Buf("gains")
    stg = sb("stg", [128, 2, 512], F32)
    E.stgpool = Pool([stg[:, i, :] for i in range(2)])
    E.evac_i = 0
    ps = [st.enter_context(nc.psum_tensor("ps%d" % i, [128, 512], F32)) for i in range(8)]
    E.pspool = Pool(ps[0:6])
    E.statbanks = [(ps[6], Buf("stat0")), (ps[7], Buf("stat1"))]
    P.pool("memset", [], [E.ones_b], ap=E.ones[:], constant=1.0)
    P.pool("memset", [], [E.ones_b], ap=E.onef[:], constant=1.0)
    return E


def gcol(E, gidx, k):
    return E.gains[:, gidx * KT + k:gidx * KT + k + 1]


def evac(E, out, in_, reads, writes):
    E.evac_i += 1
    if E.evac_i % 2:
        return E.P.act(reads, writes, out, in_, AF.Copy)
    return E.P.dve("tensor_copy", reads, writes, out=out, in_=in_)


def emit_rstd_finish(E, banks, nfeat, coef, rstd, rstd_b, ncol=512):
    P = E.P
    c2 = coef * coef
    for h, (pt, pb) in enumerate(banks):
        sl = slice(h * ncol, (h + 1) * ncol)
        P.dve("tensor_scalar", [pb], [rstd_b], out=rstd[:, sl], in0=pt[:, 0:ncol], scalar1=1.0 / (nfeat * c2),
              scalar2=EPS / c2, op0=ALU.mult, op1=ALU.add)
    P.act([rstd_b], [rstd_b], rstd, rstd, AF.Ln)
    P.act([rstd_b], [rstd_b], rstd, rstd, AF.Exp, scale=-0.5)


def emit_sumsq(E, banks, src, srcb, k, nk, ncol=512, h_only=None):
    P = E.P
    sq, sqb = E.sqpool.next()
    if h_only is None:
        n = ncol * len(banks)
        P.act([srcb], [sqb], sq[:, 0:n], src, AF.Square)
        for h, (pt, pb) in enumerate(banks):
            P.mm([sqb, E.ones_b], [pb], pt[:, 0:ncol], E.ones[:], sq[:, h * ncol:(h + 1) * ncol],
                 start=(k == 0), stop=(k == nk - 1))
    else:
        pt, pb = banks[h_only]
        P.act([srcb], [sqb], sq[:, 0:ncol], src, AF.Square)
        P.mm([sqb, E.ones_b], [pb], pt[:, 0:ncol], E.ones[:], sq[:, 0:ncol], start=(k == 0), stop=(k == nk - 1))


def emit_norm_in_from_dram(E, X, gidx):
    P = E.P
    banks = E.statbanks
    for k in range(KT):
        xt, xb = E.xpool.next()
        P.dma("sync", [X.b[k]], [xb], xt, X.tile(k))
        emit_sumsq(E, banks, xt, xb, k, KT)
        P.dve("tensor_scalar", [xb, E.gains_b], [E.hb[k]], out=E.hT[:, k, :], in0=xt, scalar1=gcol(E, gidx, k),
              scalar2=None, op0=ALU.mult)
    E.ri ^= 1
    emit_rstd_finish(E, banks, D, 1.0, E.rstds[E.ri], E.rstd_bs[E.ri])


def emit_y_evac(E, py, pyb, j, h, nj):
    P = E.P
    sl = slice(h * 512, (h + 1) * 512)
    P.dve("tensor_copy", [pyb], [E.hb[j]], out=E.hT[:, j, sl], in_=py[:])
    emit_sumsq(E, E.statbanks, E.hT[:, j, sl], E.hb[j], j, nj, h_only=h)


def emit_postnorm_residual(E, X, XO, gidx, coef, next_gidx=None):
    P = E.P
    E.ri ^= 1
    ry, ryb = E.rstds[E.ri], E.rstd_bs[E.ri]
    emit_rstd_finish(E, E.statbanks, D, coef, ry, ryb)
    outs = []
    for k in range(KT):
        xt, xb = E.xpool.next()
        P.dma("sync", [X.b[k]], [xb], xt, X.tile(k))
        for h in range(E.NH):
            sl = slice(h * 512, (h + 1) * 512)
            tt, tb = E.tmppool.next()
            P.dve("scalar_tensor_tensor", [E.hb[k], ryb, E.gains_b], [tb], out=tt, in0=E.hT[:, k, sl],
                  scalar=gcol(E, gidx, k), in1=ry[:, sl], op0=ALU.mult, op1=ALU.mult)
            if h == 0:
                P.pool("tensor_tensor", [tb, xb], [xb], out=xt[:, sl], in0=tt, in1=xt[:, sl], op=ALU.add)
            else:
                P.dve("tensor_tensor", [tb, xb], [xb], out=xt[:, sl], in0=tt, in1=xt[:, sl], op=ALU.add)
        outs.append(P.dma("sync", [xb], [XO.b[k]], XO.tile(k), xt))
        if next_gidx is not None:
            emit_sumsq(E, E.statbanks, xt, xb, k, KT)
            P.dve("tensor_scalar", [xb, E.gains_b], [E.hb[k]], out=E.hT[:, k, :], in0=xt, scalar1=gcol(E, next_gidx, k),
                  scalar2=None, op0=ALU.mult)
    if next_gidx is not None:
        E.ri ^= 1
        emit_rstd_finish(E, E.statbanks, D, 1.0, E.rstds[E.ri], E.rstd_bs[E.ri])
    return outs


def load_wchunk(E, W, nk, c0, ncols):
    assert nk * ncols <= 2048
    wt, wb = E.wpool.next()
    wv = wt[:, 0:nk * ncols].rearrange("p (k n) -> p k n", k=nk)
    Wv = W.rearrange("(k p) n -> p k n", p=128)
    E.P.dma("gpsimd", [], [wb], wv, Wv[:, :, c0:c0 + ncols])
    return wv, wb


def emit_ffn(E, X, XO, w_in, w_out, g_post, next_gidx):
    P = E.P
    r, rb = E.rstds[E.ri], E.rstd_bs[E.ri]
    for f in range(FT):
        wgv, wgb = load_wchunk(E, w_in, KT, f * 128, 128)
        wuv, wub = load_wchunk(E, w_in, KT, DFF + f * 128, 128)
        for h in range(E.NH):
            sl = slice(h * 512, (h + 1) * 512)
            pg, pgb = E.pspool.next()
            pu, pub = E.pspool.next()
            for k in range(KT):
                P.mm([wgb, E.hb[k]], [pgb], pg[:], wgv[:, k, :], E.hT[:, k, sl], start=(k == 0), stop=(k == KT - 1))
            for k in range(KT):
                P.mm([wub, E.hb[k]], [pub], pu[:], wuv[:, k, :], E.hT[:, k, sl], start=(k == 0), stop=(k == KT - 1))
            t1, t1b = E.tmppool.next()
            t3, t3b = E.tmppool.next()
            P.dve("tensor_tensor", [pgb, rb], [t1b], out=t1, in0=pg[:], in1=r[:, sl], op=ALU.mult)
            P.act([t1b], [t1b], t1, t1, AF.Silu)
            P.dve("tensor_tensor", [pub, rb], [t3b], out=t3, in0=pu[:], in1=r[:, sl], op=ALU.mult)
            P.pool("tensor_tensor", [t1b, t3b], [E.ab[f]], out=E.actT[:, f, sl], in0=t1, in1=t3, op=ALU.mult)
    cut(2)
    w_out_v = w_out.rearrange("(f p) n -> p f n", p=128)
    fr = [(0, 16), (16, 32), (32, FT)]
    for j in range(KT):
        slots = []
        for (f0, f1) in fr:
            wt, wb = E.wpool.next()
            wv = wt.rearrange("p (f n) -> p f n", n=128)
            P.dma("gpsimd", [], [wb], wv[:, 0:f1 - f0, :], w_out_v[:, f0:f1, j * 128:(j + 1) * 128])
            slots.append((wv, wb))
        for h in range(E.NH):
            sl = slice(h * 512, (h + 1) * 512)
            py, pyb = E.pspool.next()
            for (wv, wb), (f0, f1) in zip(slots, fr):
                for f in range(f0, f1):
                    P.mm([wb, E.ab[f]], [pyb], py[:], wv[:, f - f0, :], E.actT[:, f, sl],
                         start=(f == 0), stop=(f == FT - 1))
            emit_y_evac(E, py, pyb, j, h, KT)
    cut(3)
    return emit_postnorm_residual(E, X, XO, g_post, 0.5, next_gidx)


def emit_lin_fm(E, W, nk, in_tiles, in_bufs, c0, consumer, ncols=128):
    P = E.P
    wv, wb = load_wchunk(E, W, nk, c0, ncols)
    for h in range(E.NH):
        sl = slice(h * 512, (h + 1) * 512)
        pt, pb = E.pspool.next()
        for k in range(nk):
            P.mm([wb, in_bufs[k]], [pb], pt[0:ncols, :], wv[:, k, :], in_tiles[k][:, sl],
                 start=(k == 0), stop=(k == nk - 1))
        consumer(pt, pb, h)


def evac_rstd(E, out, pt, pb, h, writes):
    sl = slice(h * 512, (h + 1) * 512)
    return E.P.dve("tensor_tensor", [pb, E.rstd_bs[E.ri]], writes, out=out, in0=pt[:], in1=E.rstds[E.ri][:, sl], op=ALU.mult)


def emit_lin_fm_to_dram(E, W, nk, in_tiles, in_bufs, c0, OUT, row0, outs):
    def cons(pt, pb, h):
        stg, sb_ = E.stgpool.next()
        evac_rstd(E, stg, pt, pb, h, [sb_])
        outs.append(E.P.dma("sync", [sb_], [], OUT[row0:row0 + 128, h * 512:(h + 1) * 512], stg))
    emit_lin_fm(E, W, nk, in_tiles, in_bufs, c0, cons)


def emit_rstd_cols(E):
    P = E.P
    r, rb = E.rstds[E.ri], E.rstd_bs[E.ri]
    pt, pb = E.pspool.next()
    nt = E.T // 128
    for tt in range(nt):
        P.mm([rb, E.ones_b], [pb], pt[:, tt:tt + 1], r[0:1, tt * 128:(tt + 1) * 128], E.onef[:])
    P.dve("tensor_copy", [pb], [E.rcol_b], out=E.rcol[:], in_=pt[:, 0:nt])


def emit_lin_tm_to_dram(E, W, nk, in_tiles, in_bufs, c0, ncols, OUT, ocol0, outs, T=None):
    P = E.P
    T = T or E.T
    nch = ncols // 128
    wvs = [load_wchunk(E, W, nk, c0 + i * 128, 128) for i in range(nch)]
    for tt in range(T // 128):
        pt, pb = E.pspool.next()
        for i, (wv, wb) in enumerate(wvs):
            for k in range(nk):
                P.mm([wb, in_bufs[k]], [pb], pt[:, i * 128:(i + 1) * 128], in_tiles[k][:, tt * 128:(tt + 1) * 128],
                     wv[:, k, :], start=(k == 0), stop=(k == nk - 1))
        stg, sb_ = E.stgpool.next()
        P.dve("tensor_scalar", [pb, E.rcol_b], [sb_], out=stg[:, 0:ncols], in0=pt[:, 0:ncols], scalar1=E.rcol[:, tt:tt + 1],
              scalar2=None, op0=ALU.mult)
        outs.append(P.dma("sync", [sb_], [], OUT[tt * 128:(tt + 1) * 128, ocol0:ocol0 + ncols], stg[:, 0:ncols]))


def emit_mem_kv(E, memT, w_mkv, gidx):
    P = E.P
    mh = E.actT[:, 0:4, :].rearrange("p a (b m) -> p (a b) m", m=MEM)
    mhb = [E.ab[k // 4] for k in range(KT)]
    E.memKT = E.sb("memKT", [128, 4, MEM], BF16)
    E.memKT_b = Buf("memKT")
    E.memV = E.sb("memV", [128, 2, 512], BF16)
    E.memV_b = Buf("memV")
    rstd_m = E.sb("rstd_m", [128, MEM], F32)
    rmb = Buf("rstd_m")
    mv = memT.rearrange("(k p) t -> p k t", p=128)
    banks = [E.pspool.next()]
    for k in range(KT):
        xt, xb = E.xpool.next()
        P.dma("sync", [], [xb], xt[:, 0:MEM], mv[:, k, :])
        emit_sumsq(E, banks, xt[:, 0:MEM], xb, k, KT, ncol=MEM)
        P.dve("tensor_scalar", [xb, E.gains_b], [mhb[k]], out=mh[:, k, :], in0=xt[:, 0:MEM], scalar1=gcol(E, gidx, k),
              scalar2=None, op0=ALU.mult)
    emit_rstd_finish(E, banks, D, 1.0, rstd_m[:], rmb, ncol=MEM)
    rmcol = E.sb("rmcol", [128, 2], F32)
    pc, pcb = E.pspool.next()
    for mt in range(2):
        P.mm([rmb, E.ones_b], [pcb], pc[:, mt:mt + 1], rstd_m[0:1, mt * 128:(mt + 1) * 128], E.onef[:])
    P.dve("tensor_copy", [pcb], [rmb], out=rmcol[:], in_=pc[:, 0:2])
    for hd in range(4):
        wv, wb = load_wchunk(E, w_mkv, KT, hd * 128, 128)
        pt, pb = E.pspool.next()
        for k in range(KT):
            P.mm([wb, mhb[k]], [pb], pt[:, 0:MEM], wv[:, k, :], mh[:, k, :], start=(k == 0), stop=(k == KT - 1))
        P.dve("tensor_tensor", [pb, rmb], [E.memKT_b], out=E.memKT[:, hd, :], in0=pt[:, 0:MEM], in1=rstd_m[:], op=ALU.mult)
    for c in range(4):
        wv, wb = load_wchunk(E, w_mkv, KT, 512 + c * 128, 128)
        for mt in range(2):
            pt, pb = E.pspool.next()
            for k in range(KT):
                P.mm([wb, mhb[k]], [pb], pt[:, 0:128], mh[:, k, mt * 128:(mt + 1) * 128], wv[:, k, :],
                     start=(k == 0), stop=(k == KT - 1))
            P.dve("tensor_scalar", [pb, rmb], [E.memV_b], out=E.memV[:, mt, c * 128:(c + 1) * 128], in0=pt[:, 0:128],
                  scalar1=rmcol[:, mt:mt + 1], scalar2=None, op0=ALU.mult)


def emit_mem_attn(E, OMEM, outs):
    P = E.P
    et = E.sb("ET", [128, 2, 2, 512], BF16)
    etpool = Pool([et[:, i] for i in range(2)])
    for hd in range(4):
        for h in range(E.NH):
            sl = slice(h * 512, (h + 1) * 512)
            ET, ETb = etpool.next()
            for mt in range(2):
                ps, psb = E.pspool.next()
                P.mm([E.memKT_b, E.mq_b[hd]], [psb], ps[:], E.memKT[:, hd, mt * 128:(mt + 1) * 128], E.mqT[:, hd, sl])
                P.act([psb], [ETb], ET[:, mt, :], ps[:], AF.Exp, scale=SCALE)
            po, pob = E.pspool.next()
            pd, pdb = E.pspool.next()
            for mt in range(2):
                P.mm([E.memV_b, ETb], [pob], po[:], E.memV[:, mt, hd * 128:(hd + 1) * 128], ET[:, mt, :],
                     start=(mt == 0), stop=(mt == 1))
            for mt in range(2):
                P.mm([E.ones_b, ETb], [pdb], pd[:], E.ones[:], ET[:, mt, :], start=(mt == 0), stop=(mt == 1))
            tt, tb = E.tmppool.next()
            P.dve("reciprocal", [pdb], [tb], out=tt, in_=pd[:])
            stg, sb_ = E.stgpool.next()
            P.dve("tensor_tensor", [tb, pob], [sb_], out=stg, in0=po[:], in1=tt, op=ALU.mult)
            outs.append(P.dma("sync", [sb_], [], OMEM[hd * 128:(hd + 1) * 128, sl], stg))


def build_ts(prev, nxt, has_kv):
    import contextlib
    nc = bass.Bass("TRN2", target_bir_lowering=False)
    T = TOK
    di = lambda name, shape: nc.dram_tensor(name, shape, F32, kind="ExternalInput").ap()
    do = lambda name, shape: nc.dram_tensor(name, shape, F32, kind="ExternalOutput").ap()
    dint = lambda name, shape: nc.dram_tensor(name, shape, F32, kind="Internal").ap()
    x_in = di("x_in", [D, T])
    gains = di("gains", [128, 14 * KT])
    P = Prog(nc)
    outs = []
    with contextlib.ExitStack() as st:
        E = make_env(nc, st, P)
        P.dma("sync", [], [E.gains_b], E.gains[:], gains)
        X = DT(x_in, KT, "xin")
        hin = [E.hT[:, k, :] for k in range(KT)]
        try:
            _build_ts_body(nc, E, P, X, hin, prev, nxt, has_kv, outs, di, do, dint, T)
        except StopBuild:
            pass
        P.emit(final_wait=outs)
    return nc


def _build_ts_body(nc, E, P, X, hin, prev, nxt, has_kv, outs, di, do, dint, T):
    if True:
        if prev is not None:
            nko = 16 if prev == "dn" else 8
            o_in = di("o_in", [nko * 128, T])
            w_o = di("w_o", [nko * 128, D])
            f2_in = di("f2_in", [D, 2 * DFF])
            f2_out = di("f2_out", [DFF, D])
            ov = o_in.rearrange("(k p) t -> p k t", p=128)
            for k in range(nko):
                P.dma("gpsimd", [], [E.ab[k]], E.actT[:, k, :], ov[:, k, :])
            ain = [E.actT[:, k, :] for k in range(nko)]
            for j in range(KT):
                def cons(pt, pb, h, j=j):
                    emit_y_evac(E, pt, pb, j, h, KT)
                emit_lin_fm(E, w_o, nko, ain, E.ab, j * 128, cons)
            X2 = DT(dint("x2", [D, T]), KT, "x2")
            emit_postnorm_residual(E, X, X2, 3, 1.0, next_gidx=4)
            if nxt is None:
                X3 = DT(do("x_out", [D, T]), KT, "x3")
            else:
                X3 = DT(dint("x3", [D, T]), KT, "x3")
            o3 = emit_ffn(E, X2, X3, f2_in, f2_out, 5, (None if nxt is None else (13 if has_kv else 6)))
            if nxt is None:
                outs += o3
            X = X3
        if nxt is not None:
            f1_in = di("f1_in", [D, 2 * DFF])
            f1_out = di("f1_out", [DFF, D])
            memT = di("memT", [D, MEM])
            w_mkv = di("w_mkv", [D, 1024])
            if prev is None:
                emit_norm_in_from_dram(E, X, 6)
                cut(1)
            if has_kv:
                w_kv = di("w_kv", [D, 3072])
                kT_o = do("kT", [1536, T])
                v_o = do("v", [T, 1536])
                emit_rstd_cols(E)
                for c in range(12):
                    emit_lin_fm_to_dram(E, w_kv, KT, hin, E.hb, c * 128, kT_o, c * 128, outs)
                for c in range(3):
                    emit_lin_tm_to_dram(E, w_kv, KT, hin, E.hb, 1536 + c * 512, 512, v_o, c * 512, outs)
                emit_norm_in_from_dram(E, X, 6)
            X1 = DT(do("x_out", [D, T]), KT, "x1")
            outs += emit_ffn(E, X, X1, f1_in, f1_out, 7, 8)
            cut(4)
            E.mqT = E.sb("mqT", [128, 4, T], BF16)
            E.mq_b = [Buf("mq%d" % i) for i in range(4)]
            omem = do("omemT", [512, T])
            if nxt == "dn":
                w_p = di("w_p", [D, A_IN])
                qkvT = do("qkvT", [4608, T])
                z_o = do("z", [T, 1536])
                gb_o = do("gb", [T, 24])
                dnc = di("dnc", [128, 24])
                mq0 = 6168
                emit_rstd_cols(E)
                cut(5)
                for c in range(36):
                    emit_lin_fm_to_dram(E, w_p, KT, hin, E.hb, c * 128, qkvT, c * 128, outs)
                for c in range(3):
                    emit_lin_tm_to_dram(E, w_p, KT, hin, E.hb, 4608 + c * 512, 512, z_o, c * 512, outs)
                cut(6)
                dn_c = E.sb("dnc_sb", [128, 24], F32)
                dn_cb = Buf("dnc")
                nea = E.sb("nea", [128, 12], F32)
                neab = Buf("nea")
                gt = E.sb("gt", [128, 2, 24], F32)
                gtpool = Pool([gt[:, i, :] for i in range(2)])
                P.dma("sync", [], [dn_cb], dn_c[:], dnc)
                P.act([dn_cb], [neab], nea[:], dn_c[:, 12:24], AF.Exp)
                P.dve("tensor_scalar", [neab], [neab], out=nea[:], in0=nea[:], scalar1=-1.0, scalar2=None, op0=ALU.mult)
                wv, wb = load_wchunk(E, w_p, KT, 6144, 24)
                for tt in range(T // 128):
                    pt, pb = E.pspool.next()
                    for k in range(KT):
                        P.mm([wb, E.hb[k]], [pb], pt[:, 0:24], E.hT[:, k, tt * 128:(tt + 1) * 128], wv[:, k, :],
                             start=(k == 0), stop=(k == KT - 1))
                    g1, g1b = gtpool.next()
                    stg, sb_ = E.stgpool.next()
                    P.dve("tensor_scalar", [pb, E.rcol_b], [g1b], out=g1[:, 0:24], in0=pt[:, 0:24], scalar1=E.rcol[:, tt:tt + 1],
                          scalar2=None, op0=ALU.mult)
                    P.act([g1b], [sb_], stg[:, 12:24], g1[:, 12:24], AF.Sigmoid)
                    P.dve("tensor_tensor", [g1b, dn_cb], [g1b], out=g1[:, 0:12], in0=g1[:, 0:12], in1=dn_c[:, 0:12], op=ALU.add)
                    P.act([g1b], [g1b], g1[:, 0:12], g1[:, 0:12], AF.Exp)
                    P.dve("tensor_scalar", [g1b], [g1b], out=g1[:, 0:12], in0=g1[:, 0:12], scalar1=1.0, scalar2=None, op0=ALU.add)
                    P.act([g1b], [g1b], g1[:, 0:12], g1[:, 0:12], AF.Ln)
                    P.dve("tensor_tensor", [g1b, neab, sb_], [sb_], out=stg[:, 0:12], in0=g1[:, 0:12], in1=nea[:], op=ALU.mult)
                    outs.append(P.dma("sync", [sb_], [], gb_o[tt * 128:(tt + 1) * 128, :], stg[:, 0:24]))
            else:
                w_p = di("w_p", [D, 2048])
                qT = do("qT", [1536, T])
                mq0 = 1536
                for c in range(12):
                    emit_lin_fm_to_dram(E, w_p, KT, hin, E.hb, c * 128, qT, c * 128, outs)
            for hd in range(4):
                def cons(pt, pb, h, hd=hd):
                    evac_rstd(E, E.mqT[:, hd, h * 512:(h + 1) * 512], pt, pb, h, [E.mq_b[hd]])
                emit_lin_fm(E, w_p, KT, hin, E.hb, mq0 + hd * 128, cons)
            cut(8)
            emit_mem_kv(E, memT, w_mkv, 12)
            cut(9)
            emit_mem_attn(E, omem, outs)


def gains_layout(g):
    G = g.shape[0]
    return np.ascontiguousarray(g.reshape(G, KT, 128).transpose(2, 0, 1).reshape(128, G * KT))


CH = 64
GC = 8
GT = CH * GC


def bc_last(ap, n):
    return bass.AP(ap.tensor, ap.offset, [list(x) for x in ap.ap] + [[0, n]])


def bc_mid(ap, n):
    a = [list(x) for x in ap.ap]
    return bass.AP(ap.tensor, ap.offset, [a[0], [0, n]] + a[1:])


def build_dn(S=SEQ, HPC=2, IDT=BF16, stage=99):
    import contextlib
    nc = bass.Bass("TRN2", target_bir_lowering=False)
    NCH = S // CH
    NG = S // GT
    di = lambda name, shape: nc.dram_tensor(name, shape, F32, kind="ExternalInput").ap()
    qkv = di("qkv", [HPC, 3, 128, S])
    convw = di("convw", [128, HPC * 3 * 4])
    zc = di("zc", [CH, NCH, HPC, 128])
    gcol = di("gcol", [CH, HPC, NCH])
    bcol = di("bcol", [CH, HPC, NCH])
    onorm = di("onorm", [CH, 128])
    consts = di("consts", [CH, 5, CH])
    o_out = nc.dram_tensor("o", [CH, NCH, HPC, 128], F32, kind="ExternalOutput").ap()
    P = Prog(nc)
    outs = []
    with contextlib.ExitStack() as st:
        sb = lambda name, shape, dt: st.enter_context(nc.sbuf_tensor(name, shape, dt))
        ps = [st.enter_context(nc.psum_tensor("ps%d" % i, [128, 512], F32)) for i in range(8)]
        pspool = Pool(ps)
        cst = sb("cst", [CH, 5, CH], F32)
        cst_b = Buf("cst")
        P.dma("sync", [], [cst_b], cst[:], consts)
        tri, tris, ident, strict, inclT = [cst[:, i, :] for i in range(5)]
        ones = sb("ones32", [128, 128], F32)
        ones_b = Buf("ones")
        P.pool("memset", [], [ones_b], ap=ones[:], constant=1.0)
        ident128 = sb("ident128", [128, 128], F32)
        id_b = Buf("id128")
        P.pool("memset", [], [id_b], ap=ident128[:], constant=0.0)
        P.dma("sync", [id_b], [id_b], ident128[0:CH, 0:CH], consts[:, 2, :])
        P.dma("sync", [id_b], [id_b], ident128[CH:128, CH:128], consts[:, 2, :])
        identb = sb("identb", [CH, CH], BF16)
        P.dve("tensor_copy", [cst_b], [cst_b], out=identb[:], in_=cst[:, 2, :])
        cw = sb("cw", [128, HPC * 12], F32)
        cw_b = Buf("cw")
        P.dma("sync", [], [cw_b], cw[:], convw)
        dg = sb("dg", [128, HPC * 12, 128], BF16)
        dg_b = Buf("dg")
        for col in range(HPC * 12):
            (P.dve if col % 2 else P.pool)("tensor_scalar", [id_b, cw_b, dg_b], [dg_b], out=dg[:, col, :], in0=ident128[:],
                                           scalar1=cw[:, col:col + 1], scalar2=None, op0=ALU.mult)
        on = sb("on", [CH, 128], F32)
        on_b = Buf("on")
        P.dma("sync", [], [on_b], on[:], onorm)
        NA = HPC * NCH
        g_all = sb("g_all", [CH, NA], F32)
        b_all = sb("b_all", [CH, NA], F32)
        gc_all = sb("gc_all", [CH, NA], F32)
        egc = sb("egc", [CH, NA], F32)
        bg = sb("bg", [CH, NA], F32)
        ekd = sb("ekd", [CH, NA], F32)
        egl = sb("egl", [128, NA], F32)
        gate_b = Buf("gates")
        P.dma("sync", [], [gate_b], g_all[:], gcol.rearrange("p h n -> p (h n)"))
        P.dma("sync", [gate_b], [gate_b], b_all[:], bcol.rearrange("p h n -> p (h n)"))
        strict_x = sb("strict_x", [CH, GC, CH], F32)
        inclT_x = sb("inclT_x", [CH, GC, CH], F32)
        ident_x = sb("ident_x", [CH, GC, CH], F32)
        mx_b = Buf("maskx")
        P.dve("tensor_copy", [cst_b], [mx_b], out=strict_x[:], in_=bc_mid(strict, GC))
        P.dve("tensor_copy", [cst_b, mx_b], [mx_b], out=inclT_x[:], in_=bc_mid(inclT, GC))
        P.dve("tensor_copy", [cst_b, mx_b], [mx_b], out=ident_x[:], in_=bc_mid(ident, GC))
        assert NA <= 512
        p1, p1b = pspool.next()
        P.mm([cst_b, gate_b], [p1b], p1[0:CH, 0:NA], tri, g_all[:])
        P.dve("tensor_copy", [p1b], [gate_b], out=gc_all[:], in_=p1[0:CH, 0:NA])
        p2, p2b = pspool.next()
        P.mm([ones_b, gate_b], [p2b], p2[:, 0:NA], ones[0:CH, :], g_all[:])
        P.act([p2b], [gate_b], egl[:], p2[:, 0:NA], AF.Exp)
        P.act([gate_b], [gate_b], egc[:], gc_all[:], AF.Exp)
        P.dve("tensor_tensor", [gate_b], [gate_b], out=bg[:], in0=b_all[:], in1=egc[:], op=ALU.mult)
        P.dve("tensor_tensor", [gate_b, p2b], [gate_b], out=ekd[:], in0=p2[0:CH, 0:NA], in1=gc_all[:], op=ALU.subtract)
        P.act([gate_b], [gate_b], ekd[:], ekd[:], AF.Exp)

        def mk(name, shape, dt, n):
            t = sb(name, [shape[0], n] + shape[1:], dt)
            return Pool([t[:, i] for i in range(n)])
        PL = []
        for hh in range(HPC):
            s_ = "_%d" % hh
            PL.append(dict(
                raw=mk("raw" + s_, [128, 3, 3 + GT], BF16, 2), cv=mk("cv" + s_, [128, 3, GT], F32, 1),
                sq=mk("sqd" + s_, [128, GT], F32, 2), rs=mk("rsd" + s_, [128, GT], F32, 2),
                KT=mk("KTb" + s_, [128, GT], BF16, 1), KV32=mk("KV32" + s_, [128, 2, GT], F32, 1),
                TG=mk("TG" + s_, [CH, GC, CH], F32, 1), DM=mk("DM" + s_, [CH, GC, CH], F32, 1),
                DTM=mk("DTM" + s_, [CH, GC, CH], F32, 1), LM=mk("LM" + s_, [CH, 2, GC, CH], IDT, 3),
                X=mk("Xc" + s_, [CH, GC, CH], IDT, 1), Kbg=mk("Kbg" + s_, [CH, GC, 128], IDT, 1),
                Vb=mk("Vb" + s_, [CH, GC, 128], IDT, 1),
                QT=mk("QT" + s_, [128, GT], BF16, 2), kd=mk("kd" + s_, [CH, GC, 128], BF16, 2),
                U=mk("U" + s_, [CH, GC, 128], F32, 2), WT=mk("WT" + s_, [128, GC, CH], BF16, 2),
                aT=mk("aT" + s_, [CH, GC, CH], BF16, 2), z=mk("zt" + s_, [CH, GC, 128], F32, 2),
                o=mk("ot" + s_, [CH, GC, 128], F32, 2), vn=mk("vn" + s_, [CH, 128], BF16, 2),
                t1=mk("t1" + s_, [CH, 128], F32, 2), st=mk("stat" + s_, [CH, 2, GC], F32, 2),
                sq2=mk("sq2" + s_, [CH, GC, 128], F32, 1), oT=mk("oTt" + s_, [128, GT], F32, 2),
            ))
        S32 = sb("S32", [128, HPC, 128], F32)
        Sbf = sb("Sbf", [128, HPC, 128], BF16)
        S_b = [Buf("S%d" % h) for h in range(HPC)]
        Sbf_b = [Buf("Sbf%d" % h) for h in range(HPC)]
        for hh in range(HPC):
            P.pool("memset", [], [S_b[hh]], ap=S32[:, hh, :], constant=0.0)
            P.pool("memset", [], [Sbf_b[hh]], ap=Sbf[:, hh, :], constant=0.0)

        def prep(hh, gi, H):
            pl = PL[hh]
            t0 = gi * GT
            c0 = hh * NCH + gi * GC
            raw, rawb = pl["raw"].next()
            if gi == 0:
                P.pool("memset", [], [rawb], ap=raw[:, :, 0:3], constant=0.0)
                P.dma("gpsimd", [rawb], [rawb], raw[:, :, 3:3 + GT], qkv[hh, :, :, 0:GT].rearrange("w p t -> p w t"))
            else:
                P.dma("gpsimd", [], [rawb], raw[:], qkv[hh, :, :, t0 - 3:t0 + GT].rearrange("w p t -> p w t"))
            zt, ztb = pl["z"].next()
            P.dma("sync", [], [ztb], zt, zc[:, gi * GC:(gi + 1) * GC, hh, :])
            cv, cvb = pl["cv"].next()
            for w in range(3):
                col = (hh * 3 + w) * 4
                pc, pcb = pspool.next()
                for j in range(4):
                    P.mm([rawb, dg_b], [pcb], pc[:], dg[:, col + j, :], raw[:, w, j:j + GT], start=(j == 0), stop=(j == 3))
                P.act([pcb], [cvb], cv[:, w, :], pc[:], AF.Silu)
                yield
            P.act([ztb], [ztb], zt, zt, AF.Silu)
            P.pool("tensor_tensor", [ztb, on_b], [ztb], out=zt, in0=zt, in1=bc_mid(on[:], GC), op=ALU.mult)
            yield
            QT, QTb = pl["QT"].next()
            KTt, KTb = pl["KT"].next()
            KV32, KV32b = pl["KV32"].next()
            for w in range(2):
                sq, sqb = pl["sq"].next()
                P.act([cvb], [sqb], sq, cv[:, w, :], AF.Square)
                pn, pnb = pspool.next()
                P.mm([sqb, ones_b], [pnb], pn[:], ones[:], sq)
                rs, rsb = pl["rs"].next()
                P.dve("tensor_scalar", [pnb], [rsb], out=rs, in0=pn[:], scalar1=EPS, scalar2=None, op0=ALU.add)
                P.act([rsb], [rsb], rs, rs, AF.Ln)
                P.act([rsb], [rsb], rs, rs, AF.Exp, scale=-0.5)
                if w == 0:
                    P.dve("scalar_tensor_tensor", [cvb, rsb], [QTb], out=QT, in0=cv[:, 0, :], scalar=SCALE, in1=rs,
                          op0=ALU.mult, op1=ALU.mult)
                else:
                    P.dve("tensor_tensor", [cvb, rsb], [KV32b], out=KV32[:, 0, :], in0=cv[:, 1, :], in1=rs, op=ALU.mult)
                    P.act([KV32b], [KTb], KTt, KV32[:, 0, :], AF.Copy)
                yield
            P.pool("tensor_copy", [cvb, KV32b], [KV32b], out=KV32[:, 1, :], in_=cv[:, 2, :])
            TG, TGb = pl["TG"].next()
            for c in range(GC):
                P.pool("tensor_scalar", [cst_b, gate_b], [TGb], out=TG[:, c, :], in0=tri,
                       scalar1=g_all[:, c0 + c:c0 + c + 1], scalar2=None, op0=ALU.mult)
            pdt, pdtb = pspool.next()
            P.mm([cst_b, TGb], [pdtb], pdt[0:CH, :], tris, TG.rearrange("p c i -> p (c i)"))
            pd, pdb = pspool.next()
            for c in range(GC):
                P.mm([cst_b, TGb], [pdb], pd[0:CH, c * CH:(c + 1) * CH], TG[:, c, :], tris)
            DM, DMb = pl["DM"].next()
            DTM, DTMb = pl["DTM"].next()
            P.act([pdb], [DMb], DM.rearrange("p c i -> p (c i)"), pd[0:CH, :], AF.Exp)
            P.pool("tensor_tensor", [DMb, mx_b], [DMb], out=DM, in0=DM, in1=strict_x[:], op=ALU.mult)
            P.act([pdtb], [DTMb], DTM.rearrange("p c i -> p (c i)"), pdt[0:CH, :], AF.Exp)
            P.pool("tensor_tensor", [DTMb, mx_b], [DTMb], out=DTM, in0=DTM, in1=inclT_x[:], op=ALU.mult)
            yield
            pkk, pkkb = pspool.next()
            pqk, pqkb = pspool.next()
            for c in range(GC):
                cs = slice(c * CH, (c + 1) * CH)
                P.mm([KTb], [pkkb], pkk[0:CH, cs], KTt[:, cs], KTt[:, cs])
            for c in range(GC):
                cs = slice(c * CH, (c + 1) * CH)
                P.mm([KTb, QTb], [pqkb], pqk[0:CH, cs], KTt[:, cs], QT[:, cs])
            aT, aTb = pl["aT"].next()
            P.dve("tensor_tensor", [pqkb, DTMb], [aTb], out=aT.rearrange("p c i -> p (c i)"), in0=pqk[0:CH, :],
                  in1=DTM.rearrange("p c i -> p (c i)"), op=ALU.mult)
            LM, LMb = pl["LM"].next()
            L0, M0 = LM[:, 0], LM[:, 1]
            for c in range(GC):
                P.dve("scalar_tensor_tensor", [pkkb, DMb, gate_b], [LMb], out=L0[:, c, :], in0=pkk[0:CH, c * CH:(c + 1) * CH],
                      scalar=b_all[:, c0 + c:c0 + c + 1], in1=DM[:, c, :], op0=ALU.mult, op1=ALU.mult)
            ptr, ptrb = pspool.next()
            idl = ident if IDT == F32 else identb[:]
            for c in range(GC):
                P.mm([LMb, cst_b], [ptrb], ptr[0:CH, c * CH:(c + 1) * CH], L0[:, c, :], idl)
            P.act([ptrb], [LMb], M0.rearrange("p c i -> p (c i)"), ptr[0:CH, :], AF.Copy)
            yield
            X, Xb = pl["X"].next()
            P.dve("tensor_tensor", [LMb, mx_b], [Xb], out=X, in0=ident_x[:], in1=M0, op=ALU.subtract)
            Lc, Mc, LMcb = L0, M0, LMb
            for r in range(5):
                LMn, LMnb = pl["LM"].next()
                Ln, Mn = LMn[:, 0], LMn[:, 1]
                pl_, plb = pspool.next()
                pm, pmb = pspool.next()
                for c in range(GC):
                    P.mm([LMcb], [plb], pl_[0:CH, c * CH:(c + 1) * CH], Mc[:, c, :], Lc[:, c, :])
                for c in range(GC):
                    P.mm([LMcb], [pmb], pm[0:CH, c * CH:(c + 1) * CH], Lc[:, c, :], Mc[:, c, :])
                P.act([plb], [LMnb], Ln.rearrange("p c i -> p (c i)"), pl_[0:CH, :], AF.Copy)
                P.dve("tensor_copy", [pmb], [LMnb], out=Mn.rearrange("p c i -> p (c i)"), in_=pm[0:CH, :])
                px, pxb = pspool.next()
                for c in range(GC):
                    P.mm([LMnb, Xb], [pxb], px[0:CH, c * CH:(c + 1) * CH], Ln[:, c, :], X[:, c, :])
                P.dve("tensor_tensor", [pxb, Xb], [Xb], out=X.rearrange("p c i -> p (c i)"), in0=px[0:CH, :],
                      in1=X.rearrange("p c i -> p (c i)"), op=ALU.add)
                Lc, Mc, LMcb = Ln, Mn, LMnb
                yield
            Kbg, Kbgb = pl["Kbg"].next()
            kd, kdb = pl["kd"].next()
            Vb, Vbb = pl["Vb"].next()
            for half in range(2):
                pk, pkb = pspool.next()
                pv, pvb = pspool.next()
                for c4 in range(4):
                    c = half * 4 + c4
                    cs = slice(c * CH, (c + 1) * CH)
                    P.mm([KV32b, id_b], [pkb], pk[0:CH, c4 * 128:(c4 + 1) * 128], KV32[:, 0, cs], ident128[:])
                    P.mm([KV32b, id_b], [pvb], pv[0:CH, c4 * 128:(c4 + 1) * 128], KV32[:, 1, cs], ident128[:])
                hs = slice(half * 4, half * 4 + 4)
                gs = slice(c0 + half * 4, c0 + half * 4 + 4)
                pk3 = pk[0:CH, :].rearrange("p (c d) -> p c d", d=128)
                pv3 = pv[0:CH, :].rearrange("p (c d) -> p c d", d=128)
                P.dve("tensor_tensor", [pkb, gate_b], [Kbgb], out=Kbg[:, hs, :], in0=pk3, in1=bc_last(bg[:, gs], 128), op=ALU.mult)
                P.dve("tensor_tensor", [pkb, gate_b], [kdb], out=kd[:, hs, :], in0=pk3, in1=bc_last(ekd[:, gs], 128), op=ALU.mult)
                P.dve("tensor_tensor", [pvb, gate_b], [Vbb], out=Vb[:, hs, :], in0=pv3, in1=bc_last(b_all[:, gs], 128), op=ALU.mult)
                yield
            if IDT == F32:
                Xm, Xmb = X, Xb
            else:
                Xm, Xmb = X, Xb
            U, Ub = pl["U"].next()
            WT, WTb = pl["WT"].next()
            for half in range(2):
                pu, pub = pspool.next()
                pw, pwb = pspool.next()
                for c4 in range(4):
                    c = half * 4 + c4
                    P.mm([Xmb, Vbb], [pub], pu[0:CH, c4 * 128:(c4 + 1) * 128], Xm[:, c, :], Vb[:, c, :])
                for c4 in range(4):
                    c = half * 4 + c4
                    P.mm([Xmb, Kbgb], [pwb], pw[:, c4 * CH:(c4 + 1) * CH], Kbg[:, c, :], Xm[:, c, :])
                P.act([pub], [Ub], U[:, half * 4:half * 4 + 4, :].rearrange("p c d -> p (c d)"), pu[0:CH, :], AF.Copy)
                P.dve("tensor_copy", [pwb], [WTb], out=WT[:, half * 4:half * 4 + 4, :].rearrange("p c i -> p (c i)"),
                      in_=pw[:, 0:4 * CH])
                yield
            H[(hh, gi)] = dict(QT=QT, QTb=QTb, aT=aT, aTb=aTb, kd=kd, kdb=kdb, U=U, Ub=Ub, WT=WT, WTb=WTb, zt=zt, ztb=ztb)

        def rec(hh, gi, H):
            pl = PL[hh]
            h = H.pop((hh, gi))
            QT, QTb, aT, aTb, kd, kdb = h["QT"], h["QTb"], h["aT"], h["aTb"], h["kd"], h["kdb"]
            U, Ub, WT, WTb, zt, ztb = h["U"], h["Ub"], h["WT"], h["WTb"], h["zt"], h["ztb"]
            c0 = hh * NCH + gi * GC
            ot, otb = pl["o"].next()
            for c in range(GC):
                col = c0 + c
                cs = slice(c * CH, (c + 1) * CH)
                pws, pwsb = pspool.next()
                P.mm([WTb, Sbf_b[hh]], [pwsb], pws[0:CH, 0:128], WT[:, c, :], Sbf[:, hh, :])
                vn, vnb = pl["vn"].next()
                P.dve("tensor_tensor", [Ub, pwsb], [vnb], out=vn, in0=U[:, c, :], in1=pws[0:CH, 0:128], op=ALU.subtract)
                P.mm([QTb, Sbf_b[hh]], [pwsb], pws[0:CH, 128:256], QT[:, cs], Sbf[:, hh, :])
                P.mm([aTb, vnb], [pwsb], pws[0:CH, 256:384], aT[:, c, :], vn)
                pss, pssb = pspool.next()
                P.mm([kdb, vnb], [pssb], pss[:, 0:128], kd[:, c, :], vn)
                P.dve("scalar_tensor_tensor", [pssb, S_b[hh], gate_b], [S_b[hh]], out=S32[:, hh, :], in0=S32[:, hh, :],
                      scalar=egl[:, col:col + 1], in1=pss[:, 0:128], op0=ALU.mult, op1=ALU.add)
                P.act([S_b[hh]], [Sbf_b[hh]], Sbf[:, hh, :], S32[:, hh, :], AF.Copy)
                t1, t1b = pl["t1"].next()
                P.pool("tensor_scalar", [pwsb, gate_b], [t1b], out=t1, in0=pws[0:CH, 128:256], scalar1=egc[:, col:col + 1],
                       scalar2=None, op0=ALU.mult) if False else P.dve(
                    "tensor_scalar", [pwsb, gate_b], [t1b], out=t1, in0=pws[0:CH, 128:256], scalar1=egc[:, col:col + 1],
                    scalar2=None, op0=ALU.mult)
                P.dve("tensor_tensor", [t1b, pwsb], [otb], out=ot[:, c, :], in0=t1, in1=pws[0:CH, 256:384], op=ALU.add)
                yield
            stt, sttb = pl["st"].next()
            sq2, sq2b = pl["sq2"].next()
            P.pool("tensor_tensor", [otb], [sq2b], out=sq2, in0=ot, in1=ot, op=ALU.mult)
            P.dve("tensor_reduce", [sq2b], [sttb], out=stt[:, 0, :], in_=sq2, axis=mybir.AxisListType.X, op=ALU.add)
            P.dve("tensor_scalar", [sttb], [sttb], out=stt[:, 1, :], in0=stt[:, 0, :], scalar1=1.0 / 128, scalar2=EPS,
                  op0=ALU.mult, op1=ALU.add)
            P.act([sttb], [sttb], stt[:, 1, :], stt[:, 1, :], AF.Sqrt)
            P.dve("reciprocal", [sttb], [sttb], out=stt[:, 1, :], in_=stt[:, 1, :])
            P.pool("tensor_tensor", [otb, sttb], [otb], out=ot, in0=ot, in1=bc_last(stt[:, 1, :], 128), op=ALU.mult)
            P.pool("tensor_tensor", [otb, ztb], [otb], out=ot, in0=ot, in1=zt, op=ALU.mult)
            outs.append(P.dma("sync", [otb], [], o_out[:, gi * GC:(gi + 1) * GC, hh, :], ot))
            yield

        def interleave(gens):
            gens = list(gens)
            while gens:
                for g_ in list(gens):
                    try:
                        next(g_)
                    except StopIteration:
                        gens.remove(g_)

        H = {}
        interleave([prep(hh, 0, H) for hh in range(HPC)])
        for gi in range(NG):
            gens = [rec(hh, gi, H) for hh in range(HPC)]
            if gi + 1 < NG:
                gens += [prep(hh, gi + 1, H) for hh in range(HPC)]
            interleave(gens)
        P.emit(final_wait=outs)
    return nc


def dn_consts():
    i = np.arange(CH)
    tri = (i[:, None] <= i[None, :])
    tris = (i[:, None] > i[None, :])
    ident = np.eye(CH, dtype=bool)
    strict = (i[:, None] > i[None, :])
    inclT = (i[None, :] >= i[:, None])
    return np.ascontiguousarray(np.stack([tri, tris, ident, strict, inclT], 1).astype(np.float32))


DILS = (1, 4, 16)
HALF = SEQ // 2
HALO = 2048
NVB = 48


def build_dil():
    import contextlib
    nc = bass.Bass("TRN2", target_bir_lowering=False)
    di = lambda name, shape: nc.dram_tensor(name, shape, F32, kind="ExternalInput").ap()
    q_in = di("q", [3, 128, HALF])
    k_in = di("k", [3, 128, HALO + HALF])
    v_in = di("vblk", [128, 3, NVB, 128])
    bias_in = di("bias", [128, 3, 384])
    o_out = nc.dram_tensor("oT", [128, HALF], F32, kind="ExternalOutput").ap()
    P = Prog(nc)
    with contextlib.ExitStack() as st:
        sb = lambda name, shape, dt: st.enter_context(nc.sbuf_tensor(name, shape, dt))
        ps = [st.enter_context(nc.psum_tensor("ps%d" % i, [128, 512], F32)) for i in range(8)]
        pspool = Pool(ps)
        qT = sb("qT", [128, 3, HALF], BF16)
        kT = sb("kT", [128, 3, HALO + HALF], BF16)
        vb = sb("vb", [128, 3, NVB, 128], BF16)
        bias = sb("bias_sb", [128, 3, 384], F32)
        ones = sb("ones", [128, 128], BF16)
        accO = sb("accO", [128, HALF], F32)
        accD = sb("accD", [128, HALF], F32)
        qb, kb, vbb, bb, ob, accb = [[Buf("q%d" % g) for g in range(3)], [Buf("k%d" % g) for g in range(3)],
                                     [Buf("v%d" % g) for g in range(3)], Buf("bias"), Buf("ones"), Buf("acc")]
        for g in range(3):
            P.dma("gpsimd", [], [qb[g]], qT[:, g, :], q_in[g])
            P.dma("gpsimd", [], [kb[g]], kT[:, g, :], k_in[g])
            P.dma("gpsimd", [], [vbb[g]], vb[:, g], v_in[:, g])
        P.dma("sync", [], [bb], bias[:], bias_in)
        P.pool("memset", [], [ob], ap=ones[:], constant=1.0)
        tmp_t = sb("tmp", [128, 2, 256], F32)
        tmp_p = Pool([tmp_t[:, i, :] for i in range(2)])
        E_t = sb("Et", [128, 3, 256], BF16)
        E_p = Pool([E_t[:, i, :] for i in range(3)])

        def strided(t3, g, start, d, n):
            base = t3[:, g, start:start + 1]
            a = [list(x) for x in base.ap]
            return bass.AP(base.tensor, base.offset, [a[0], [d, n]])

        for g, d in enumerate(DILS):
            nstream = d
            nblk = HALF // (128 * d)
            vi = 0
            for c in range(nstream):
                po = pd = None
                for m in range(-1, nblk):
                    kcol = HALO + 128 * m * d + c
                    K_ap = strided(kT, g, kcol, d, 128)
                    if m == -1:
                        q0, nq, boff = c, 128, 256
                    elif m == nblk - 1:
                        q0, nq, boff = 128 * m * d + c, 128, 0
                    else:
                        q0, nq, boff = 128 * m * d + c, 256, 0
                    Q_ap = strided(qT, g, q0, d, nq)
                    pss, pssb = pspool.next()
                    P.mm([kb[g], qb[g]], [pssb], pss[:, 0:nq], K_ap, Q_ap)
                    tt, tb = tmp_p.next()
                    P.dve("scalar_tensor_tensor", [pssb, bb], [tb], out=tt[:, 0:nq], in0=pss[:, 0:nq], scalar=SCALE,
                          in1=bias[:, g, boff:boff + nq], op0=ALU.mult, op1=ALU.add)
                    Et, Eb = E_p.next()
                    P.act([tb], [Eb], Et[:, 0:nq], tt[:, 0:nq], AF.Exp)
                    V_ap = vb[:, g, vi, :]
                    vi += 1
                    if m >= 0:
                        P.mm([vbb[g], Eb], [pob], po[:, 0:128], V_ap, Et[:, 0:128], start=False, stop=True)
                        P.mm([ob, Eb], [pdb], pd[:, 0:128], ones[:], Et[:, 0:128], start=False, stop=True)
                        cols = strided(accO, None, 0, 1, 1) if False else None
                        oc = 128 * m * d + c
                        aO = bass.AP(accO[:, oc:oc + 1].tensor, accO[:, oc:oc + 1].offset,
                                     [list(accO[:, oc:oc + 1].ap[0]), [d, 128]])
                        aD = bass.AP(accD[:, oc:oc + 1].tensor, accD[:, oc:oc + 1].offset,
                                     [list(accD[:, oc:oc + 1].ap[0]), [d, 128]])
                        if g == 0:
                            P.act([pob], [accb], aO, po[:, 0:128], AF.Copy)
                            P.dve("tensor_copy", [pdb], [accb], out=aD, in_=pd[:, 0:128])
                        else:
                            P.dve("tensor_tensor", [pob, accb], [accb], out=aO, in0=aO, in1=po[:, 0:128], op=ALU.add)
                            P.dve("tensor_tensor", [pdb, accb], [accb], out=aD, in0=aD, in1=pd[:, 0:128], op=ALU.add)
                    if m < nblk - 1:
                        po, pob = pspool.next()
                        pd, pdb = pspool.next()
                        e0 = 0 if m == -1 else 128
                        P.mm([vbb[g], Eb], [pob], po[:, 0:128], V_ap, Et[:, e0:e0 + 128], start=True, stop=False)
                        P.mm([ob, Eb], [pdb], pd[:, 0:128], ones[:], Et[:, e0:e0 + 128], start=True, stop=False)
        outs = []
        for h in range(HALF // 512):
            sl = slice(h * 512, (h + 1) * 512)
            P.dve("reciprocal", [accb], [accb], out=accD[:, sl], in_=accD[:, sl])
            P.dve("tensor_tensor", [accb], [accb], out=accO[:, sl], in0=accO[:, sl], in1=accD[:, sl], op=ALU.mult)
            outs.append(P.dma("sync", [accb], [], o_out[:, sl], accO[:, sl]))
        P.emit(final_wait=outs)
    return nc


def alibi_slopes():
    return np.exp2(-8.0 * np.arange(1, 13, dtype=np.float32) / 12).astype(np.float32)


def dil_bias(j, s):
    sl = alibi_slopes()
    kq = np.arange(256)[None, :] - np.arange(128)[:, None]
    valid = (kq >= 0) & (kq <= 128)
    out = np.full((128, 3, 384), -1e30, np.float32)
    for g, d in enumerate(DILS):
        b = np.where(valid, -sl[4 * g + j] * (kq * d).astype(np.float32), np.float32(-1e30)).astype(np.float32)
        out[:, g, 0:256] = b
        if s > 0:
            out[:, g, 256:384] = b[:, 128:256]
    return out


def dil_layout(q, k, v, j, s):
    t0 = s * HALF
    qs = np.stack([q[t0:t0 + HALF, 4 * g + j, :].T for g in range(3)])
    kext = np.zeros((3, 128, HALO + HALF), np.float32)
    vblk = np.zeros((128, 3, NVB, 128), np.float32)
    for g, d in enumerate(DILS):
        h = 4 * g + j
        lo = t0 - HALO
        src_lo = max(lo, 0)
        kext[g][:, src_lo - lo:] = k[src_lo:t0 + HALF, h, :].T
        nblk = HALF // (128 * d)
        vi = 0
        for c in range(d):
            for m in range(-1, nblk):
                tok = t0 + (128 * m + np.arange(128)) * d + c
                if tok[0] >= 0:
                    vblk[:, g, vi, :] = v[tok, h, :]
                vi += 1
    return {"q": np.ascontiguousarray(qs), "k": kext, "vblk": vblk, "bias": dil_bias(j, s)}


_NC_CACHE = {}


def _prog(key, fn):
    if key not in _NC_CACHE:
        _NC_CACHE[key] = fn()
    return _NC_CACHE[key]


def _run(nc, in_maps):
    res = run_bass_kernel_spmd(nc, in_maps, core_ids=list(range(len(in_maps))))
    return res.results


def kernel(x, mem, norm_gains, ffn_w_in, ffn_w_out, mem_norm_gain, w_mem_kv, dn_w_in, dn_conv, dn_a_log,
           dn_dt_bias, dn_o_norm, dn_w_out, kv_norm_gain, w_kv, dil_w_in, dil_w_out):
    f32 = np.float32
    x = np.asarray(x, f32)
    mem = np.asarray(mem, f32)
    A = lambda a: np.ascontiguousarray(np.asarray(a, f32))
    norm_gains, ffn_w_in, ffn_w_out = A(norm_gains), A(ffn_w_in), A(ffn_w_out)
    mem_norm_gain, w_mem_kv, dn_w_in, dn_conv = A(mem_norm_gain), A(w_mem_kv), A(dn_w_in), A(dn_conv)
    dn_a_log, dn_dt_bias, dn_o_norm, dn_w_out = A(dn_a_log), A(dn_dt_bias), A(dn_o_norm), A(dn_w_out)
    kv_norm_gain, w_kv, dil_w_in, dil_w_out = A(kv_norm_gain), A(w_kv), A(dil_w_in), A(dil_w_out)
    kinds = ["dn", "dn", "dil", "dil"]
    memT = np.ascontiguousarray(mem[0].T)
    xs = [np.ascontiguousarray(x[0, i * TOK:(i + 1) * TOK].T) for i in range(NCORES)]
    consts = dn_consts()
    o_in = None
    ksh = vsh = None
    for stage in range(5):
        prev = kinds[stage - 1] if stage > 0 else None
        nxt = kinds[stage] if stage < 4 else None
        has_kv = (stage == 2)
        nc = _prog(("ts", prev, nxt, has_kv), lambda: build_ts(prev, nxt, has_kv))
        g14 = np.zeros((14, D), f32)
        common = {}
        if prev is not None:
            lp = stage - 1
            g14[0:6] = norm_gains[lp]
            common.update(w_o=(dn_w_out[lp] if prev == "dn" else dil_w_out[lp - 2]),
                          f2_in=ffn_w_in[lp, 1], f2_out=ffn_w_out[lp, 1])
        if nxt is not None:
            ln = stage
            g14[6:12] = norm_gains[ln]
            g14[12] = mem_norm_gain[ln]
            g14[13] = kv_norm_gain
            common.update(f1_in=ffn_w_in[ln, 0], f1_out=ffn_w_out[ln, 0], memT=memT, w_mkv=w_mem_kv[ln])
            if nxt == "dn":
                common.update(w_p=dn_w_in[ln],
                              dnc=np.ascontiguousarray(np.tile(np.concatenate([dn_dt_bias[ln], dn_a_log[ln]])[None], (128, 1))))
            else:
                common.update(w_p=dil_w_in[ln - 2])
            if has_kv:
                common.update(w_kv=w_kv)
        common["gains"] = gains_layout(g14)
        ims = []
        for i in range(NCORES):
            im = dict(common)
            im["x_in"] = xs[i]
            if prev is not None:
                im["o_in"] = o_in[i]
            ims.append(im)
        res = _run(nc, ims)
        xs = [r["x_out"] for r in res]
        if nxt is None:
            break
        omem = [r["omemT"] for r in res]
        if nxt == "dn":
            ln = stage
            qkvT = np.concatenate([r["qkvT"] for r in res], axis=1)
            zf = np.concatenate([r["z"] for r in res], axis=0)
            gb = np.concatenate([r["gb"] for r in res], axis=0)
            NCH = SEQ // CH
            q4 = qkvT.reshape(3, 12, 128, SEQ)
            z4 = zf.reshape(NCH, CH, 12, 128)
            g3 = gb[:, 0:12].reshape(NCH, CH, 12)
            b3 = gb[:, 12:24].reshape(NCH, CH, 12)
            cw = dn_conv[ln].reshape(4, 3, 12, 128)
            ncd = _prog(("dn",), lambda: build_dn())
            ims = []
            for c in range(NCORES):
                h0 = 2 * (c % 6)
                ims.append({
                    "qkv": np.ascontiguousarray(q4[:, h0:h0 + 2].transpose(1, 0, 2, 3)),
                    "convw": np.ascontiguousarray(cw[:, :, h0:h0 + 2, :].transpose(3, 2, 1, 0).reshape(128, 24)),
                    "zc": np.ascontiguousarray(z4[:, :, h0:h0 + 2].transpose(1, 0, 2, 3)),
                    "gcol": np.ascontiguousarray(g3[:, :, h0:h0 + 2].transpose(1, 2, 0)),
                    "bcol": np.ascontiguousarray(b3[:, :, h0:h0 + 2].transpose(1, 2, 0)),
                    "onorm": np.ascontiguousarray(np.tile(dn_o_norm[ln][None], (CH, 1))),
                    "consts": consts,
                })
            res = _run(ncd, ims)
            of = np.zeros((SEQ, 12, 128), f32)
            for c in range(6):
                of[:, 2 * c:2 * c + 2] = res[c]["o"].transpose(1, 0, 2, 3).reshape(SEQ, 2, 128)
            of = of.reshape(SEQ, 1536)
        else:
            qT = np.concatenate([r["qT"] for r in res], axis=1)
            if has_kv:
                ksh = np.ascontiguousarray(np.concatenate([r["kT"] for r in res], axis=1).T).reshape(SEQ, 12, 128)
                vsh = np.concatenate([r["v"] for r in res], axis=0).reshape(SEQ, 12, 128)
            qf = np.ascontiguousarray(qT.T).reshape(SEQ, 12, 128)
            ncl = _prog(("dil",), lambda: build_dil())
            ims = [dil_layout(qf, ksh, vsh, c // 2, c % 2) for c in range(NCORES)]
            res = _run(ncl, ims)
            of = np.zeros((SEQ, 4, 128), f32)
            for c in range(NCORES):
                j, s = c // 2, c % 2
                of[s * HALF:(s + 1) * HALF, j] = res[c]["oT"].T
            of = of.reshape(SEQ, 512)
        o_in = [np.ascontiguousarray(np.concatenate([of[i * TOK:(i + 1) * TOK].T, omem[i]], axis=0)) for i in range(NCORES)]
    out = np.concatenate([xi.T for xi in xs], axis=0)[None]
    return np.ascontiguousarray(out.astype(f32))
```

```python
import math
import numpy as np
import concourse.bass as bass
import concourse.mybir as mybir
from concourse.bass_utils import run_bass_kernel_spmd

F32 = mybir.dt.float32
BF16 = mybir.dt.bfloat16
AF = mybir.ActivationFunctionType
ALU = mybir.AluOpType

D = 2048
KT = D // 128
SEQ = 8192
NCORES = 8
TOK = SEQ // NCORES
DFF = 5504
FT = DFF // 128
EPS = 1e-6


class Buf:
    __slots__ = ("name", "last_w", "readers")

    def __init__(self, name):
        self.name = name
        self.last_w = None
        self.readers = []


class Instr:
    __slots__ = ("eng", "fn", "dma", "deps", "signals", "sig_idx", "sem", "semval", "idx")

    def __init__(self, eng, fn, dma):
        self.eng = eng
        self.fn = fn
        self.dma = dma
        self.deps = []
        self.signals = False
        self.sig_idx = 0
        self.sem = None
        self.semval = 0
        self.idx = 0


ENGS = ("tensor", "vector", "scalar", "gpsimd", "sync")
N_DMA_SEMS = 6


class Prog:
    def __init__(self, nc):
        self.nc = nc
        self.streams = {e: [] for e in ENGS}
        self.n = 0

    def op(self, eng, fn, reads=(), writes=(), dma=False):
        ins = Instr(eng, fn, dma)
        ins.idx = self.n
        self.n += 1
        deps = {}
        for b in reads:
            if b.last_w is not None:
                deps[id(b.last_w)] = b.last_w
        for b in writes:
            if b.last_w is not None:
                deps[id(b.last_w)] = b.last_w
            for r in b.readers:
                deps[id(r)] = r
        deps.pop(id(ins), None)
        ins.deps = list(deps.values())
        for b in reads:
            if not dma:
                b.readers = [r for r in b.readers if r.dma or r.eng != eng]
            b.readers.append(ins)
        for b in writes:
            b.last_w = ins
            b.readers = []
        self.streams[eng].append(ins)
        return ins

    def mm(self, reads, writes, out, lhsT, rhs, start=True, stop=True):
        return self.op("tensor", ("matmul", dict(out=out, lhsT=lhsT, rhs=rhs, start=start, stop=stop)), reads, writes)

    def tr(self, reads, writes, out, in_, identity):
        return self.op("tensor", ("transpose", dict(out=out, in_=in_, identity=identity)), reads, writes)

    def dve(self, meth, reads, writes, **kw):
        return self.op("vector", (meth, kw), reads, writes)

    def act(self, reads, writes, out, in_, func, **kw):
        return self.op("scalar", ("activation", dict(out=out, in_=in_, func=func, **kw)), reads, writes)

    def pool(self, meth, reads, writes, **kw):
        return self.op("gpsimd", (meth, kw), reads, writes)

    def dma(self, eng, reads, writes, out, in_):
        return self.op(eng, ("dma_start", dict(out=out, in_=in_)), reads, writes, dma=True)

    def emit(self, final_wait=()):
        nc = self.nc
        def need_sync(ins, d):
            if d.dma:
                return True
            if d.eng == ins.eng:
                if ins.eng == "tensor" and not ins.dma:
                    return False
                return True
            return True

        for e in ENGS:
            for ins in self.streams[e]:
                for d in ins.deps:
                    if not d.dma and need_sync(ins, d):
                        d.signals = True
        import contextlib
        with contextlib.ExitStack() as st:
            esem = {e: st.enter_context(nc.semaphore("s_" + e)) for e in ENGS}
            dsem = {e: [st.enter_context(nc.semaphore("d_%s%d" % (e, i))) for i in range(N_DMA_SEMS)]
                    for e in ("gpsimd", "sync", "scalar")}
            for e in ENGS:
                c = 0
                k = 0
                for ins in self.streams[e]:
                    if ins.dma:
                        ins.sem = dsem[e][k % N_DMA_SEMS]
                        ins.semval = 16 * (k // N_DMA_SEMS + 1)
                        k += 1
                    elif ins.signals:
                        c += 1
                        ins.sig_idx = c
            block = st.enter_context(nc.Block())
            streams = self.streams

            def run(e, eng):
                seen = {}
                k = 0
                prev_on_sem = {}
                for ins in streams[e]:
                    waits = {}
                    for d in ins.deps:
                        if d.dma:
                            key = ("d", id(d.sem))
                            if waits.get(key, (None, 0))[1] < d.semval:
                                waits[key] = (d.sem, d.semval)
                        elif need_sync(ins, d):
                            key = ("e", d.eng)
                            if waits.get(key, (None, 0))[1] < d.sig_idx:
                                waits[key] = (esem[d.eng], d.sig_idx)
                    if ins.dma:
                        key = ("d", id(ins.sem))
                        pv = ins.semval - 16
                        if pv > 0 and waits.get(key, (None, 0))[1] < pv:
                            waits[key] = (ins.sem, pv)
                    for key, (sem, val) in waits.items():
                        if seen.get(key, 0) >= val:
                            continue
                        seen[key] = val
                        eng.wait_ge(sem, val)
                    r = getattr(eng, ins.fn[0])(**ins.fn[1])
                    if ins.dma:
                        r.then_inc(ins.sem, 16)
                    elif ins.signals:
                        r.then_inc(esem[e], 1)
                if e == "sync":
                    for d in final_wait:
                        eng.wait_ge(d.sem, d.semval)

            @block.tensor
            def _(eng):
                run("tensor", eng)

            @block.vector
            def _(eng):
                run("vector", eng)

            @block.scalar
            def _(eng):
                run("scalar", eng)

            @block.gpsimd
            def _(eng):
                run("gpsimd", eng)

            @block.sync
            def _(eng):
                run("sync", eng)


class Pool:
    def __init__(self, tiles):
        self.tiles = tiles
        self.bufs = [Buf("pool") for _ in tiles]
        self.i = 0

    def next(self):
        i = self.i % len(self.tiles)
        self.i += 1
        return self.tiles[i], self.bufs[i]


MEM = 256
DN_W = 1536
A_IN = 4 * DN_W + 24 + 512
SCALE = 128 ** -0.5


class Env:
    pass


class StopBuild(Exception):
    pass


CUT = [0]


def cut(n):
    if CUT[0] == n:
        raise StopBuild()


class DT:
    def __init__(self, ap, nk, name="dt"):
        self.ap = ap
        self.nk = nk
        self.v = ap.rearrange("(k p) t -> p k t", p=128)
        self.b = [Buf("%s%d" % (name, k)) for k in range(nk)]

    def tile(self, k):
        return self.v[:, k, :]


def make_env(nc, st, P, T=TOK):
    E = Env()
    E.nc, E.P, E.T, E.st = nc, P, T, st
    E.NH = T // 512
    sb = lambda name, shape, dt: st.enter_context(nc.sbuf_tensor(name, shape, dt))
    E.sb = sb
    E.hT = sb("hT", [128, KT, T], BF16)
    E.hb = [Buf("h%d" % k) for k in range(KT)]
    E.actT = sb("actT", [128, FT, T], BF16)
    E.ab = [Buf("a%d" % f) for f in range(FT)]
    NW = 6
    wt = sb("wts", [128, NW, 2048], BF16)
    E.wpool = Pool([wt[:, i, :] for i in range(NW)])
    rs = sb("rstd", [128, 2, T], F32)
    E.rstds = [rs[:, 0, :], rs[:, 1, :]]
    E.rstd_bs = [Buf("rstd0"), Buf("rstd1")]
    E.ri = 0
    sq = sb("sq", [128, 2, T], BF16)
    E.sqpool = Pool([sq[:, i, :] for i in range(2)])
    tmp = sb("tmpf", [128, 5, 512], F32)
    E.tmppool = Pool([tmp[:, i, :] for i in range(5)])
    xin = sb("xin", [128, 2, T], F32)
    E.xpool = Pool([xin[:, i, :] for i in range(2)])
    E.ones = sb("ones", [128, 128], BF16)
    E.ones_b = Buf("ones")
    E.onef = sb("onef", [1, 1], F32)
    E.rcol = sb("rcol", [128, T // 128], F32)
    E.rcol_b = Buf("rcol")
    E.gains = sb("gains_sb", [128, 14 * KT], F32)
    E.gains_b = Buf("gains")
    stg = sb("stg", [128, 4, 512], F32)
    E.stgpool = Pool([stg[:, i, :] for i in range(4)])
    E.evac_i = 0
    E.pending = []
    ps = [st.enter_context(nc.psum_tensor("ps%d" % i, [128, 512], F32)) for i in range(8)]
    E.pspool = Pool(ps[0:6])
    E.statbanks = [(ps[6], Buf("stat0")), (ps[7], Buf("stat1"))]
    P.pool("memset", [], [E.ones_b], ap=E.ones[:], constant=1.0)
    P.pool("memset", [], [E.ones_b], ap=E.onef[:], constant=1.0)
    return E


def gcol(E, gidx, k):
    return E.gains[:, gidx * KT + k:gidx * KT + k + 1]


def evac(E, out, in_, reads, writes):
    E.evac_i += 1
    if E.evac_i % 2:
        return E.P.act(reads, writes, out, in_, AF.Copy)
    return E.P.dve("tensor_copy", reads, writes, out=out, in_=in_)


def emit_rstd_finish(E, banks, nfeat, coef, rstd, rstd_b, ncol=512):
    P = E.P
    c2 = coef * coef
    for h, (pt, pb) in enumerate(banks):
        sl = slice(h * ncol, (h + 1) * ncol)
        P.dve("tensor_scalar", [pb], [rstd_b], out=rstd[:, sl], in0=pt[:, 0:ncol], scalar1=1.0 / (nfeat * c2),
              scalar2=EPS / c2, op0=ALU.mult, op1=ALU.add)
    P.act([rstd_b], [rstd_b], rstd, rstd, AF.Ln)
    P.act([rstd_b], [rstd_b], rstd, rstd, AF.Exp, scale=-0.5)


def emit_sumsq(E, banks, src, srcb, k, nk, ncol=512, h_only=None):
    P = E.P
    sq, sqb = E.sqpool.next()
    if h_only is None:
        n = ncol * len(banks)
        P.act([srcb], [sqb], sq[:, 0:n], src, AF.Square)
        for h, (pt, pb) in enumerate(banks):
            P.mm([sqb, E.ones_b], [pb], pt[:, 0:ncol], E.ones[:], sq[:, h * ncol:(h + 1) * ncol],
                 start=(k == 0), stop=(k == nk - 1))
    else:
        pt, pb = banks[h_only]
        P.act([srcb], [sqb], sq[:, 0:ncol], src, AF.Square)
        P.mm([sqb, E.ones_b], [pb], pt[:, 0:ncol], E.ones[:], sq[:, 0:ncol], start=(k == 0), stop=(k == nk - 1))


def emit_norm_in_from_dram(E, X, gidx):
    P = E.P
    banks = E.statbanks
    for k in range(KT):
        xt, xb = E.xpool.next()
        P.dma("sync", [X.b[k]], [xb], xt, X.tile(k))
        emit_sumsq(E, banks, xt, xb, k, KT)
        P.dve("tensor_scalar", [xb, E.gains_b], [E.hb[k]], out=E.hT[:, k, :], in0=xt, scalar1=gcol(E, gidx, k),
              scalar2=None, op0=ALU.mult)
    E.ri ^= 1
    emit_rstd_finish(E, banks, D, 1.0, E.rstds[E.ri], E.rstd_bs[E.ri])


def emit_y_evac(E, py, pyb, j, h, nj):
    P = E.P
    sl = slice(h * 512, (h + 1) * 512)
    flush_pending(E)
    P.dve("tensor_copy", [pyb], [E.hb[j]], out=E.hT[:, j, sl], in_=py[:])
    sq, sqb = E.sqpool.next()
    P.act([E.hb[j]], [sqb], sq[:, 0:512], E.hT[:, j, sl], AF.Square)
    pt, pb = E.statbanks[h]
    E.pending.append(dict(reads=[sqb, E.ones_b], writes=[pb], out=pt[:, 0:512], lhsT=E.ones[:], rhs=sq[:, 0:512],
                          start=(j == 0), stop=(j == nj - 1)))


def flush_pending(E):
    for p in E.pending:
        E.P.mm(p["reads"], p["writes"], p["out"], p["lhsT"], p["rhs"], start=p["start"], stop=p["stop"])
    E.pending = []


def emit_postnorm_residual(E, X, XO, gidx, coef, next_gidx=None):
    P = E.P
    flush_pending(E)
    E.ri ^= 1
    ry, ryb = E.rstds[E.ri], E.rstd_bs[E.ri]
    emit_rstd_finish(E, E.statbanks, D, coef, ry, ryb)
    outs = []
    for k in range(KT):
        xt, xb = E.xpool.next()
        P.dma("sync", [X.b[k]], [xb], xt, X.tile(k))
        for h in range(E.NH):
            sl = slice(h * 512, (h + 1) * 512)
            tt, tb = E.tmppool.next()
            P.dve("scalar_tensor_tensor", [E.hb[k], ryb, E.gains_b], [tb], out=tt, in0=E.hT[:, k, sl],
                  scalar=gcol(E, gidx, k), in1=ry[:, sl], op0=ALU.mult, op1=ALU.mult)
            if h == 0:
                P.pool("tensor_tensor", [tb, xb], [xb], out=xt[:, sl], in0=tt, in1=xt[:, sl], op=ALU.add)
            else:
                P.dve("tensor_tensor", [tb, xb], [xb], out=xt[:, sl], in0=tt, in1=xt[:, sl], op=ALU.add)
        outs.append(P.dma("sync", [xb], [XO.b[k]], XO.tile(k), xt))
        if next_gidx is not None:
            emit_sumsq(E, E.statbanks, xt, xb, k, KT)
            P.dve("tensor_scalar", [xb, E.gains_b], [E.hb[k]], out=E.hT[:, k, :], in0=xt, scalar1=gcol(E, next_gidx, k),
                  scalar2=None, op0=ALU.mult)
    if next_gidx is not None:
        E.ri ^= 1
        emit_rstd_finish(E, E.statbanks, D, 1.0, E.rstds[E.ri], E.rstd_bs[E.ri])
    return outs


def load_wchunk(E, W, nk, c0, ncols):
    assert nk * ncols <= 2048
    wt, wb = E.wpool.next()
    wv = wt[:, 0:nk * ncols].rearrange("p (k n) -> p k n", k=nk)
    Wv = W.rearrange("(k p) n -> p k n", p=128)
    E.P.dma("gpsimd", [], [wb], wv, Wv[:, :, c0:c0 + ncols])
    return wv, wb


def emit_ffn(E, X, XO, w_in, w_out, g_post, next_gidx):
    P = E.P
    r, rb = E.rstds[E.ri], E.rstd_bs[E.ri]
    for f in range(FT):
        wgv, wgb = load_wchunk(E, w_in, KT, f * 128, 128)
        wuv, wub = load_wchunk(E, w_in, KT, DFF + f * 128, 128)
        for h in range(E.NH):
            sl = slice(h * 512, (h + 1) * 512)
            pg, pgb = E.pspool.next()
            pu, pub = E.pspool.next()
            for k in range(KT):
                P.mm([wgb, E.hb[k]], [pgb], pg[:], wgv[:, k, :], E.hT[:, k, sl], start=(k == 0), stop=(k == KT - 1))
            for k in range(KT):
                P.mm([wub, E.hb[k]], [pub], pu[:], wuv[:, k, :], E.hT[:, k, sl], start=(k == 0), stop=(k == KT - 1))
            t1, t1b = E.tmppool.next()
            t3, t3b = E.tmppool.next()
            P.dve("tensor_tensor", [pgb, rb], [t1b], out=t1, in0=pg[:], in1=r[:, sl], op=ALU.mult)
            P.act([t1b], [t1b], t1, t1, AF.Silu)
            P.dve("tensor_tensor", [pub, rb], [t3b], out=t3, in0=pu[:], in1=r[:, sl], op=ALU.mult)
            P.dve("tensor_tensor", [t1b, t3b], [E.ab[f]], out=E.actT[:, f, sl], in0=t1, in1=t3, op=ALU.mult)
    cut(2)
    w_out_v = w_out.rearrange("(f p) n -> p f n", p=128)
    fr = [(0, 16), (16, 32), (32, FT)]
    for j in range(KT):
        slots = []
        for (f0, f1) in fr:
            wt, wb = E.wpool.next()
            wv = wt.rearrange("p (f n) -> p f n", n=128)
            P.dma("gpsimd", [], [wb], wv[:, 0:f1 - f0, :], w_out_v[:, f0:f1, j * 128:(j + 1) * 128])
            slots.append((wv, wb))
        for h in range(E.NH):
            sl = slice(h * 512, (h + 1) * 512)
            py, pyb = E.pspool.next()
            for (wv, wb), (f0, f1) in zip(slots, fr):
                for f in range(f0, f1):
                    P.mm([wb, E.ab[f]], [pyb], py[:], wv[:, f - f0, :], E.actT[:, f, sl],
                         start=(f == 0), stop=(f == FT - 1))
            emit_y_evac(E, py, pyb, j, h, KT)
    cut(3)
    return emit_postnorm_residual(E, X, XO, g_post, 0.5, next_gidx)


def emit_lin_fm(E, W, nk, in_tiles, in_bufs, c0, consumer, ncols=128):
    P = E.P
    wv, wb = load_wchunk(E, W, nk, c0, ncols)
    for h in range(E.NH):
        sl = slice(h * 512, (h + 1) * 512)
        pt, pb = E.pspool.next()
        for k in range(nk):
            P.mm([wb, in_bufs[k]], [pb], pt[0:ncols, :], wv[:, k, :], in_tiles[k][:, sl],
                 start=(k == 0), stop=(k == nk - 1))
        consumer(pt, pb, h)


def evac_rstd(E, out, pt, pb, h, writes):
    sl = slice(h * 512, (h + 1) * 512)
    return E.P.dve("tensor_tensor", [pb, E.rstd_bs[E.ri]], writes, out=out, in0=pt[:], in1=E.rstds[E.ri][:, sl], op=ALU.mult)


def emit_lin_fm_to_dram(E, W, nk, in_tiles, in_bufs, c0, OUT, row0, outs):
    def cons(pt, pb, h):
        stg, sb_ = E.stgpool.next()
        evac_rstd(E, stg, pt, pb, h, [sb_])
        outs.append(E.P.dma("sync", [sb_], [], OUT[row0:row0 + 128, h * 512:(h + 1) * 512], stg))
    emit_lin_fm(E, W, nk, in_tiles, in_bufs, c0, cons)


def emit_rstd_cols(E):
    P = E.P
    r, rb = E.rstds[E.ri], E.rstd_bs[E.ri]
    pt, pb = E.pspool.next()
    nt = E.T // 128
    for tt in range(nt):
        P.mm([rb, E.ones_b], [pb], pt[:, tt:tt + 1], r[0:1, tt * 128:(tt + 1) * 128], E.onef[:])
    P.dve("tensor_copy", [pb], [E.rcol_b], out=E.rcol[:], in_=pt[:, 0:nt])


def emit_lin_tm_to_dram(E, W, nk, in_tiles, in_bufs, c0, ncols, OUT, ocol0, outs, T=None):
    P = E.P
    T = T or E.T
    nch = ncols // 128
    wvs = [load_wchunk(E, W, nk, c0 + i * 128, 128) for i in range(nch)]
    for tt in range(T // 128):
        pt, pb = E.pspool.next()
        for i, (wv, wb) in enumerate(wvs):
            for k in range(nk):
                P.mm([wb, in_bufs[k]], [pb], pt[:, i * 128:(i + 1) * 128], in_tiles[k][:, tt * 128:(tt + 1) * 128],
                     wv[:, k, :], start=(k == 0), stop=(k == nk - 1))
        stg, sb_ = E.stgpool.next()
        P.dve("tensor_scalar", [pb, E.rcol_b], [sb_], out=stg[:, 0:ncols], in0=pt[:, 0:ncols], scalar1=E.rcol[:, tt:tt + 1],
              scalar2=None, op0=ALU.mult)
        outs.append(P.dma("sync", [sb_], [], OUT[tt * 128:(tt + 1) * 128, ocol0:ocol0 + ncols], stg[:, 0:ncols]))


def emit_mem_kv(E, memT, w_mkv, gidx):
    P = E.P
    mh = E.actT[:, 0:4, :].rearrange("p a (b m) -> p (a b) m", m=MEM)
    mhb = [E.ab[k // 4] for k in range(KT)]
    E.memKT = E.sb("memKT", [128, 4, MEM], BF16)
    E.memKT_b = Buf("memKT")
    E.memV = E.sb("memV", [128, 2, 512], BF16)
    E.memV_b = Buf("memV")
    rstd_m = E.sb("rstd_m", [128, MEM], F32)
    rmb = Buf("rstd_m")
    mv = memT.rearrange("(k p) t -> p k t", p=128)
    banks = [E.pspool.next()]
    for k in range(KT):
        xt, xb = E.xpool.next()
        P.dma("sync", [], [xb], xt[:, 0:MEM], mv[:, k, :])
        emit_sumsq(E, banks, xt[:, 0:MEM], xb, k, KT, ncol=MEM)
        P.dve("tensor_scalar", [xb, E.gains_b], [mhb[k]], out=mh[:, k, :], in0=xt[:, 0:MEM], scalar1=gcol(E, gidx, k),
              scalar2=None, op0=ALU.mult)
    emit_rstd_finish(E, banks, D, 1.0, rstd_m[:], rmb, ncol=MEM)
    rmcol = E.sb("rmcol", [128, 2], F32)
    pc, pcb = E.pspool.next()
    for mt in range(2):
        P.mm([rmb, E.ones_b], [pcb], pc[:, mt:mt + 1], rstd_m[0:1, mt * 128:(mt + 1) * 128], E.onef[:])
    P.dve("tensor_copy", [pcb], [rmb], out=rmcol[:], in_=pc[:, 0:2])
    for hd in range(4):
        wv, wb = load_wchunk(E, w_mkv, KT, hd * 128, 128)
        pt, pb = E.pspool.next()
        for k in range(KT):
            P.mm([wb, mhb[k]], [pb], pt[:, 0:MEM], wv[:, k, :], mh[:, k, :], start=(k == 0), stop=(k == KT - 1))
        P.dve("tensor_tensor", [pb, rmb], [E.memKT_b], out=E.memKT[:, hd, :], in0=pt[:, 0:MEM], in1=rstd_m[:], op=ALU.mult)
    for c in range(4):
        wv, wb = load_wchunk(E, w_mkv, KT, 512 + c * 128, 128)
        for mt in range(2):
            pt, pb = E.pspool.next()
            for k in range(KT):
                P.mm([wb, mhb[k]], [pb], pt[:, 0:128], mh[:, k, mt * 128:(mt + 1) * 128], wv[:, k, :],
                     start=(k == 0), stop=(k == KT - 1))
            P.dve("tensor_scalar", [pb, rmb], [E.memV_b], out=E.memV[:, mt, c * 128:(c + 1) * 128], in0=pt[:, 0:128],
                  scalar1=rmcol[:, mt:mt + 1], scalar2=None, op0=ALU.mult)


def emit_mem_attn(E, OMEM, outs):
    P = E.P
    et = E.sb("ET", [128, 2, 2, 512], BF16)
    etpool = Pool([et[:, i] for i in range(2)])
    for hd in range(4):
        for h in range(E.NH):
            sl = slice(h * 512, (h + 1) * 512)
            ET, ETb = etpool.next()
            for mt in range(2):
                ps, psb = E.pspool.next()
                P.mm([E.memKT_b, E.mq_b[hd]], [psb], ps[:], E.memKT[:, hd, mt * 128:(mt + 1) * 128], E.mqT[:, hd, sl])
                P.act([psb], [ETb], ET[:, mt, :], ps[:], AF.Exp, scale=SCALE)
            po, pob = E.pspool.next()
            pd, pdb = E.pspool.next()
            for mt in range(2):
                P.mm([E.memV_b, ETb], [pob], po[:], E.memV[:, mt, hd * 128:(hd + 1) * 128], ET[:, mt, :],
                     start=(mt == 0), stop=(mt == 1))
            for mt in range(2):
                P.mm([E.ones_b, ETb], [pdb], pd[:], E.ones[:], ET[:, mt, :], start=(mt == 0), stop=(mt == 1))
            tt, tb = E.tmppool.next()
            P.dve("reciprocal", [pdb], [tb], out=tt, in_=pd[:])
            stg, sb_ = E.stgpool.next()
            P.dve("tensor_tensor", [tb, pob], [sb_], out=stg, in0=po[:], in1=tt, op=ALU.mult)
            outs.append(P.dma("sync", [sb_], [], OMEM[hd * 128:(hd + 1) * 128, sl], stg))


def build_ts(prev, nxt, has_kv):
    import contextlib
    nc = bass.Bass("TRN2", target_bir_lowering=False)
    T = TOK
    di = lambda name, shape: nc.dram_tensor(name, shape, F32, kind="ExternalInput").ap()
    do = lambda name, shape: nc.dram_tensor(name, shape, F32, kind="ExternalOutput").ap()
    dint = lambda name, shape: nc.dram_tensor(name, shape, F32, kind="Internal").ap()
    x_in = di("x_in", [D, T])
    gains = di("gains", [128, 14 * KT])
    P = Prog(nc)
    outs = []
    with contextlib.ExitStack() as st:
        E = make_env(nc, st, P)
        P.dma("sync", [], [E.gains_b], E.gains[:], gains)
        X = DT(x_in, KT, "xin")
        hin = [E.hT[:, k, :] for k in range(KT)]
        try:
            _build_ts_body(nc, E, P, X, hin, prev, nxt, has_kv, outs, di, do, dint, T)
        except StopBuild:
            pass
        P.emit(final_wait=outs)
    return nc


def _build_ts_body(nc, E, P, X, hin, prev, nxt, has_kv, outs, di, do, dint, T):
    if True:
        if prev is not None:
            nko = 16 if prev == "dn" else 8
            o_in = di("o_in", [nko * 128, T])
            w_o = di("w_o", [nko * 128, D])
            f2_in = di("f2_in", [D, 2 * DFF])
            f2_out = di("f2_out", [DFF, D])
            ov = o_in.rearrange("(k p) t -> p k t", p=128)
            for k in range(nko):
                P.dma("gpsimd", [], [E.ab[k]], E.actT[:, k, :], ov[:, k, :])
            ain = [E.actT[:, k, :] for k in range(nko)]
            for j in range(KT):
                def cons(pt, pb, h, j=j):
                    emit_y_evac(E, pt, pb, j, h, KT)
                emit_lin_fm(E, w_o, nko, ain, E.ab, j * 128, cons)
            X2 = DT(dint("x2", [D, T]), KT, "x2")
            emit_postnorm_residual(E, X, X2, 3, 1.0, next_gidx=4)
            if nxt is None:
                X3 = DT(do("x_out", [D, T]), KT, "x3")
            else:
                X3 = DT(dint("x3", [D, T]), KT, "x3")
            o3 = emit_ffn(E, X2, X3, f2_in, f2_out, 5, (None if nxt is None else (13 if has_kv else 6)))
            if nxt is None:
                outs += o3
            X = X3
        if nxt is not None:
            f1_in = di("f1_in", [D, 2 * DFF])
            f1_out = di("f1_out", [DFF, D])
            memT = di("memT", [D, MEM])
            w_mkv = di("w_mkv", [D, 1024])
            if prev is None:
                emit_norm_in_from_dram(E, X, 6)
                cut(1)
            if has_kv:
                w_kv = di("w_kv", [D, 3072])
                kT_o = do("kT", [1536, T])
                v_o = do("v", [T, 1536])
                emit_rstd_cols(E)
                for c in range(12):
                    emit_lin_fm_to_dram(E, w_kv, KT, hin, E.hb, c * 128, kT_o, c * 128, outs)
                for c in range(3):
                    emit_lin_tm_to_dram(E, w_kv, KT, hin, E.hb, 1536 + c * 512, 512, v_o, c * 512, outs)
                emit_norm_in_from_dram(E, X, 6)
            X1 = DT(do("x_out", [D, T]), KT, "x1")
            outs += emit_ffn(E, X, X1, f1_in, f1_out, 7, 8)
            cut(4)
            E.mqT = E.sb("mqT", [128, 4, T], BF16)
            E.mq_b = [Buf("mq%d" % i) for i in range(4)]
            omem = do("omemT", [512, T])
            if nxt == "dn":
                w_p = di("w_p", [D, A_IN])
                qkvT = do("qkvT", [4608, T])
                z_o = do("z", [T, 1536])
                gb_o = do("gb", [T, 24])
                dnc = di("dnc", [128, 24])
                mq0 = 6168
                emit_rstd_cols(E)
                cut(5)
                for c in range(36):
                    emit_lin_fm_to_dram(E, w_p, KT, hin, E.hb, c * 128, qkvT, c * 128, outs)
                for c in range(3):
                    emit_lin_tm_to_dram(E, w_p, KT, hin, E.hb, 4608 + c * 512, 512, z_o, c * 512, outs)
                cut(6)
                dn_c = E.sb("dnc_sb", [128, 24], F32)
                dn_cb = Buf("dnc")
                nea = E.sb("nea", [128, 12], F32)
                neab = Buf("nea")
                gt = E.sb("gt", [128, 2, 24], F32)
                gtpool = Pool([gt[:, i, :] for i in range(2)])
                P.dma("sync", [], [dn_cb], dn_c[:], dnc)
                P.act([dn_cb], [neab], nea[:], dn_c[:, 12:24], AF.Exp)
                P.dve("tensor_scalar", [neab], [neab], out=nea[:], in0=nea[:], scalar1=-1.0, scalar2=None, op0=ALU.mult)
                wv, wb = load_wchunk(E, w_p, KT, 6144, 24)
                for tt in range(T // 128):
                    pt, pb = E.pspool.next()
                    for k in range(KT):
                        P.mm([wb, E.hb[k]], [pb], pt[:, 0:24], E.hT[:, k, tt * 128:(tt + 1) * 128], wv[:, k, :],
                             start=(k == 0), stop=(k == KT - 1))
                    g1, g1b = gtpool.next()
                    stg, sb_ = E.stgpool.next()
                    P.dve("tensor_scalar", [pb, E.rcol_b], [g1b], out=g1[:, 0:24], in0=pt[:, 0:24], scalar1=E.rcol[:, tt:tt + 1],
                          scalar2=None, op0=ALU.mult)
                    P.act([g1b], [sb_], stg[:, 12:24], g1[:, 12:24], AF.Sigmoid)
                    P.dve("tensor_tensor", [g1b, dn_cb], [g1b], out=g1[:, 0:12], in0=g1[:, 0:12], in1=dn_c[:, 0:12], op=ALU.add)
                    P.act([g1b], [g1b], g1[:, 0:12], g1[:, 0:12], AF.Exp)
                    P.dve("tensor_scalar", [g1b], [g1b], out=g1[:, 0:12], in0=g1[:, 0:12], scalar1=1.0, scalar2=None, op0=ALU.add)
                    P.act([g1b], [g1b], g1[:, 0:12], g1[:, 0:12], AF.Ln)
                    P.dve("tensor_tensor", [g1b, neab, sb_], [sb_], out=stg[:, 0:12], in0=g1[:, 0:12], in1=nea[:], op=ALU.mult)
                    outs.append(P.dma("sync", [sb_], [], gb_o[tt * 128:(tt + 1) * 128, :], stg[:, 0:24]))
            else:
                w_p = di("w_p", [D, 2048])
                qT = do("qT", [1536, T])
                mq0 = 1536
                for c in range(12):
                    emit_lin_fm_to_dram(E, w_p, KT, hin, E.hb, c * 128, qT, c * 128, outs)
            for hd in range(4):
                def cons(pt, pb, h, hd=hd):
                    evac_rstd(E, E.mqT[:, hd, h * 512:(h + 1) * 512], pt, pb, h, [E.mq_b[hd]])
                emit_lin_fm(E, w_p, KT, hin, E.hb, mq0 + hd * 128, cons)
            cut(8)
            emit_mem_kv(E, memT, w_mkv, 12)
            cut(9)
            emit_mem_attn(E, omem, outs)


def gains_layout(g):
    G = g.shape[0]
    return np.ascontiguousarray(g.reshape(G, KT, 128).transpose(2, 0, 1).reshape(128, G * KT))


CH = 64
GC = 8
GT = CH * GC


def bc_last(ap, n):
    return bass.AP(ap.tensor, ap.offset, [list(x) for x in ap.ap] + [[0, n]])


def bc_mid(ap, n):
    a = [list(x) for x in ap.ap]
    return bass.AP(ap.tensor, ap.offset, [a[0], [0, n]] + a[1:])


def build_dn(S=SEQ, HPC=2, IDT=BF16, stage=99):
    import contextlib
    nc = bass.Bass("TRN2", target_bir_lowering=False)
    NCH = S // CH
    NG = S // GT
    di = lambda name, shape: nc.dram_tensor(name, shape, F32, kind="ExternalInput").ap()
    qkv = di("qkv", [HPC, 3, 128, S])
    convw = di("convw", [128, HPC * 3 * 4])
    zc = di("zc", [CH, NCH, HPC, 128])
    gcol = di("gcol", [CH, HPC, NCH])
    bcol = di("bcol", [CH, HPC, NCH])
    onorm = di("onorm", [CH, 128])
    consts = di("consts", [CH, 5, CH])
    o_out = nc.dram_tensor("o", [CH, NCH, HPC, 128], F32, kind="ExternalOutput").ap()
    P = Prog(nc)
    outs = []
    with contextlib.ExitStack() as st:
        sb = lambda name, shape, dt: st.enter_context(nc.sbuf_tensor(name, shape, dt))
        ps = [st.enter_context(nc.psum_tensor("ps%d" % i, [128, 512], F32)) for i in range(8)]
        pspool = Pool(ps)
        cst = sb("cst", [CH, 5, CH], F32)
        cst_b = Buf("cst")
        P.dma("sync", [], [cst_b], cst[:], consts)
        tri, tris, ident, strict, inclT = [cst[:, i, :] for i in range(5)]
        ones = sb("ones32", [128, 128], F32)
        ones_b = Buf("ones")
        P.pool("memset", [], [ones_b], ap=ones[:], constant=1.0)
        ident128 = sb("ident128", [128, 128], F32)
        id_b = Buf("id128")
        P.pool("memset", [], [id_b], ap=ident128[:], constant=0.0)
        P.dma("sync", [id_b], [id_b], ident128[0:CH, 0:CH], consts[:, 2, :])
        P.dma("sync", [id_b], [id_b], ident128[CH:128, CH:128], consts[:, 2, :])
        identb = sb("identb", [CH, CH], BF16)
        P.dve("tensor_copy", [cst_b], [cst_b], out=identb[:], in_=cst[:, 2, :])
        cw = sb("cw", [128, HPC * 12], F32)
        cw_b = Buf("cw")
        P.dma("sync", [], [cw_b], cw[:], convw)
        dg = sb("dg", [128, HPC * 12, 128], BF16)
        dg_b = Buf("dg")
        for col in range(HPC * 12):
            (P.dve if col % 2 else P.pool)("tensor_scalar", [id_b, cw_b, dg_b], [dg_b], out=dg[:, col, :], in0=ident128[:],
                                           scalar1=cw[:, col:col + 1], scalar2=None, op0=ALU.mult)
        on = sb("on", [CH, 128], F32)
        on_b = Buf("on")
        P.dma("sync", [], [on_b], on[:], onorm)
        NA = HPC * NCH
        g_all = sb("g_all", [CH, NA], F32)
        b_all = sb("b_all", [CH, NA], F32)
        gc_all = sb("gc_all", [CH, NA], F32)
        egc = sb("egc", [CH, NA], F32)
        bg = sb("bg", [CH, NA], F32)
        ekd = sb("ekd", [CH, NA], F32)
        egl = sb("egl", [128, NA], F32)
        gate_b = Buf("gates")
        P.dma("sync", [], [gate_b], g_all[:], gcol.rearrange("p h n -> p (h n)"))
        P.dma("sync", [gate_b], [gate_b], b_all[:], bcol.rearrange("p h n -> p (h n)"))
        strict_x = sb("strict_x", [CH, GC, CH], F32)
        inclT_x = sb("inclT_x", [CH, GC, CH], F32)
        ident_x = sb("ident_x", [CH, GC, CH], F32)
        mx_b = Buf("maskx")
        P.dve("tensor_copy", [cst_b], [mx_b], out=strict_x[:], in_=bc_mid(strict, GC))
        P.dve("tensor_copy", [cst_b, mx_b], [mx_b], out=inclT_x[:], in_=bc_mid(inclT, GC))
        P.dve("tensor_copy", [cst_b, mx_b], [mx_b], out=ident_x[:], in_=bc_mid(ident, GC))
        assert NA <= 512
        p1, p1b = pspool.next()
        P.mm([cst_b, gate_b], [p1b], p1[0:CH, 0:NA], tri, g_all[:])
        P.dve("tensor_copy", [p1b], [gate_b], out=gc_all[:], in_=p1[0:CH, 0:NA])
        p2, p2b = pspool.next()
        P.mm([ones_b, gate_b], [p2b], p2[:, 0:NA], ones[0:CH, :], g_all[:])
        P.act([p2b], [gate_b], egl[:], p2[:, 0:NA], AF.Exp)
        P.act([gate_b], [gate_b], egc[:], gc_all[:], AF.Exp)
        P.dve("tensor_tensor", [gate_b], [gate_b], out=bg[:], in0=b_all[:], in1=egc[:], op=ALU.mult)
        P.dve("tensor_tensor", [gate_b, p2b], [gate_b], out=ekd[:], in0=p2[0:CH, 0:NA], in1=gc_all[:], op=ALU.subtract)
        P.act([gate_b], [gate_b], ekd[:], ekd[:], AF.Exp)

        def mk(name, shape, dt, n):
            t = sb(name, [shape[0], n] + shape[1:], dt)
            return Pool([t[:, i] for i in range(n)])
        PL = []
        for hh in range(HPC):
            s_ = "_%d" % hh
            PL.append(dict(
                raw=mk("raw" + s_, [128, 3, 3 + GT], BF16, 2), cv=mk("cv" + s_, [128, 3, GT], F32, 1),
                sq=mk("sqd" + s_, [128, GT], F32, 2), rs=mk("rsd" + s_, [128, GT], F32, 2),
                KT=mk("KTb" + s_, [128, GT], BF16, 1), KV32=mk("KV32" + s_, [128, 2, GT], F32, 1),
                TG=mk("TG" + s_, [CH, GC, CH], F32, 1), DM=mk("DM" + s_, [CH, GC, CH], F32, 1),
                DTM=mk("DTM" + s_, [CH, GC, CH], F32, 1), LM=mk("LM" + s_, [CH, 2, GC, CH], IDT, 3),
                X=mk("Xc" + s_, [CH, GC, CH], IDT, 1), Kbg=mk("Kbg" + s_, [CH, GC, 128], IDT, 1),
                Vb=mk("Vb" + s_, [CH, GC, 128], IDT, 1),
                QT=mk("QT" + s_, [128, GT], BF16, 2), kd=mk("kd" + s_, [CH, GC, 128], BF16, 2),
                U=mk("U" + s_, [CH, GC, 128], F32, 2), WT=mk("WT" + s_, [128, GC, CH], BF16, 2),
                aT=mk("aT" + s_, [CH, GC, CH], BF16, 2), z=mk("zt" + s_, [CH, GC, 128], F32, 2),
                o=mk("ot" + s_, [CH, GC, 128], F32, 2), vn=mk("vn" + s_, [CH, 128], BF16, 2),
                t1=mk("t1" + s_, [CH, 128], F32, 2), st=mk("stat" + s_, [CH, 2, GC], F32, 2),
                sq2=mk("sq2" + s_, [CH, GC, 128], F32, 1), oT=mk("oTt" + s_, [128, GT], F32, 2),
            ))
        S32 = sb("S32", [128, HPC, 128], F32)
        Sbf = sb("Sbf", [128, HPC, 128], BF16)
        S_b = [Buf("S%d" % h) for h in range(HPC)]
        Sbf_b = [Buf("Sbf%d" % h) for h in range(HPC)]
        for hh in range(HPC):
            P.pool("memset", [], [S_b[hh]], ap=S32[:, hh, :], constant=0.0)
            P.pool("memset", [], [Sbf_b[hh]], ap=Sbf[:, hh, :], constant=0.0)

        assert HPC == 2
        prep_ps = [Pool(ps[0:2]), Pool(ps[2:4])]
        rec_ps = [Pool(ps[4:6]), Pool(ps[6:8])]
        def prep(hh, gi, H):
            pl = PL[hh]
            pspool = prep_ps[hh]
            t0 = gi * GT
            c0 = hh * NCH + gi * GC
            raw, rawb = pl["raw"].next()
            if gi == 0:
                P.pool("memset", [], [rawb], ap=raw[:, :, 0:3], constant=0.0)
                P.dma("gpsimd", [rawb], [rawb], raw[:, :, 3:3 + GT], qkv[hh, :, :, 0:GT].rearrange("w p t -> p w t"))
            else:
                P.dma("gpsimd", [], [rawb], raw[:], qkv[hh, :, :, t0 - 3:t0 + GT].rearrange("w p t -> p w t"))
            zt, ztb = pl["z"].next()
            P.dma("sync", [], [ztb], zt, zc[:, gi * GC:(gi + 1) * GC, hh, :])
            cv, cvb = pl["cv"].next()
            for w in range(3):
                col = (hh * 3 + w) * 4
                pc, pcb = pspool.next()
                for j in range(4):
                    P.mm([rawb, dg_b], [pcb], pc[:], dg[:, col + j, :], raw[:, w, j:j + GT], start=(j == 0), stop=(j == 3))
                yield
                P.act([pcb], [cvb], cv[:, w, :], pc[:], AF.Silu)
                yield
            P.act([ztb], [ztb], zt, zt, AF.Silu)
            P.pool("tensor_tensor", [ztb, on_b], [ztb], out=zt, in0=zt, in1=bc_mid(on[:], GC), op=ALU.mult)
            yield
            QT, QTb = pl["QT"].next()
            KTt, KTb = pl["KT"].next()
            KV32, KV32b = pl["KV32"].next()
            for w in range(2):
                sq, sqb = pl["sq"].next()
                P.act([cvb], [sqb], sq, cv[:, w, :], AF.Square)
                yield
                pn, pnb = pspool.next()
                P.mm([sqb, ones_b], [pnb], pn[:], ones[:], sq)
                yield
                rs, rsb = pl["rs"].next()
                P.dve("tensor_scalar", [pnb], [rsb], out=rs, in0=pn[:], scalar1=EPS, scalar2=None, op0=ALU.add)
                yield
                P.act([rsb], [rsb], rs, rs, AF.Ln)
                P.act([rsb], [rsb], rs, rs, AF.Exp, scale=-0.5)
                yield
                if w == 0:
                    P.dve("scalar_tensor_tensor", [cvb, rsb], [QTb], out=QT, in0=cv[:, 0, :], scalar=SCALE, in1=rs,
                          op0=ALU.mult, op1=ALU.mult)
                else:
                    P.dve("tensor_tensor", [cvb, rsb], [KV32b], out=KV32[:, 0, :], in0=cv[:, 1, :], in1=rs, op=ALU.mult)
                    yield
                    P.act([KV32b], [KTb], KTt, KV32[:, 0, :], AF.Copy)
                yield
            P.pool("tensor_copy", [cvb, KV32b], [KV32b], out=KV32[:, 1, :], in_=cv[:, 2, :])
            TG, TGb = pl["TG"].next()
            for c in range(GC):
                P.pool("tensor_scalar", [cst_b, gate_b], [TGb], out=TG[:, c, :], in0=tri,
                       scalar1=g_all[:, c0 + c:c0 + c + 1], scalar2=None, op0=ALU.mult)
            yield
            pdt, pdtb = pspool.next()
            P.mm([cst_b, TGb], [pdtb], pdt[0:CH, :], tris, TG.rearrange("p c i -> p (c i)"))
            pd, pdb = pspool.next()
            for c in range(GC):
                P.mm([cst_b, TGb], [pdb], pd[0:CH, c * CH:(c + 1) * CH], TG[:, c, :], tris)
            yield
            DM, DMb = pl["DM"].next()
            DTM, DTMb = pl["DTM"].next()
            P.act([pdb], [DMb], DM.rearrange("p c i -> p (c i)"), pd[0:CH, :], AF.Exp)
            P.act([pdtb], [DTMb], DTM.rearrange("p c i -> p (c i)"), pdt[0:CH, :], AF.Exp)
            yield
            P.pool("tensor_tensor", [DMb, mx_b], [DMb], out=DM, in0=DM, in1=strict_x[:], op=ALU.mult)
            P.pool("tensor_tensor", [DTMb, mx_b], [DTMb], out=DTM, in0=DTM, in1=inclT_x[:], op=ALU.mult)
            yield
            pkk, pkkb = pspool.next()
            pqk, pqkb = pspool.next()
            for c in range(GC):
                cs = slice(c * CH, (c + 1) * CH)
                P.mm([KTb], [pkkb], pkk[0:CH, cs], KTt[:, cs], KTt[:, cs])
            for c in range(GC):
                cs = slice(c * CH, (c + 1) * CH)
                P.mm([KTb, QTb], [pqkb], pqk[0:CH, cs], KTt[:, cs], QT[:, cs])
            yield
            aT, aTb = pl["aT"].next()
            P.dve("tensor_tensor", [pqkb, DTMb], [aTb], out=aT.rearrange("p c i -> p (c i)"), in0=pqk[0:CH, :],
                  in1=DTM.rearrange("p c i -> p (c i)"), op=ALU.mult)
            LM, LMb = pl["LM"].next()
            L0, M0 = LM[:, 0], LM[:, 1]
            for c in range(GC):
                P.dve("scalar_tensor_tensor", [pkkb, DMb, gate_b], [LMb], out=L0[:, c, :], in0=pkk[0:CH, c * CH:(c + 1) * CH],
                      scalar=b_all[:, c0 + c:c0 + c + 1], in1=DM[:, c, :], op0=ALU.mult, op1=ALU.mult)
            yield
            ptr, ptrb = pspool.next()
            idl = ident if IDT == F32 else identb[:]
            for c in range(GC):
                P.mm([LMb, cst_b], [ptrb], ptr[0:CH, c * CH:(c + 1) * CH], L0[:, c, :], idl)
            yield
            P.act([ptrb], [LMb], M0.rearrange("p c i -> p (c i)"), ptr[0:CH, :], AF.Copy)
            yield
            X, Xb = pl["X"].next()
            P.dve("tensor_tensor", [LMb, mx_b], [Xb], out=X, in0=ident_x[:], in1=M0, op=ALU.subtract)
            Lc, Mc, LMcb = L0, M0, LMb
            for r in range(5):
                LMn, LMnb = pl["LM"].next()
                Ln, Mn = LMn[:, 0], LMn[:, 1]
                pl_, plb = pspool.next()
                pm, pmb = pspool.next()
                for c in range(GC):
                    P.mm([LMcb], [plb], pl_[0:CH, c * CH:(c + 1) * CH], Mc[:, c, :], Lc[:, c, :])
                for c in range(GC):
                    P.mm([LMcb], [pmb], pm[0:CH, c * CH:(c + 1) * CH], Lc[:, c, :], Mc[:, c, :])
                yield
                P.act([plb], [LMnb], Ln.rearrange("p c i -> p (c i)"), pl_[0:CH, :], AF.Copy)
                P.dve("tensor_copy", [pmb], [LMnb], out=Mn.rearrange("p c i -> p (c i)"), in_=pm[0:CH, :])
                yield
                px, pxb = pspool.next()
                for c in range(GC):
                    P.mm([LMnb, Xb], [pxb], px[0:CH, c * CH:(c + 1) * CH], Ln[:, c, :], X[:, c, :])
                yield
                P.dve("tensor_tensor", [pxb, Xb], [Xb], out=X.rearrange("p c i -> p (c i)"), in0=px[0:CH, :],
                      in1=X.rearrange("p c i -> p (c i)"), op=ALU.add)
                Lc, Mc, LMcb = Ln, Mn, LMnb
                yield
            Kbg, Kbgb = pl["Kbg"].next()
            kd, kdb = pl["kd"].next()
            Vb, Vbb = pl["Vb"].next()
            for half in range(2):
                pk, pkb = pspool.next()
                pv, pvb = pspool.next()
                for c4 in range(4):
                    c = half * 4 + c4
                    cs = slice(c * CH, (c + 1) * CH)
                    P.mm([KV32b, id_b], [pkb], pk[0:CH, c4 * 128:(c4 + 1) * 128], KV32[:, 0, cs], ident128[:])
                    P.mm([KV32b, id_b], [pvb], pv[0:CH, c4 * 128:(c4 + 1) * 128], KV32[:, 1, cs], ident128[:])
                yield
                hs = slice(half * 4, half * 4 + 4)
                gs = slice(c0 + half * 4, c0 + half * 4 + 4)
                pk3 = pk[0:CH, :].rearrange("p (c d) -> p c d", d=128)
                pv3 = pv[0:CH, :].rearrange("p (c d) -> p c d", d=128)
                P.dve("tensor_tensor", [pkb, gate_b], [Kbgb], out=Kbg[:, hs, :], in0=pk3, in1=bc_last(bg[:, gs], 128), op=ALU.mult)
                P.dve("tensor_tensor", [pkb, gate_b], [kdb], out=kd[:, hs, :], in0=pk3, in1=bc_last(ekd[:, gs], 128), op=ALU.mult)
                P.dve("tensor_tensor", [pvb, gate_b], [Vbb], out=Vb[:, hs, :], in0=pv3, in1=bc_last(b_all[:, gs], 128), op=ALU.mult)
                yield
            if IDT == F32:
                Xm, Xmb = X, Xb
            else:
                Xm, Xmb = X, Xb
            U, Ub = pl["U"].next()
            WT, WTb = pl["WT"].next()
            for half in range(2):
                pu, pub = pspool.next()
                pw, pwb = pspool.next()
                for c4 in range(4):
                    c = half * 4 + c4
                    P.mm([Xmb, Vbb], [pub], pu[0:CH, c4 * 128:(c4 + 1) * 128], Xm[:, c, :], Vb[:, c, :])
                for c4 in range(4):
                    c = half * 4 + c4
                    P.mm([Xmb, Kbgb], [pwb], pw[:, c4 * CH:(c4 + 1) * CH], Kbg[:, c, :], Xm[:, c, :])
                yield
                P.act([pub], [Ub], U[:, half * 4:half * 4 + 4, :].rearrange("p c d -> p (c d)"), pu[0:CH, :], AF.Copy)
                P.dve("tensor_copy", [pwb], [WTb], out=WT[:, half * 4:half * 4 + 4, :].rearrange("p c i -> p (c i)"),
                      in_=pw[:, 0:4 * CH])
                yield
            H[(hh, gi)] = dict(QT=QT, QTb=QTb, aT=aT, aTb=aTb, kd=kd, kdb=kdb, U=U, Ub=Ub, WT=WT, WTb=WTb, zt=zt, ztb=ztb)

        def rec(hh, gi, H):
            pl = PL[hh]
            pspool = rec_ps[hh]
            h = H.pop((hh, gi))
            QT, QTb, aT, aTb, kd, kdb = h["QT"], h["QTb"], h["aT"], h["aTb"], h["kd"], h["kdb"]
            U, Ub, WT, WTb, zt, ztb = h["U"], h["Ub"], h["WT"], h["WTb"], h["zt"], h["ztb"]
            c0 = hh * NCH + gi * GC
            ot, otb = pl["o"].next()
            for c in range(GC):
                col = c0 + c
                cs = slice(c * CH, (c + 1) * CH)
                pws, pwsb = pspool.next()
                P.mm([WTb, Sbf_b[hh]], [pwsb], pws[0:CH, 0:128], WT[:, c, :], Sbf[:, hh, :])
                yield
                vn, vnb = pl["vn"].next()
                P.dve("tensor_tensor", [Ub, pwsb], [vnb], out=vn, in0=U[:, c, :], in1=pws[0:CH, 0:128], op=ALU.subtract)
                yield
                P.mm([QTb, Sbf_b[hh]], [pwsb], pws[0:CH, 128:256], QT[:, cs], Sbf[:, hh, :])
                P.mm([aTb, vnb], [pwsb], pws[0:CH, 256:384], aT[:, c, :], vn)
                pss, pssb = pspool.next()
                P.mm([kdb, vnb], [pssb], pss[:, 0:128], kd[:, c, :], vn)
                yield
                P.dve("scalar_tensor_tensor", [pssb, S_b[hh], gate_b], [S_b[hh]], out=S32[:, hh, :], in0=S32[:, hh, :],
                      scalar=egl[:, col:col + 1], in1=pss[:, 0:128], op0=ALU.mult, op1=ALU.add)
                yield
                P.act([S_b[hh]], [Sbf_b[hh]], Sbf[:, hh, :], S32[:, hh, :], AF.Copy)
                t1, t1b = pl["t1"].next()
                P.pool("tensor_scalar", [pwsb, gate_b], [t1b], out=t1, in0=pws[0:CH, 128:256], scalar1=egc[:, col:col + 1],
                       scalar2=None, op0=ALU.mult) if False else P.dve(
                    "tensor_scalar", [pwsb, gate_b], [t1b], out=t1, in0=pws[0:CH, 128:256], scalar1=egc[:, col:col + 1],
                    scalar2=None, op0=ALU.mult)
                P.dve("tensor_tensor", [t1b, pwsb], [otb], out=ot[:, c, :], in0=t1, in1=pws[0:CH, 256:384], op=ALU.add)
                yield
            stt, sttb = pl["st"].next()
            sq2, sq2b = pl["sq2"].next()
            P.pool("tensor_tensor", [otb], [sq2b], out=sq2, in0=ot, in1=ot, op=ALU.mult)
            yield
            P.dve("tensor_reduce", [sq2b], [sttb], out=stt[:, 0, :], in_=sq2, axis=mybir.AxisListType.X, op=ALU.add)
            P.dve("tensor_scalar", [sttb], [sttb], out=stt[:, 1, :], in0=stt[:, 0, :], scalar1=1.0 / 128, scalar2=EPS,
                  op0=ALU.mult, op1=ALU.add)
            yield
            P.act([sttb], [sttb], stt[:, 1, :], stt[:, 1, :], AF.Sqrt)
            yield
            P.dve("reciprocal", [sttb], [sttb], out=stt[:, 1, :], in_=stt[:, 1, :])
            yield
            P.pool("tensor_tensor", [otb, sttb], [otb], out=ot, in0=ot, in1=bc_last(stt[:, 1, :], 128), op=ALU.mult)
            P.pool("tensor_tensor", [otb, ztb], [otb], out=ot, in0=ot, in1=zt, op=ALU.mult)
            outs.append(P.dma("sync", [otb], [], o_out[:, gi * GC:(gi + 1) * GC, hh, :], ot))
            yield

        def interleave(gens):
            gens = list(gens)
            while gens:
                for g_ in list(gens):
                    try:
                        next(g_)
                    except StopIteration:
                        gens.remove(g_)

        H = {}
        interleave([prep(hh, 0, H) for hh in range(HPC)])
        for gi in range(NG):
            gens = [rec(hh, gi, H) for hh in range(HPC)]
            if gi + 1 < NG:
                gens += [prep(hh, gi + 1, H) for hh in range(HPC)]
            interleave(gens)
        P.emit(final_wait=outs)
    return nc


def dn_consts():
    i = np.arange(CH)
    tri = (i[:, None] <= i[None, :])
    tris = (i[:, None] > i[None, :])
    ident = np.eye(CH, dtype=bool)
    strict = (i[:, None] > i[None, :])
    inclT = (i[None, :] >= i[:, None])
    return np.ascontiguousarray(np.stack([tri, tris, ident, strict, inclT], 1).astype(np.float32))


DILS = (1, 4, 16)
HALF = SEQ // 2
HALO = 2048
NVB = 48


def build_dil():
    import contextlib
    nc = bass.Bass("TRN2", target_bir_lowering=False)
    di = lambda name, shape: nc.dram_tensor(name, shape, F32, kind="ExternalInput").ap()
    q_in = di("q", [3, 128, HALF])
    k_in = di("k", [3, 128, HALO + HALF])
    v_in = di("vblk", [128, 3, NVB, 128])
    bias_in = di("bias", [128, 3, 384])
    o_out = nc.dram_tensor("oT", [128, HALF], F32, kind="ExternalOutput").ap()
    P = Prog(nc)
    with contextlib.ExitStack() as st:
        sb = lambda name, shape, dt: st.enter_context(nc.sbuf_tensor(name, shape, dt))
        ps = [st.enter_context(nc.psum_tensor("ps%d" % i, [128, 512], F32)) for i in range(8)]
        pspool = Pool(ps)
        qT = sb("qT", [128, 3, HALF], BF16)
        kT = sb("kT", [128, 3, HALO + HALF], BF16)
        vb = sb("vb", [128, 3, NVB, 128], BF16)
        bias = sb("bias_sb", [128, 3, 384], F32)
        ones = sb("ones", [128, 128], BF16)
        accO = sb("accO", [128, HALF], F32)
        accD = sb("accD", [128, HALF], F32)
        qb, kb, vbb, bb, ob, accb = [[Buf("q%d" % g) for g in range(3)], [Buf("k%d" % g) for g in range(3)],
                                     [Buf("v%d" % g) for g in range(3)], Buf("bias"), Buf("ones"), Buf("acc")]
        for g in range(3):
            P.dma("gpsimd", [], [qb[g]], qT[:, g, :], q_in[g])
            P.dma("gpsimd", [], [kb[g]], kT[:, g, :], k_in[g])
            P.dma("gpsimd", [], [vbb[g]], vb[:, g], v_in[:, g])
        P.dma("sync", [], [bb], bias[:], bias_in)
        P.pool("memset", [], [ob], ap=ones[:], constant=1.0)
        tmp_t = sb("tmp", [128, 2, 256], F32)
        tmp_p = Pool([tmp_t[:, i, :] for i in range(2)])
        E_t = sb("Et", [128, 3, 256], BF16)
        E_p = Pool([E_t[:, i, :] for i in range(3)])

        def strided(t3, g, start, d, n):
            base = t3[:, g, start:start + 1]
            a = [list(x) for x in base.ap]
            return bass.AP(base.tensor, base.offset, [a[0], [d, n]])

        for g, d in enumerate(DILS):
            nstream = d
            nblk = HALF // (128 * d)
            vi = 0
            for c in range(nstream):
                po = pd = None
                for m in range(-1, nblk):
                    kcol = HALO + 128 * m * d + c
                    K_ap = strided(kT, g, kcol, d, 128)
                    if m == -1:
                        q0, nq, boff = c, 128, 256
                    elif m == nblk - 1:
                        q0, nq, boff = 128 * m * d + c, 128, 0
                    else:
                        q0, nq, boff = 128 * m * d + c, 256, 0
                    Q_ap = strided(qT, g, q0, d, nq)
                    pss, pssb = pspool.next()
                    P.mm([kb[g], qb[g]], [pssb], pss[:, 0:nq], K_ap, Q_ap)
                    tt, tb = tmp_p.next()
                    P.dve("scalar_tensor_tensor", [pssb, bb], [tb], out=tt[:, 0:nq], in0=pss[:, 0:nq], scalar=SCALE,
                          in1=bias[:, g, boff:boff + nq], op0=ALU.mult, op1=ALU.add)
                    Et, Eb = E_p.next()
                    P.act([tb], [Eb], Et[:, 0:nq], tt[:, 0:nq], AF.Exp)
                    V_ap = vb[:, g, vi, :]
                    vi += 1
                    if m >= 0:
                        P.mm([vbb[g], Eb], [pob], po[:, 0:128], V_ap, Et[:, 0:128], start=False, stop=True)
                        P.mm([ob, Eb], [pdb], pd[:, 0:128], ones[:], Et[:, 0:128], start=False, stop=True)
                        cols = strided(accO, None, 0, 1, 1) if False else None
                        oc = 128 * m * d + c
                        aO = bass.AP(accO[:, oc:oc + 1].tensor, accO[:, oc:oc + 1].offset,
                                     [list(accO[:, oc:oc + 1].ap[0]), [d, 128]])
                        aD = bass.AP(accD[:, oc:oc + 1].tensor, accD[:, oc:oc + 1].offset,
                                     [list(accD[:, oc:oc + 1].ap[0]), [d, 128]])
                        if g == 0:
                            P.act([pob], [accb], aO, po[:, 0:128], AF.Copy)
                            P.dve("tensor_copy", [pdb], [accb], out=aD, in_=pd[:, 0:128])
                        else:
                            P.dve("tensor_tensor", [pob, accb], [accb], out=aO, in0=aO, in1=po[:, 0:128], op=ALU.add)
                            P.dve("tensor_tensor", [pdb, accb], [accb], out=aD, in0=aD, in1=pd[:, 0:128], op=ALU.add)
                    if m < nblk - 1:
                        po, pob = pspool.next()
                        pd, pdb = pspool.next()
                        e0 = 0 if m == -1 else 128
                        P.mm([vbb[g], Eb], [pob], po[:, 0:128], V_ap, Et[:, e0:e0 + 128], start=True, stop=False)
                        P.mm([ob, Eb], [pdb], pd[:, 0:128], ones[:], Et[:, e0:e0 + 128], start=True, stop=False)
        outs = []
        for h in range(HALF // 512):
            sl = slice(h * 512, (h + 1) * 512)
            P.dve("reciprocal", [accb], [accb], out=accD[:, sl], in_=accD[:, sl])
            P.dve("tensor_tensor", [accb], [accb], out=accO[:, sl], in0=accO[:, sl], in1=accD[:, sl], op=ALU.mult)
            outs.append(P.dma("sync", [accb], [], o_out[:, sl], accO[:, sl]))
        P.emit(final_wait=outs)
    return nc


def alibi_slopes():
    return np.exp2(-8.0 * np.arange(1, 13, dtype=np.float32) / 12).astype(np.float32)


def dil_bias(j, s):
    sl = alibi_slopes()
    kq = np.arange(256)[None, :] - np.arange(128)[:, None]
    valid = (kq >= 0) & (kq <= 128)
    out = np.full((128, 3, 384), -1e30, np.float32)
    for g, d in enumerate(DILS):
        b = np.where(valid, -sl[4 * g + j] * (kq * d).astype(np.float32), np.float32(-1e30)).astype(np.float32)
        out[:, g, 0:256] = b
        if s > 0:
            out[:, g, 256:384] = b[:, 128:256]
    return out


def dil_layout(q, k, v, j, s):
    t0 = s * HALF
    qs = np.stack([q[t0:t0 + HALF, 4 * g + j, :].T for g in range(3)])
    kext = np.zeros((3, 128, HALO + HALF), np.float32)
    vblk = np.zeros((128, 3, NVB, 128), np.float32)
    for g, d in enumerate(DILS):
        h = 4 * g + j
        lo = t0 - HALO
        src_lo = max(lo, 0)
        kext[g][:, src_lo - lo:] = k[src_lo:t0 + HALF, h, :].T
        nblk = HALF // (128 * d)
        vi = 0
        for c in range(d):
            for m in range(-1, nblk):
                tok = t0 + (128 * m + np.arange(128)) * d + c
                if tok[0] >= 0:
                    vblk[:, g, vi, :] = v[tok, h, :]
                vi += 1
    return {"q": np.ascontiguousarray(qs), "k": kext, "vblk": vblk, "bias": dil_bias(j, s)}


_NC_CACHE = {}


def _prog(key, fn):
    if key not in _NC_CACHE:
        _NC_CACHE[key] = fn()
    return _NC_CACHE[key]


def _run(nc, in_maps):
    res = run_bass_kernel_spmd(nc, in_maps, core_ids=list(range(len(in_maps))))
    return res.results


def kernel(x, mem, norm_gains, ffn_w_in, ffn_w_out, mem_norm_gain, w_mem_kv, dn_w_in, dn_conv, dn_a_log,
           dn_dt_bias, dn_o_norm, dn_w_out, kv_norm_gain, w_kv, dil_w_in, dil_w_out):
    f32 = np.float32
    x = np.asarray(x, f32)
    mem = np.asarray(mem, f32)
    A = lambda a: np.ascontiguousarray(np.asarray(a, f32))
    norm_gains, ffn_w_in, ffn_w_out = A(norm_gains), A(ffn_w_in), A(ffn_w_out)
    mem_norm_gain, w_mem_kv, dn_w_in, dn_conv = A(mem_norm_gain), A(w_mem_kv), A(dn_w_in), A(dn_conv)
    dn_a_log, dn_dt_bias, dn_o_norm, dn_w_out = A(dn_a_log), A(dn_dt_bias), A(dn_o_norm), A(dn_w_out)
    kv_norm_gain, w_kv, dil_w_in, dil_w_out = A(kv_norm_gain), A(w_kv), A(dil_w_in), A(dil_w_out)
    kinds = ["dn", "dn", "dil", "dil"]
    memT = np.ascontiguousarray(mem[0].T)
    xs = [np.ascontiguousarray(x[0, i * TOK:(i + 1) * TOK].T) for i in range(NCORES)]
    consts = dn_consts()
    o_in = None
    ksh = vsh = None
    for stage in range(5):
        prev = kinds[stage - 1] if stage > 0 else None
        nxt = kinds[stage] if stage < 4 else None
        has_kv = (stage == 2)
        nc = _prog(("ts", prev, nxt, has_kv), lambda: build_ts(prev, nxt, has_kv))
        g14 = np.zeros((14, D), f32)
        common = {}
        if prev is not None:
            lp = stage - 1
            g14[0:6] = norm_gains[lp]
            common.update(w_o=(dn_w_out[lp] if prev == "dn" else dil_w_out[lp - 2]),
                          f2_in=ffn_w_in[lp, 1], f2_out=ffn_w_out[lp, 1])
        if nxt is not None:
            ln = stage
            g14[6:12] = norm_gains[ln]
            g14[12] = mem_norm_gain[ln]
            g14[13] = kv_norm_gain
            common.update(f1_in=ffn_w_in[ln, 0], f1_out=ffn_w_out[ln, 0], memT=memT, w_mkv=w_mem_kv[ln])
            if nxt == "dn":
                common.update(w_p=dn_w_in[ln],
                              dnc=np.ascontiguousarray(np.tile(np.concatenate([dn_dt_bias[ln], dn_a_log[ln]])[None], (128, 1))))
            else:
                common.update(w_p=dil_w_in[ln - 2])
            if has_kv:
                common.update(w_kv=w_kv)
        common["gains"] = gains_layout(g14)
        ims = []
        for i in range(NCORES):
            im = dict(common)
            im["x_in"] = xs[i]
            if prev is not None:
                im["o_in"] = o_in[i]
            ims.append(im)
        res = _run(nc, ims)
        xs = [r["x_out"] for r in res]
        if nxt is None:
            break
        omem = [r["omemT"] for r in res]
        if nxt == "dn":
            ln = stage
            qkvT = np.concatenate([r["qkvT"] for r in res], axis=1)
            zf = np.concatenate([r["z"] for r in res], axis=0)
            gb = np.concatenate([r["gb"] for r in res], axis=0)
            NCH = SEQ // CH
            q4 = qkvT.reshape(3, 12, 128, SEQ)
            z4 = zf.reshape(NCH, CH, 12, 128)
            g3 = gb[:, 0:12].reshape(NCH, CH, 12)
            b3 = gb[:, 12:24].reshape(NCH, CH, 12)
            cw = dn_conv[ln].reshape(4, 3, 12, 128)
            ncd = _prog(("dn",), lambda: build_dn())
            ims = []
            for c in range(NCORES):
                h0 = 2 * (c % 6)
                ims.append({
                    "qkv": np.ascontiguousarray(q4[:, h0:h0 + 2].transpose(1, 0, 2, 3)),
                    "convw": np.ascontiguousarray(cw[:, :, h0:h0 + 2, :].transpose(3, 2, 1, 0).reshape(128, 24)),
                    "zc": np.ascontiguousarray(z4[:, :, h0:h0 + 2].transpose(1, 0, 2, 3)),
                    "gcol": np.ascontiguousarray(g3[:, :, h0:h0 + 2].transpose(1, 2, 0)),
                    "bcol": np.ascontiguousarray(b3[:, :, h0:h0 + 2].transpose(1, 2, 0)),
                    "onorm": np.ascontiguousarray(np.tile(dn_o_norm[ln][None], (CH, 1))),
                    "consts": consts,
                })
            res = _run(ncd, ims)
            of = np.zeros((SEQ, 12, 128), f32)
            for c in range(6):
                of[:, 2 * c:2 * c + 2] = res[c]["o"].transpose(1, 0, 2, 3).reshape(SEQ, 2, 128)
            of = of.reshape(SEQ, 1536)
        else:
            qT = np.concatenate([r["qT"] for r in res], axis=1)
            if has_kv:
                ksh = np.ascontiguousarray(np.concatenate([r["kT"] for r in res], axis=1).T).reshape(SEQ, 12, 128)
                vsh = np.concatenate([r["v"] for r in res], axis=0).reshape(SEQ, 12, 128)
            qf = np.ascontiguousarray(qT.T).reshape(SEQ, 12, 128)
            ncl = _prog(("dil",), lambda: build_dil())
            ims = [dil_layout(qf, ksh, vsh, c // 2, c % 2) for c in range(NCORES)]
            res = _run(ncl, ims)
            of = np.zeros((SEQ, 4, 128), f32)
            for c in range(NCORES):
                j, s = c // 2, c % 2
                of[s * HALF:(s + 1) * HALF, j] = res[c]["oT"].T
            of = of.reshape(SEQ, 512)
        o_in = [np.ascontiguousarray(np.concatenate([of[i * TOK:(i + 1) * TOK].T, omem[i]], axis=0)) for i in range(NCORES)]
    out = np.concatenate([xi.T for xi in xs], axis=0)[None]
    return np.ascontiguousarray(out.astype(f32))
```

```python
import math
import numpy as np
import concourse.bass as bass
import concourse.mybir as mybir
from concourse.bass_utils import run_bass_kernel_spmd

F32 = mybir.dt.float32
BF16 = mybir.dt.bfloat16
AF = mybir.ActivationFunctionType
ALU = mybir.AluOpType

D = 2048
KT = D // 128
SEQ = 8192
NCORES = 8
TOK = SEQ // NCORES
DFF = 5504
FT = DFF // 128
EPS = 1e-6


class Buf:
    __slots__ = ("name", "last_w", "readers")

    def __init__(self, name):
        self.name = name
        self.last_w = None
        self.readers = []


class Instr:
    __slots__ = ("eng", "fn", "dma", "deps", "signals", "sig_idx", "sem", "semval", "idx")

    def __init__(self, eng, fn, dma):
        self.eng = eng
        self.fn = fn
        self.dma = dma
        self.deps = []
        self.signals = False
        self.sig_idx = 0
        self.sem = None
        self.semval = 0
        self.idx = 0


ENGS = ("tensor", "vector", "scalar", "gpsimd", "sync")
N_DMA_SEMS = 6


class Prog:
    def __init__(self, nc):
        self.nc = nc
        self.streams = {e: [] for e in ENGS}
        self.n = 0

    def op(self, eng, fn, reads=(), writes=(), dma=False):
        ins = Instr(eng, fn, dma)
        ins.idx = self.n
        self.n += 1
        deps = {}
        for b in reads:
            if b.last_w is not None:
                deps[id(b.last_w)] = b.last_w
        for b in writes:
            if b.last_w is not None:
                deps[id(b.last_w)] = b.last_w
            for r in b.readers:
                deps[id(r)] = r
        deps.pop(id(ins), None)
        ins.deps = list(deps.values())
        for b in reads:
            if not dma:
                b.readers = [r for r in b.readers if r.dma or r.eng != eng]
            b.readers.append(ins)
        for b in writes:
            b.last_w = ins
            b.readers = []
        self.streams[eng].append(ins)
        return ins

    def mm(self, reads, writes, out, lhsT, rhs, start=True, stop=True):
        return self.op("tensor", ("matmul", dict(out=out, lhsT=lhsT, rhs=rhs, start=start, stop=stop)), reads, writes)

    def tr(self, reads, writes, out, in_, identity):
        return self.op("tensor", ("transpose", dict(out=out, in_=in_, identity=identity)), reads, writes)

    def dve(self, meth, reads, writes, **kw):
        return self.op("vector", (meth, kw), reads, writes)

    def act(self, reads, writes, out, in_, func, **kw):
        return self.op("scalar", ("activation", dict(out=out, in_=in_, func=func, **kw)), reads, writes)

    def pool(self, meth, reads, writes, **kw):
        return self.op("gpsimd", (meth, kw), reads, writes)

    def dma(self, eng, reads, writes, out, in_):
        return self.op(eng, ("dma_start", dict(out=out, in_=in_)), reads, writes, dma=True)

    def emit(self, final_wait=()):
        nc = self.nc
        def need_sync(ins, d):
            if d.dma:
                return True
            if d.eng == ins.eng:
                if ins.eng == "tensor" and not ins.dma:
                    return False
                return True
            return True

        for e in ENGS:
            for ins in self.streams[e]:
                for d in ins.deps:
                    if not d.dma and need_sync(ins, d):
                        d.signals = True
        import contextlib
        with contextlib.ExitStack() as st:
            esem = {e: st.enter_context(nc.semaphore("s_" + e)) for e in ENGS}
            dsem = {e: [st.enter_context(nc.semaphore("d_%s%d" % (e, i))) for i in range(N_DMA_SEMS)]
                    for e in ("gpsimd", "sync", "scalar")}
            for e in ENGS:
                c = 0
                k = 0
                for ins in self.streams[e]:
                    if ins.dma:
                        ins.sem = dsem[e][k % N_DMA_SEMS]
                        ins.semval = 16 * (k // N_DMA_SEMS + 1)
                        k += 1
                    elif ins.signals:
                        c += 1
                        ins.sig_idx = c
            block = st.enter_context(nc.Block())
            streams = self.streams

            def run(e, eng):
                seen = {}
                k = 0
                prev_on_sem = {}
                for ins in streams[e]:
                    waits = {}
                    for d in ins.deps:
                        if d.dma:
                            key = ("d", id(d.sem))
                            if waits.get(key, (None, 0))[1] < d.semval:
                                waits[key] = (d.sem, d.semval)
                        elif need_sync(ins, d):
                            key = ("e", d.eng)
                            if waits.get(key, (None, 0))[1] < d.sig_idx:
                                waits[key] = (esem[d.eng], d.sig_idx)
                    if ins.dma:
                        key = ("d", id(ins.sem))
                        pv = ins.semval - 16
                        if pv > 0 and waits.get(key, (None, 0))[1] < pv:
                            waits[key] = (ins.sem, pv)
                    for key, (sem, val) in waits.items():
                        if seen.get(key, 0) >= val:
                            continue
                        seen[key] = val
                        eng.wait_ge(sem, val)
                    r = getattr(eng, ins.fn[0])(**ins.fn[1])
                    if ins.dma:
                        r.then_inc(ins.sem, 16)
                    elif ins.signals:
                        r.then_inc(esem[e], 1)
                if e == "sync":
                    for d in final_wait:
                        eng.wait_ge(d.sem, d.semval)

            @block.tensor
            def _(eng):
                run("tensor", eng)

            @block.vector
            def _(eng):
                run("vector", eng)

            @block.scalar
            def _(eng):
                run("scalar", eng)

            @block.gpsimd
            def _(eng):
                run("gpsimd", eng)

            @block.sync
            def _(eng):
                run("sync", eng)


class Pool:
    def __init__(self, tiles):
        self.tiles = tiles
        self.bufs = [Buf("pool") for _ in tiles]
        self.i = 0

    def next(self):
        i = self.i % len(self.tiles)
        self.i += 1
        return self.tiles[i], self.bufs[i]


MEM = 256
DN_W = 1536
A_IN = 4 * DN_W + 24 + 512
SCALE = 128 ** -0.5


class Env:
    pass


class StopBuild(Exception):
    pass


CUT = [0]


def cut(n):
    if CUT[0] == n:
        raise StopBuild()


class DT:
    def __init__(self, ap, nk, name="dt"):
        self.ap = ap
        self.nk = nk
        self.v = ap.rearrange("(k p) t -> p k t", p=128)
        self.b = [Buf("%s%d" % (name, k)) for k in range(nk)]

    def tile(self, k):
        return self.v[:, k, :]


def make_env(nc, st, P, T=TOK):
    E = Env()
    E.nc, E.P, E.T, E.st = nc, P, T, st
    E.NH = T // 512
    sb = lambda name, shape, dt: st.enter_context(nc.sbuf_tensor(name, shape, dt))
    E.sb = sb
    E.hT = sb("hT", [128, KT, T], BF16)
    E.hb = [Buf("h%d" % k) for k in range(KT)]
    E.actT = sb("actT", [128, FT, T], BF16)
    E.ab = [Buf("a%d" % f) for f in range(FT)]
    NW = 6
    wt = sb("wts", [128, NW, 2048], BF16)
    E.wpool = Pool([wt[:, i, :] for i in range(NW)])
    rs = sb("rstd", [128, 2, T], F32)
    E.rstds = [rs[:, 0, :], rs[:, 1, :]]
    E.rstd_bs = [Buf("rstd0"), Buf("rstd1")]
    E.ri = 0
    sq = sb("sq", [128, 2, T], BF16)
    E.sqpool = Pool([sq[:, i, :] for i in range(2)])
    tmp = sb("tmpf", [128, 4, 512], F32)
    E.tmppool = Pool([tmp[:, i, :] for i in range(4)])
    xin = sb("xin", [128, 3, T], F32)
    E.xpool = Pool([xin[:, i, :] for i in range(3)])
    E.ones = sb("ones", [128, 128], BF16)
    E.ones_b = Buf("ones")
    E.onef = sb("onef", [1, 1], F32)
    E.rcol = sb("rcol", [128, T // 128], F32)
    E.rcol_b = Buf("rcol")
    E.gains = sb("gains_sb", [128, 14 * KT], F32)
    E.gains_b = Buf("gains")
    stg = sb("stg", [128, 4, 512], F32)
    E.stgpool = Pool([stg[:, i, :] for i in range(4)])
    E.evac_i = 0
    E.pending = []
    ps = [st.enter_context(nc.psum_tensor("ps%d" % i, [128, 512], F32)) for i in range(8)]
    E.pspool = Pool(ps[0:6])
    E.statbanks = [(ps[6], Buf("stat0")), (ps[7], Buf("stat1"))]
    P.pool("memset", [], [E.ones_b], ap=E.ones[:], constant=1.0)
    P.pool("memset", [], [E.ones_b], ap=E.onef[:], constant=1.0)
    return E


def gcol(E, gidx, k):
    return E.gains[:, gidx * KT + k:gidx * KT + k + 1]


def evac(E, out, in_, reads, writes):
    E.evac_i += 1
    if E.evac_i % 2:
        return E.P.act(reads, writes, out, in_, AF.Copy)
    return E.P.dve("tensor_copy", reads, writes, out=out, in_=in_)


def emit_rstd_finish(E, banks, nfeat, coef, rstd, rstd_b, ncol=512):
    P = E.P
    c2 = coef * coef
    for h, (pt, pb) in enumerate(banks):
        sl = slice(h * ncol, (h + 1) * ncol)
        P.dve("tensor_scalar", [pb], [rstd_b], out=rstd[:, sl], in0=pt[:, 0:ncol], scalar1=1.0 / (nfeat * c2),
              scalar2=EPS / c2, op0=ALU.mult, op1=ALU.add)
    P.act([rstd_b], [rstd_b], rstd, rstd, AF.Ln)
    P.act([rstd_b], [rstd_b], rstd, rstd, AF.Exp, scale=-0.5)


def emit_sumsq(E, banks, src, srcb, k, nk, ncol=512, h_only=None):
    P = E.P
    sq, sqb = E.sqpool.next()
    if h_only is None:
        n = ncol * len(banks)
        P.act([srcb], [sqb], sq[:, 0:n], src, AF.Square)
        for h, (pt, pb) in enumerate(banks):
            P.mm([sqb, E.ones_b], [pb], pt[:, 0:ncol], E.ones[:], sq[:, h * ncol:(h + 1) * ncol],
                 start=(k == 0), stop=(k == nk - 1))
    else:
        pt, pb = banks[h_only]
        P.act([srcb], [sqb], sq[:, 0:ncol], src, AF.Square)
        P.mm([sqb, E.ones_b], [pb], pt[:, 0:ncol], E.ones[:], sq[:, 0:ncol], start=(k == 0), stop=(k == nk - 1))


def emit_norm_in_from_dram(E, X, gidx):
    P = E.P
    banks = E.statbanks
    for k in range(KT):
        xt, xb = E.xpool.next()
        P.dma("sync", [X.b[k]], [xb], xt, X.tile(k))
        emit_sumsq(E, banks, xt, xb, k, KT)
        P.dve("tensor_scalar", [xb, E.gains_b], [E.hb[k]], out=E.hT[:, k, :], in0=xt, scalar1=gcol(E, gidx, k),
              scalar2=None, op0=ALU.mult)
    E.ri ^= 1
    emit_rstd_finish(E, banks, D, 1.0, E.rstds[E.ri], E.rstd_bs[E.ri])


def emit_y_evac(E, py, pyb, j, h, nj):
    P = E.P
    sl = slice(h * 512, (h + 1) * 512)
    flush_pending(E)
    P.dve("tensor_copy", [pyb], [E.hb[j]], out=E.hT[:, j, sl], in_=py[:])
    sq, sqb = E.sqpool.next()
    P.act([E.hb[j]], [sqb], sq[:, 0:512], E.hT[:, j, sl], AF.Square)
    pt, pb = E.statbanks[h]
    E.pending.append(dict(reads=[sqb, E.ones_b], writes=[pb], out=pt[:, 0:512], lhsT=E.ones[:], rhs=sq[:, 0:512],
                          start=(j == 0), stop=(j == nj - 1)))


def flush_pending(E):
    for p in E.pending:
        E.P.mm(p["reads"], p["writes"], p["out"], p["lhsT"], p["rhs"], start=p["start"], stop=p["stop"])
    E.pending = []


def emit_postnorm_residual(E, X, XO, gidx, coef, next_gidx=None):
    P = E.P
    flush_pending(E)
    E.ri ^= 1
    ry, ryb = E.rstds[E.ri], E.rstd_bs[E.ri]
    emit_rstd_finish(E, E.statbanks, D, coef, ry, ryb)
    outs = []
    tiles = {}

    def load(k):
        xt, xb = E.xpool.next()
        P.dma("sync", [X.b[k]], [xb], xt, X.tile(k))
        tiles[k] = (xt, xb)
    load(0)
    for k in range(KT):
        if k + 1 < KT:
            load(k + 1)
        xt, xb = tiles.pop(k)
        for h in range(E.NH):
            sl = slice(h * 512, (h + 1) * 512)
            tt, tb = E.tmppool.next()
            P.dve("scalar_tensor_tensor", [E.hb[k], ryb, E.gains_b], [tb], out=tt, in0=E.hT[:, k, sl],
                  scalar=gcol(E, gidx, k), in1=ry[:, sl], op0=ALU.mult, op1=ALU.mult)
            if h == 0:
                P.pool("tensor_tensor", [tb, xb], [xb], out=xt[:, sl], in0=tt, in1=xt[:, sl], op=ALU.add)
            else:
                P.dve("tensor_tensor", [tb, xb], [xb], out=xt[:, sl], in0=tt, in1=xt[:, sl], op=ALU.add)
        outs.append(P.dma("sync", [xb], [XO.b[k]], XO.tile(k), xt))
        if next_gidx is not None:
            emit_sumsq(E, E.statbanks, xt, xb, k, KT)
            P.dve("tensor_scalar", [xb, E.gains_b], [E.hb[k]], out=E.hT[:, k, :], in0=xt, scalar1=gcol(E, next_gidx, k),
                  scalar2=None, op0=ALU.mult)
    if next_gidx is not None:
        E.ri ^= 1
        emit_rstd_finish(E, E.statbanks, D, 1.0, E.rstds[E.ri], E.rstd_bs[E.ri])
    return outs


def load_wchunk(E, W, nk, c0, ncols):
    assert nk * ncols <= 2048
    wt, wb = E.wpool.next()
    wv = wt[:, 0:nk * ncols].rearrange("p (k n) -> p k n", k=nk)
    Wv = W.rearrange("(k p) n -> p k n", p=128)
    E.P.dma("gpsimd", [], [wb], wv, Wv[:, :, c0:c0 + ncols])
    return wv, wb


def emit_ffn(E, X, XO, w_in, w_out, g_post, next_gidx):
    P = E.P
    r, rb = E.rstds[E.ri], E.rstd_bs[E.ri]
    for f in range(FT):
        wgv, wgb = load_wchunk(E, w_in, KT, f * 128, 128)
        wuv, wub = load_wchunk(E, w_in, KT, DFF + f * 128, 128)
        for h in range(E.NH):
            sl = slice(h * 512, (h + 1) * 512)
            pg, pgb = E.pspool.next()
            pu, pub = E.pspool.next()
            for k in range(KT):
                P.mm([wgb, E.hb[k]], [pgb], pg[:], wgv[:, k, :], E.hT[:, k, sl], start=(k == 0), stop=(k == KT - 1))
            for k in range(KT):
                P.mm([wub, E.hb[k]], [pub], pu[:], wuv[:, k, :], E.hT[:, k, sl], start=(k == 0), stop=(k == KT - 1))
            t1, t1b = E.tmppool.next()
            t3, t3b = E.tmppool.next()
            P.dve("tensor_tensor", [pgb, rb], [t1b], out=t1, in0=pg[:], in1=r[:, sl], op=ALU.mult)
            P.act([t1b], [t1b], t1, t1, AF.Silu)
            P.dve("tensor_tensor", [pub, rb], [t3b], out=t3, in0=pu[:], in1=r[:, sl], op=ALU.mult)
            P.dve("tensor_tensor", [t1b, t3b], [E.ab[f]], out=E.actT[:, f, sl], in0=t1, in1=t3, op=ALU.mult)
    cut(2)
    w_out_v = w_out.rearrange("(f p) n -> p f n", p=128)
    fr = [(0, 16), (16, 32), (32, FT)]
    for j in range(KT):
        slots = []
        for (f0, f1) in fr:
            wt, wb = E.wpool.next()
            wv = wt.rearrange("p (f n) -> p f n", n=128)
            P.dma("gpsimd", [], [wb], wv[:, 0:f1 - f0, :], w_out_v[:, f0:f1, j * 128:(j + 1) * 128])
            slots.append((wv, wb))
        for h in range(E.NH):
            sl = slice(h * 512, (h + 1) * 512)
            py, pyb = E.pspool.next()
            for (wv, wb), (f0, f1) in zip(slots, fr):
                for f in range(f0, f1):
                    P.mm([wb, E.ab[f]], [pyb], py[:], wv[:, f - f0, :], E.actT[:, f, sl],
                         start=(f == 0), stop=(f == FT - 1))
            emit_y_evac(E, py, pyb, j, h, KT)
    cut(3)
    return emit_postnorm_residual(E, X, XO, g_post, 0.5, next_gidx)


def emit_lin_fm(E, W, nk, in_tiles, in_bufs, c0, consumer, ncols=128):
    P = E.P
    wv, wb = load_wchunk(E, W, nk, c0, ncols)
    for h in range(E.NH):
        sl = slice(h * 512, (h + 1) * 512)
        pt, pb = E.pspool.next()
        for k in range(nk):
            P.mm([wb, in_bufs[k]], [pb], pt[0:ncols, :], wv[:, k, :], in_tiles[k][:, sl],
                 start=(k == 0), stop=(k == nk - 1))
        consumer(pt, pb, h)


def evac_rstd(E, out, pt, pb, h, writes):
    sl = slice(h * 512, (h + 1) * 512)
    return E.P.dve("tensor_tensor", [pb, E.rstd_bs[E.ri]], writes, out=out, in0=pt[:], in1=E.rstds[E.ri][:, sl], op=ALU.mult)


def emit_lin_fm_to_dram(E, W, nk, in_tiles, in_bufs, c0, OUT, row0, outs):
    def cons(pt, pb, h):
        stg, sb_ = E.stgpool.next()
        evac_rstd(E, stg, pt, pb, h, [sb_])
        outs.append(E.P.dma("sync", [sb_], [], OUT[row0:row0 + 128, h * 512:(h + 1) * 512], stg))
    emit_lin_fm(E, W, nk, in_tiles, in_bufs, c0, cons)


def emit_rstd_cols(E):
    P = E.P
    r, rb = E.rstds[E.ri], E.rstd_bs[E.ri]
    pt, pb = E.pspool.next()
    nt = E.T // 128
    for tt in range(nt):
        P.mm([rb, E.ones_b], [pb], pt[:, tt:tt + 1], r[0:1, tt * 128:(tt + 1) * 128], E.onef[:])
    P.dve("tensor_copy", [pb], [E.rcol_b], out=E.rcol[:], in_=pt[:, 0:nt])


def emit_lin_tm_to_dram(E, W, nk, in_tiles, in_bufs, c0, ncols, OUT, ocol0, outs, T=None):
    P = E.P
    T = T or E.T
    nch = ncols // 128
    wvs = [load_wchunk(E, W, nk, c0 + i * 128, 128) for i in range(nch)]
    for tt in range(T // 128):
        pt, pb = E.pspool.next()
        for i, (wv, wb) in enumerate(wvs):
            for k in range(nk):
                P.mm([wb, in_bufs[k]], [pb], pt[:, i * 128:(i + 1) * 128], in_tiles[k][:, tt * 128:(tt + 1) * 128],
                     wv[:, k, :], start=(k == 0), stop=(k == nk - 1))
        stg, sb_ = E.stgpool.next()
        P.dve("tensor_scalar", [pb, E.rcol_b], [sb_], out=stg[:, 0:ncols], in0=pt[:, 0:ncols], scalar1=E.rcol[:, tt:tt + 1],
              scalar2=None, op0=ALU.mult)
        outs.append(P.dma("sync", [sb_], [], OUT[tt * 128:(tt + 1) * 128, ocol0:ocol0 + ncols], stg[:, 0:ncols]))


def emit_mem_kv(E, memT, w_mkv, gidx):
    P = E.P
    mh = E.actT[:, 0:4, :].rearrange("p a (b m) -> p (a b) m", m=MEM)
    mhb = [E.ab[k // 4] for k in range(KT)]
    E.memKT = E.sb("memKT", [128, 4, MEM], BF16)
    E.memKT_b = Buf("memKT")
    E.memV = E.sb("memV", [128, 2, 512], BF16)
    E.memV_b = Buf("memV")
    rstd_m = E.sb("rstd_m", [128, MEM], F32)
    rmb = Buf("rstd_m")
    mv = memT.rearrange("(k p) t -> p k t", p=128)
    banks = [E.pspool.next()]
    for k in range(KT):
        xt, xb = E.xpool.next()
        P.dma("sync", [], [xb], xt[:, 0:MEM], mv[:, k, :])
        emit_sumsq(E, banks, xt[:, 0:MEM], xb, k, KT, ncol=MEM)
        P.dve("tensor_scalar", [xb, E.gains_b], [mhb[k]], out=mh[:, k, :], in0=xt[:, 0:MEM], scalar1=gcol(E, gidx, k),
              scalar2=None, op0=ALU.mult)
    emit_rstd_finish(E, banks, D, 1.0, rstd_m[:], rmb, ncol=MEM)
    rmcol = E.sb("rmcol", [128, 2], F32)
    pc, pcb = E.pspool.next()
    for mt in range(2):
        P.mm([rmb, E.ones_b], [pcb], pc[:, mt:mt + 1], rstd_m[0:1, mt * 128:(mt + 1) * 128], E.onef[:])
    P.dve("tensor_copy", [pcb], [rmb], out=rmcol[:], in_=pc[:, 0:2])
    yield
    for hd in range(4):
        wv, wb = load_wchunk(E, w_mkv, KT, hd * 128, 128)
        pt, pb = E.pspool.next()
        for k in range(KT):
            P.mm([wb, mhb[k]], [pb], pt[:, 0:MEM], wv[:, k, :], mh[:, k, :], start=(k == 0), stop=(k == KT - 1))
        P.dve("tensor_tensor", [pb, rmb], [E.memKT_b], out=E.memKT[:, hd, :], in0=pt[:, 0:MEM], in1=rstd_m[:], op=ALU.mult)
        yield
    for c in range(4):
        wv, wb = load_wchunk(E, w_mkv, KT, 512 + c * 128, 128)
        for mt in range(2):
            pt, pb = E.pspool.next()
            for k in range(KT):
                P.mm([wb, mhb[k]], [pb], pt[:, 0:128], mh[:, k, mt * 128:(mt + 1) * 128], wv[:, k, :],
                     start=(k == 0), stop=(k == KT - 1))
            P.dve("tensor_scalar", [pb, rmb], [E.memV_b], out=E.memV[:, mt, c * 128:(c + 1) * 128], in0=pt[:, 0:128],
                  scalar1=rmcol[:, mt:mt + 1], scalar2=None, op0=ALU.mult)
        yield


def emit_mem_attn(E, OMEM, outs):
    P = E.P
    et = E.sb("ET", [128, 2, 2, 512], BF16)
    etpool = Pool([et[:, i] for i in range(2)])
    for hd in range(4):
        for h in range(E.NH):
            sl = slice(h * 512, (h + 1) * 512)
            ET, ETb = etpool.next()
            for mt in range(2):
                ps, psb = E.pspool.next()
                P.mm([E.memKT_b, E.mq_b[hd]], [psb], ps[:], E.memKT[:, hd, mt * 128:(mt + 1) * 128], E.mqT[:, hd, sl])
                P.act([psb], [ETb], ET[:, mt, :], ps[:], AF.Exp, scale=SCALE)
            po, pob = E.pspool.next()
            pd, pdb = E.pspool.next()
            for mt in range(2):
                P.mm([E.memV_b, ETb], [pob], po[:], E.memV[:, mt, hd * 128:(hd + 1) * 128], ET[:, mt, :],
                     start=(mt == 0), stop=(mt == 1))
            for mt in range(2):
                P.mm([E.ones_b, ETb], [pdb], pd[:], E.ones[:], ET[:, mt, :], start=(mt == 0), stop=(mt == 1))
            tt, tb = E.tmppool.next()
            P.dve("reciprocal", [pdb], [tb], out=tt, in_=pd[:])
            stg, sb_ = E.stgpool.next()
            P.dve("tensor_tensor", [tb, pob], [sb_], out=stg, in0=po[:], in1=tt, op=ALU.mult)
            outs.append(P.dma("sync", [sb_], [], OMEM[hd * 128:(hd + 1) * 128, sl], stg))
            yield


def build_ts(prev, nxt, has_kv):
    import contextlib
    nc = bass.Bass("TRN2", target_bir_lowering=False)
    T = TOK
    di = lambda name, shape: nc.dram_tensor(name, shape, F32, kind="ExternalInput").ap()
    do = lambda name, shape: nc.dram_tensor(name, shape, F32, kind="ExternalOutput").ap()
    dint = lambda name, shape: nc.dram_tensor(name, shape, F32, kind="Internal").ap()
    x_in = di("x_in", [D, T])
    gains = di("gains", [128, 14 * KT])
    P = Prog(nc)
    outs = []
    with contextlib.ExitStack() as st:
        E = make_env(nc, st, P)
        P.dma("sync", [], [E.gains_b], E.gains[:], gains)
        X = DT(x_in, KT, "xin")
        hin = [E.hT[:, k, :] for k in range(KT)]
        try:
            _build_ts_body(nc, E, P, X, hin, prev, nxt, has_kv, outs, di, do, dint, T)
        except StopBuild:
            pass
        P.emit(final_wait=outs)
    return nc


def _build_ts_body(nc, E, P, X, hin, prev, nxt, has_kv, outs, di, do, dint, T):
    if True:
        if prev is not None:
            nko = 16 if prev == "dn" else 8
            o_in = di("o_in", [nko * 128, T])
            w_o = di("w_o", [nko * 128, D])
            f2_in = di("f2_in", [D, 2 * DFF])
            f2_out = di("f2_out", [DFF, D])
            ov = o_in.rearrange("(k p) t -> p k t", p=128)
            for k in range(nko):
                P.dma("gpsimd", [], [E.ab[k]], E.actT[:, k, :], ov[:, k, :])
            ain = [E.actT[:, k, :] for k in range(nko)]
            for j in range(KT):
                def cons(pt, pb, h, j=j):
                    emit_y_evac(E, pt, pb, j, h, KT)
                emit_lin_fm(E, w_o, nko, ain, E.ab, j * 128, cons)
            X2 = DT(dint("x2", [D, T]), KT, "x2")
            emit_postnorm_residual(E, X, X2, 3, 1.0, next_gidx=4)
            if nxt is None:
                X3 = DT(do("x_out", [D, T]), KT, "x3")
            else:
                X3 = DT(dint("x3", [D, T]), KT, "x3")
            o3 = emit_ffn(E, X2, X3, f2_in, f2_out, 5, (None if nxt is None else (13 if has_kv else 6)))
            if nxt is None:
                outs += o3
            X = X3
        if nxt is not None:
            f1_in = di("f1_in", [D, 2 * DFF])
            f1_out = di("f1_out", [DFF, D])
            memT = di("memT", [D, MEM])
            w_mkv = di("w_mkv", [D, 1024])
            if prev is None:
                emit_norm_in_from_dram(E, X, 6)
                cut(1)
            if has_kv:
                w_kv = di("w_kv", [D, 3072])
                kT_o = do("kT", [1536, T])
                v_o = do("v", [T, 1536])
                emit_rstd_cols(E)
                for c in range(12):
                    emit_lin_fm_to_dram(E, w_kv, KT, hin, E.hb, c * 128, kT_o, c * 128, outs)
                for c in range(3):
                    emit_lin_tm_to_dram(E, w_kv, KT, hin, E.hb, 1536 + c * 512, 512, v_o, c * 512, outs)
                emit_norm_in_from_dram(E, X, 6)
            X1 = DT(do("x_out", [D, T]), KT, "x1")
            outs += emit_ffn(E, X, X1, f1_in, f1_out, 7, 8)
            cut(4)
            E.mqT = E.sb("mqT", [128, 4, T], BF16)
            E.mq_b = [Buf("mq%d" % i) for i in range(4)]
            omem = do("omemT", [512, T])
            w_p = di("w_p", [D, A_IN if nxt == "dn" else 2048])
            mq0 = 6168 if nxt == "dn" else 1536
            for hd in range(4):
                def cons(pt, pb, h, hd=hd):
                    evac_rstd(E, E.mqT[:, hd, h * 512:(h + 1) * 512], pt, pb, h, [E.mq_b[hd]])
                emit_lin_fm(E, w_p, KT, hin, E.hb, mq0 + hd * 128, cons)

            def _bg():
                yield from emit_mem_kv(E, memT, w_mkv, 12)
                yield from emit_mem_attn(E, omem, outs)
            bg = _bg()
            if nxt == "dn":
                qkvT = do("qkvT", [4608, T])
                z_o = do("z", [T, 1536])
                gb_o = do("gb", [T, 24])
                dnc = di("dnc", [128, 24])
                emit_rstd_cols(E)
                cut(5)
                for c in range(36):
                    emit_lin_fm_to_dram(E, w_p, KT, hin, E.hb, c * 128, qkvT, c * 128, outs)
                    next(bg, None)
                for c in range(3):
                    emit_lin_tm_to_dram(E, w_p, KT, hin, E.hb, 4608 + c * 512, 512, z_o, c * 512, outs)
                cut(6)
                dn_c = E.sb("dnc_sb", [128, 24], F32)
                dn_cb = Buf("dnc")
                nea = E.sb("nea", [128, 12], F32)
                neab = Buf("nea")
                gt = E.sb("gt", [128, 2, 24], F32)
                gtpool = Pool([gt[:, i, :] for i in range(2)])
                P.dma("sync", [], [dn_cb], dn_c[:], dnc)
                P.act([dn_cb], [neab], nea[:], dn_c[:, 12:24], AF.Exp)
                P.dve("tensor_scalar", [neab], [neab], out=nea[:], in0=nea[:], scalar1=-1.0, scalar2=None, op0=ALU.mult)
                wv, wb = load_wchunk(E, w_p, KT, 6144, 24)
                for tt in range(T // 128):
                    pt, pb = E.pspool.next()
                    for k in range(KT):
                        P.mm([wb, E.hb[k]], [pb], pt[:, 0:24], E.hT[:, k, tt * 128:(tt + 1) * 128], wv[:, k, :],
                             start=(k == 0), stop=(k == KT - 1))
                    g1, g1b = gtpool.next()
                    stg, sb_ = E.stgpool.next()
                    P.dve("tensor_scalar", [pb, E.rcol_b], [g1b], out=g1[:, 0:24], in0=pt[:, 0:24], scalar1=E.rcol[:, tt:tt + 1],
                          scalar2=None, op0=ALU.mult)
                    P.act([g1b], [sb_], stg[:, 12:24], g1[:, 12:24], AF.Sigmoid)
                    P.dve("tensor_tensor", [g1b, dn_cb], [g1b], out=g1[:, 0:12], in0=g1[:, 0:12], in1=dn_c[:, 0:12], op=ALU.add)
                    P.act([g1b], [g1b], g1[:, 0:12], g1[:, 0:12], AF.Exp)
                    P.dve("tensor_scalar", [g1b], [g1b], out=g1[:, 0:12], in0=g1[:, 0:12], scalar1=1.0, scalar2=None, op0=ALU.add)
                    P.act([g1b], [g1b], g1[:, 0:12], g1[:, 0:12], AF.Ln)
                    P.dve("tensor_tensor", [g1b, neab, sb_], [sb_], out=stg[:, 0:12], in0=g1[:, 0:12], in1=nea[:], op=ALU.mult)
                    outs.append(P.dma("sync", [sb_], [], gb_o[tt * 128:(tt + 1) * 128, :], stg[:, 0:24]))
            else:
                qT = do("qT", [1536, T])
                for c in range(12):
                    emit_lin_fm_to_dram(E, w_p, KT, hin, E.hb, c * 128, qT, c * 128, outs)
                    next(bg, None)
            for _ in bg:
                pass


def gains_layout(g):
    G = g.shape[0]
    return np.ascontiguousarray(g.reshape(G, KT, 128).transpose(2, 0, 1).reshape(128, G * KT))


CH = 64
GC = 8
GT = CH * GC


def bc_last(ap, n):
    return bass.AP(ap.tensor, ap.offset, [list(x) for x in ap.ap] + [[0, n]])


def bc_mid(ap, n):
    a = [list(x) for x in ap.ap]
    return bass.AP(ap.tensor, ap.offset, [a[0], [0, n]] + a[1:])


def build_dn(S=SEQ, HPC=2, IDT=BF16, stage=99):
    import contextlib
    nc = bass.Bass("TRN2", target_bir_lowering=False)
    NCH = S // CH
    NG = S // GT
    di = lambda name, shape: nc.dram_tensor(name, shape, F32, kind="ExternalInput").ap()
    qkv = di("qkv", [HPC, 3, 128, S])
    convw = di("convw", [128, HPC * 3 * 4])
    zc = di("zc", [CH, NCH, HPC, 128])
    gcol = di("gcol", [CH, HPC, NCH])
    bcol = di("bcol", [CH, HPC, NCH])
    onorm = di("onorm", [CH, 128])
    consts = di("consts", [CH, 5, CH])
    o_out = nc.dram_tensor("o", [CH, NCH, HPC, 128], F32, kind="ExternalOutput").ap()
    P = Prog(nc)
    outs = []
    with contextlib.ExitStack() as st:
        sb = lambda name, shape, dt: st.enter_context(nc.sbuf_tensor(name, shape, dt))
        ps = [st.enter_context(nc.psum_tensor("ps%d" % i, [128, 512], F32)) for i in range(8)]
        pspool = Pool(ps)
        cst = sb("cst", [CH, 5, CH], F32)
        cst_b = Buf("cst")
        P.dma("sync", [], [cst_b], cst[:], consts)
        tri, tris, ident, strict, inclT = [cst[:, i, :] for i in range(5)]
        ones = sb("ones32", [128, 128], F32)
        ones_b = Buf("ones")
        P.pool("memset", [], [ones_b], ap=ones[:], constant=1.0)
        ident128 = sb("ident128", [128, 128], F32)
        id_b = Buf("id128")
        P.pool("memset", [], [id_b], ap=ident128[:], constant=0.0)
        P.dma("sync", [id_b], [id_b], ident128[0:CH, 0:CH], consts[:, 2, :])
        P.dma("sync", [id_b], [id_b], ident128[CH:128, CH:128], consts[:, 2, :])
        identb = sb("identb", [CH, CH], BF16)
        P.dve("tensor_copy", [cst_b], [cst_b], out=identb[:], in_=cst[:, 2, :])
        cw = sb("cw", [128, HPC * 12], F32)
        cw_b = Buf("cw")
        P.dma("sync", [], [cw_b], cw[:], convw)
        dg = sb("dg", [128, HPC * 12, 128], BF16)
        dg_b = Buf("dg")
        for col in range(HPC * 12):
            (P.dve if col % 2 else P.pool)("tensor_scalar", [id_b, cw_b, dg_b], [dg_b], out=dg[:, col, :], in0=ident128[:],
                                           scalar1=cw[:, col:col + 1], scalar2=None, op0=ALU.mult)
        on = sb("on", [CH, 128], F32)
        on_b = Buf("on")
        P.dma("sync", [], [on_b], on[:], onorm)
        NA = HPC * NCH
        g_all = sb("g_all", [CH, NA], F32)
        b_all = sb("b_all", [CH, NA], F32)
        gc_all = sb("gc_all", [CH, NA], F32)
        egc = sb("egc", [CH, NA], F32)
        bg = sb("bg", [CH, NA], F32)
        ekd = sb("ekd", [CH, NA], F32)
        egl = sb("egl", [128, NA], F32)
        gate_b = Buf("gates")
        P.dma("sync", [], [gate_b], g_all[:], gcol.rearrange("p h n -> p (h n)"))
        P.dma("sync", [gate_b], [gate_b], b_all[:], bcol.rearrange("p h n -> p (h n)"))
        strict_x = sb("strict_x", [CH, GC, CH], F32)
        inclT_x = sb("inclT_x", [CH, GC, CH], F32)
        ident_x = sb("ident_x", [CH, GC, CH], F32)
        mx_b = Buf("maskx")
        P.dve("tensor_copy", [cst_b], [mx_b], out=strict_x[:], in_=bc_mid(strict, GC))
        P.dve("tensor_copy", [cst_b, mx_b], [mx_b], out=inclT_x[:], in_=bc_mid(inclT, GC))
        P.dve("tensor_copy", [cst_b, mx_b], [mx_b], out=ident_x[:], in_=bc_mid(ident, GC))
        assert NA <= 512
        p1, p1b = pspool.next()
        P.mm([cst_b, gate_b], [p1b], p1[0:CH, 0:NA], tri, g_all[:])
        P.dve("tensor_copy", [p1b], [gate_b], out=gc_all[:], in_=p1[0:CH, 0:NA])
        p2, p2b = pspool.next()
        P.mm([ones_b, gate_b], [p2b], p2[:, 0:NA], ones[0:CH, :], g_all[:])
        P.act([p2b], [gate_b], egl[:], p2[:, 0:NA], AF.Exp)
        P.act([gate_b], [gate_b], egc[:], gc_all[:], AF.Exp)
        P.dve("tensor_tensor", [gate_b], [gate_b], out=bg[:], in0=b_all[:], in1=egc[:], op=ALU.mult)
        P.dve("tensor_tensor", [gate_b, p2b], [gate_b], out=ekd[:], in0=p2[0:CH, 0:NA], in1=gc_all[:], op=ALU.subtract)
        P.act([gate_b], [gate_b], ekd[:], ekd[:], AF.Exp)

        def mk(name, shape, dt, n):
            t = sb(name, [shape[0], n] + shape[1:], dt)
            return Pool([t[:, i] for i in range(n)])
        PL = []
        for hh in range(HPC):
            s_ = "_%d" % hh
            PL.append(dict(
                raw=mk("raw" + s_, [128, 3, 3 + GT], BF16, 2), cv=mk("cv" + s_, [128, 3, GT], F32, 1),
                sq=mk("sqd" + s_, [128, GT], F32, 2), rs=mk("rsd" + s_, [128, GT], F32, 2),
                KT=mk("KTb" + s_, [128, GT], BF16, 1), KV32=mk("KV32" + s_, [128, 2, GT], F32, 1),
                TG=mk("TG" + s_, [CH, GC, CH], F32, 1), DM=mk("DM" + s_, [CH, GC, CH], F32, 1),
                DTM=mk("DTM" + s_, [CH, GC, CH], F32, 1), LM=mk("LM" + s_, [CH, 2, GC, CH], IDT, 3),
                X=mk("Xc" + s_, [CH, GC, CH], IDT, 1), Kbg=mk("Kbg" + s_, [CH, GC, 128], IDT, 1),
                Vb=mk("Vb" + s_, [CH, GC, 128], IDT, 1),
                QT=mk("QT" + s_, [128, GT], BF16, 2), kd=mk("kd" + s_, [CH, GC, 128], BF16, 2),
                U=mk("U" + s_, [CH, GC, 128], F32, 2), WT=mk("WT" + s_, [128, GC, CH], BF16, 2),
                aT=mk("aT" + s_, [CH, GC, CH], BF16, 2), z=mk("zt" + s_, [CH, GC, 128], F32, 2),
                o=mk("ot" + s_, [CH, GC, 128], F32, 2), vn=mk("vn" + s_, [CH, 128], BF16, 2),
                t1=mk("t1" + s_, [CH, 128], F32, 2), st=mk("stat" + s_, [CH, 2, GC], F32, 2),
                sq2=mk("sq2" + s_, [CH, GC, 128], F32, 1), oT=mk("oTt" + s_, [128, GT], F32, 2),
            ))
        S32 = sb("S32", [128, HPC, 128], F32)
        Sbf = sb("Sbf", [128, HPC, 128], BF16)
        S_b = [Buf("S%d" % h) for h in range(HPC)]
        Sbf_b = [Buf("Sbf%d" % h) for h in range(HPC)]
        for hh in range(HPC):
            P.pool("memset", [], [S_b[hh]], ap=S32[:, hh, :], constant=0.0)
            P.pool("memset", [], [Sbf_b[hh]], ap=Sbf[:, hh, :], constant=0.0)

        assert HPC == 2
        prep_ps = [Pool(ps[0:2]), Pool(ps[2:4])]
        rec_ps = [Pool(ps[4:6]), Pool(ps[6:8])]
        def prep(hh, gi, H):
            pl = PL[hh]
            pspool = prep_ps[hh]
            t0 = gi * GT
            c0 = hh * NCH + gi * GC
            raw, rawb = pl["raw"].next()
            if gi == 0:
                P.pool("memset", [], [rawb], ap=raw[:, :, 0:3], constant=0.0)
                P.dma("gpsimd", [rawb], [rawb], raw[:, :, 3:3 + GT], qkv[hh, :, :, 0:GT].rearrange("w p t -> p w t"))
            else:
                P.dma("gpsimd", [], [rawb], raw[:], qkv[hh, :, :, t0 - 3:t0 + GT].rearrange("w p t -> p w t"))
            zt, ztb = pl["z"].next()
            P.dma("sync", [], [ztb], zt, zc[:, gi * GC:(gi + 1) * GC, hh, :])
            cv, cvb = pl["cv"].next()
            for w in range(3):
                col = (hh * 3 + w) * 4
                pc, pcb = pspool.next()
                for j in range(4):
                    P.mm([rawb, dg_b], [pcb], pc[:], dg[:, col + j, :], raw[:, w, j:j + GT], start=(j == 0), stop=(j == 3))
                yield
                P.act([pcb], [cvb], cv[:, w, :], pc[:], AF.Silu)
                yield
            P.act([ztb], [ztb], zt, zt, AF.Silu)
            P.pool("tensor_tensor", [ztb, on_b], [ztb], out=zt, in0=zt, in1=bc_mid(on[:], GC), op=ALU.mult)
            yield
            QT, QTb = pl["QT"].next()
            KTt, KTb = pl["KT"].next()
            KV32, KV32b = pl["KV32"].next()
            for w in range(2):
                sq, sqb = pl["sq"].next()
                P.act([cvb], [sqb], sq, cv[:, w, :], AF.Square)
                yield
                pn, pnb = pspool.next()
                P.mm([sqb, ones_b], [pnb], pn[:], ones[:], sq)
                yield
                rs, rsb = pl["rs"].next()
                P.dve("tensor_scalar", [pnb], [rsb], out=rs, in0=pn[:], scalar1=EPS, scalar2=None, op0=ALU.add)
                yield
                P.act([rsb], [rsb], rs, rs, AF.Ln)
                P.act([rsb], [rsb], rs, rs, AF.Exp, scale=-0.5)
                yield
                if w == 0:
                    P.dve("scalar_tensor_tensor", [cvb, rsb], [QTb], out=QT, in0=cv[:, 0, :], scalar=SCALE, in1=rs,
                          op0=ALU.mult, op1=ALU.mult)
                else:
                    P.dve("tensor_tensor", [cvb, rsb], [KV32b], out=KV32[:, 0, :], in0=cv[:, 1, :], in1=rs, op=ALU.mult)
                    yield
                    P.act([KV32b], [KTb], KTt, KV32[:, 0, :], AF.Copy)
                yield
            P.pool("tensor_copy", [cvb, KV32b], [KV32b], out=KV32[:, 1, :], in_=cv[:, 2, :])
            TG, TGb = pl["TG"].next()
            for c in range(GC):
                P.pool("tensor_scalar", [cst_b, gate_b], [TGb], out=TG[:, c, :], in0=tri,
                       scalar1=g_all[:, c0 + c:c0 + c + 1], scalar2=None, op0=ALU.mult)
            yield
            pdt, pdtb = pspool.next()
            P.mm([cst_b, TGb], [pdtb], pdt[0:CH, :], tris, TG.rearrange("p c i -> p (c i)"))
            pd, pdb = pspool.next()
            for c in range(GC):
                P.mm([cst_b, TGb], [pdb], pd[0:CH, c * CH:(c + 1) * CH], TG[:, c, :], tris)
            yield
            DM, DMb = pl["DM"].next()
            DTM, DTMb = pl["DTM"].next()
            P.act([pdb], [DMb], DM.rearrange("p c i -> p (c i)"), pd[0:CH, :], AF.Exp)
            P.act([pdtb], [DTMb], DTM.rearrange("p c i -> p (c i)"), pdt[0:CH, :], AF.Exp)
            yield
            P.pool("tensor_tensor", [DMb, mx_b], [DMb], out=DM, in0=DM, in1=strict_x[:], op=ALU.mult)
            P.pool("tensor_tensor", [DTMb, mx_b], [DTMb], out=DTM, in0=DTM, in1=inclT_x[:], op=ALU.mult)
            yield
            pkk, pkkb = pspool.next()
            pqk, pqkb = pspool.next()
            for c in range(GC):
                cs = slice(c * CH, (c + 1) * CH)
                P.mm([KTb], [pkkb], pkk[0:CH, cs], KTt[:, cs], KTt[:, cs])
            for c in range(GC):
                cs = slice(c * CH, (c + 1) * CH)
                P.mm([KTb, QTb], [pqkb], pqk[0:CH, cs], KTt[:, cs], QT[:, cs])
            yield
            aT, aTb = pl["aT"].next()
            P.dve("tensor_tensor", [pqkb, DTMb], [aTb], out=aT.rearrange("p c i -> p (c i)"), in0=pqk[0:CH, :],
                  in1=DTM.rearrange("p c i -> p (c i)"), op=ALU.mult)
            LM, LMb = pl["LM"].next()
            L0, M0 = LM[:, 0], LM[:, 1]
            for c in range(GC):
                P.dve("scalar_tensor_tensor", [pkkb, DMb, gate_b], [LMb], out=L0[:, c, :], in0=pkk[0:CH, c * CH:(c + 1) * CH],
                      scalar=b_all[:, c0 + c:c0 + c + 1], in1=DM[:, c, :], op0=ALU.mult, op1=ALU.mult)
            yield
            ptr, ptrb = pspool.next()
            idl = ident if IDT == F32 else identb[:]
            for c in range(GC):
                P.mm([LMb, cst_b], [ptrb], ptr[0:CH, c * CH:(c + 1) * CH], L0[:, c, :], idl)
            yield
            P.act([ptrb], [LMb], M0.rearrange("p c i -> p (c i)"), ptr[0:CH, :], AF.Copy)
            yield
            X, Xb = pl["X"].next()
            P.dve("tensor_tensor", [LMb, mx_b], [Xb], out=X, in0=ident_x[:], in1=M0, op=ALU.subtract)
            Lc, Mc, LMcb = L0, M0, LMb
            for r in range(5):
                LMn, LMnb = pl["LM"].next()
                Ln, Mn = LMn[:, 0], LMn[:, 1]
                pl_, plb = pspool.next()
                pm, pmb = pspool.next()
                for c in range(GC):
                    P.mm([LMcb], [plb], pl_[0:CH, c * CH:(c + 1) * CH], Mc[:, c, :], Lc[:, c, :])
                for c in range(GC):
                    P.mm([LMcb], [pmb], pm[0:CH, c * CH:(c + 1) * CH], Lc[:, c, :], Mc[:, c, :])
                yield
                P.act([plb], [LMnb], Ln.rearrange("p c i -> p (c i)"), pl_[0:CH, :], AF.Copy)
                P.dve("tensor_copy", [pmb], [LMnb], out=Mn.rearrange("p c i -> p (c i)"), in_=pm[0:CH, :])
                yield
                px, pxb = pspool.next()
                for c in range(GC):
                    P.mm([LMnb, Xb], [pxb], px[0:CH, c * CH:(c + 1) * CH], Ln[:, c, :], X[:, c, :])
                yield
                P.dve("tensor_tensor", [pxb, Xb], [Xb], out=X.rearrange("p c i -> p (c i)"), in0=px[0:CH, :],
                      in1=X.rearrange("p c i -> p (c i)"), op=ALU.add)
                Lc, Mc, LMcb = Ln, Mn, LMnb
                yield
            Kbg, Kbgb = pl["Kbg"].next()
            kd, kdb = pl["kd"].next()
            Vb, Vbb = pl["Vb"].next()
            for half in range(2):
                pk, pkb = pspool.next()
                pv, pvb = pspool.next()
                for c4 in range(4):
                    c = half * 4 + c4
                    cs = slice(c * CH, (c + 1) * CH)
                    P.mm([KV32b, id_b], [pkb], pk[0:CH, c4 * 128:(c4 + 1) * 128], KV32[:, 0, cs], ident128[:])
                    P.mm([KV32b, id_b], [pvb], pv[0:CH, c4 * 128:(c4 + 1) * 128], KV32[:, 1, cs], ident128[:])
                yield
                hs = slice(half * 4, half * 4 + 4)
                gs = slice(c0 + half * 4, c0 + half * 4 + 4)
                pk3 = pk[0:CH, :].rearrange("p (c d) -> p c d", d=128)
                pv3 = pv[0:CH, :].rearrange("p (c d) -> p c d", d=128)
                P.dve("tensor_tensor", [pkb, gate_b], [Kbgb], out=Kbg[:, hs, :], in0=pk3, in1=bc_last(bg[:, gs], 128), op=ALU.mult)
                P.dve("tensor_tensor", [pkb, gate_b], [kdb], out=kd[:, hs, :], in0=pk3, in1=bc_last(ekd[:, gs], 128), op=ALU.mult)
                P.dve("tensor_tensor", [pvb, gate_b], [Vbb], out=Vb[:, hs, :], in0=pv3, in1=bc_last(b_all[:, gs], 128), op=ALU.mult)
                yield
            if IDT == F32:
                Xm, Xmb = X, Xb
            else:
                Xm, Xmb = X, Xb
            U, Ub = pl["U"].next()
            WT, WTb = pl["WT"].next()
            for half in range(2):
                pu, pub = pspool.next()
                pw, pwb = pspool.next()
                for c4 in range(4):
                    c = half * 4 + c4
                    P.mm([Xmb, Vbb], [pub], pu[0:CH, c4 * 128:(c4 + 1) * 128], Xm[:, c, :], Vb[:, c, :])
                for c4 in range(4):
                    c = half * 4 + c4
                    P.mm([Xmb, Kbgb], [pwb], pw[:, c4 * CH:(c4 + 1) * CH], Kbg[:, c, :], Xm[:, c, :])
                yield
                P.act([pub], [Ub], U[:, half * 4:half * 4 + 4, :].rearrange("p c d -> p (c d)"), pu[0:CH, :], AF.Copy)
                P.dve("tensor_copy", [pwb], [WTb], out=WT[:, half * 4:half * 4 + 4, :].rearrange("p c i -> p (c i)"),
                      in_=pw[:, 0:4 * CH])
                yield
            H[(hh, gi)] = dict(QT=QT, QTb=QTb, aT=aT, aTb=aTb, kd=kd, kdb=kdb, U=U, Ub=Ub, WT=WT, WTb=WTb, zt=zt, ztb=ztb)

        def rec(hh, gi, H):
            pl = PL[hh]
            pspool = rec_ps[hh]
            h = H.pop((hh, gi))
            QT, QTb, aT, aTb, kd, kdb = h["QT"], h["QTb"], h["aT"], h["aTb"], h["kd"], h["kdb"]
            U, Ub, WT, WTb, zt, ztb = h["U"], h["Ub"], h["WT"], h["WTb"], h["zt"], h["ztb"]
            c0 = hh * NCH + gi * GC
            ot, otb = pl["o"].next()
            for c in range(GC):
                col = c0 + c
                cs = slice(c * CH, (c + 1) * CH)
                pws, pwsb = pspool.next()
                P.mm([WTb, Sbf_b[hh]], [pwsb], pws[0:CH, 0:128], WT[:, c, :], Sbf[:, hh, :])
                yield
                vn, vnb = pl["vn"].next()
                P.dve("tensor_tensor", [Ub, pwsb], [vnb], out=vn, in0=U[:, c, :], in1=pws[0:CH, 0:128], op=ALU.subtract)
                yield
                P.mm([QTb, Sbf_b[hh]], [pwsb], pws[0:CH, 128:256], QT[:, cs], Sbf[:, hh, :])
                P.mm([aTb, vnb], [pwsb], pws[0:CH, 256:384], aT[:, c, :], vn)
                pss, pssb = pspool.next()
                P.mm([kdb, vnb], [pssb], pss[:, 0:128], kd[:, c, :], vn)
                yield
                P.dve("scalar_tensor_tensor", [pssb, S_b[hh], gate_b], [S_b[hh]], out=S32[:, hh, :], in0=S32[:, hh, :],
                      scalar=egl[:, col:col + 1], in1=pss[:, 0:128], op0=ALU.mult, op1=ALU.add)
                yield
                P.act([S_b[hh]], [Sbf_b[hh]], Sbf[:, hh, :], S32[:, hh, :], AF.Copy)
                t1, t1b = pl["t1"].next()
                P.pool("tensor_scalar", [pwsb, gate_b], [t1b], out=t1, in0=pws[0:CH, 128:256], scalar1=egc[:, col:col + 1],
                       scalar2=None, op0=ALU.mult) if False else P.dve(
                    "tensor_scalar", [pwsb, gate_b], [t1b], out=t1, in0=pws[0:CH, 128:256], scalar1=egc[:, col:col + 1],
                    scalar2=None, op0=ALU.mult)
                P.dve("tensor_tensor", [t1b, pwsb], [otb], out=ot[:, c, :], in0=t1, in1=pws[0:CH, 256:384], op=ALU.add)
                yield
            stt, sttb = pl["st"].next()
            sq2, sq2b = pl["sq2"].next()
            P.pool("tensor_tensor", [otb], [sq2b], out=sq2, in0=ot, in1=ot, op=ALU.mult)
            yield
            P.dve("tensor_reduce", [sq2b], [sttb], out=stt[:, 0, :], in_=sq2, axis=mybir.AxisListType.X, op=ALU.add)
            P.dve("tensor_scalar", [sttb], [sttb], out=stt[:, 1, :], in0=stt[:, 0, :], scalar1=1.0 / 128, scalar2=EPS,
                  op0=ALU.mult, op1=ALU.add)
            yield
            P.act([sttb], [sttb], stt[:, 1, :], stt[:, 1, :], AF.Sqrt)
            yield
            P.dve("reciprocal", [sttb], [sttb], out=stt[:, 1, :], in_=stt[:, 1, :])
            yield
            P.pool("tensor_tensor", [otb, sttb], [otb], out=ot, in0=ot, in1=bc_last(stt[:, 1, :], 128), op=ALU.mult)
            P.pool("tensor_tensor", [otb, ztb], [otb], out=ot, in0=ot, in1=zt, op=ALU.mult)
            outs.append(P.dma("sync", [otb], [], o_out[:, gi * GC:(gi + 1) * GC, hh, :], ot))
            yield

        def interleave(gens):
            gens = list(gens)
            while gens:
                for g_ in list(gens):
                    try:
                        next(g_)
                    except StopIteration:
                        gens.remove(g_)

        H = {}
        interleave([prep(hh, 0, H) for hh in range(HPC)])
        for gi in range(NG):
            gens = [rec(hh, gi, H) for hh in range(HPC)]
            if gi + 1 < NG:
                gens += [prep(hh, gi + 1, H) for hh in range(HPC)]
            interleave(gens)
        P.emit(final_wait=outs)
    return nc


def dn_consts():
    i = np.arange(CH)
    tri = (i[:, None] <= i[None, :])
    tris = (i[:, None] > i[None, :])
    ident = np.eye(CH, dtype=bool)
    strict = (i[:, None] > i[None, :])
    inclT = (i[None, :] >= i[:, None])
    return np.ascontiguousarray(np.stack([tri, tris, ident, strict, inclT], 1).astype(np.float32))


DILS = (1, 4, 16)
HALF = SEQ // 2
HALO = 2048
NVB = 48


def build_dil():
    import contextlib
    nc = bass.Bass("TRN2", target_bir_lowering=False)
    di = lambda name, shape: nc.dram_tensor(name, shape, F32, kind="ExternalInput").ap()
    q_in = di("q", [3, 128, HALF])
    k_in = di("k", [3, 128, HALO + HALF])
    v_in = di("vblk", [128, 3, NVB, 128])
    bias_in = di("bias", [128, 3, 384])
    o_out = nc.dram_tensor("oT", [128, HALF], F32, kind="ExternalOutput").ap()
    P = Prog(nc)
    with contextlib.ExitStack() as st:
        sb = lambda name, shape, dt: st.enter_context(nc.sbuf_tensor(name, shape, dt))
        ps = [st.enter_context(nc.psum_tensor("ps%d" % i, [128, 512], F32)) for i in range(8)]
        pspool = Pool(ps)
        qT = sb("qT", [128, 3, HALF], BF16)
        kT = sb("kT", [128, 3, HALO + HALF], BF16)
        vb = sb("vb", [128, 3, NVB, 128], BF16)
        bias = sb("bias_sb", [128, 3, 384], F32)
        ones = sb("ones", [128, 128], BF16)
        accO = sb("accO", [128, HALF], F32)
        accD = sb("accD", [128, HALF], F32)
        qb, kb, vbb, bb, ob, accb = [[Buf("q%d" % g) for g in range(3)], [Buf("k%d" % g) for g in range(3)],
                                     [Buf("v%d" % g) for g in range(3)], Buf("bias"), Buf("ones"), Buf("acc")]
        for g in range(3):
            P.dma("gpsimd", [], [qb[g]], qT[:, g, :], q_in[g])
            P.dma("gpsimd", [], [kb[g]], kT[:, g, :], k_in[g])
            P.dma("gpsimd", [], [vbb[g]], vb[:, g], v_in[:, g])
        P.dma("sync", [], [bb], bias[:], bias_in)
        P.pool("memset", [], [ob], ap=ones[:], constant=1.0)
        tmp_t = sb("tmp", [128, 2, 256], F32)
        tmp_p = Pool([tmp_t[:, i, :] for i in range(2)])
        E_t = sb("Et", [128, 3, 256], BF16)
        E_p = Pool([E_t[:, i, :] for i in range(3)])

        def strided(t3, g, start, d, n):
            base = t3[:, g, start:start + 1]
            a = [list(x) for x in base.ap]
            return bass.AP(base.tensor, base.offset, [a[0], [d, n]])

        for g, d in enumerate(DILS):
            nstream = d
            nblk = HALF // (128 * d)
            vi = 0
            for c in range(nstream):
                po = pd = None
                for m in range(-1, nblk):
                    kcol = HALO + 128 * m * d + c
                    K_ap = strided(kT, g, kcol, d, 128)
                    if m == -1:
                        q0, nq, boff = c, 128, 256
                    elif m == nblk - 1:
                        q0, nq, boff = 128 * m * d + c, 128, 0
                    else:
                        q0, nq, boff = 128 * m * d + c, 256, 0
                    Q_ap = strided(qT, g, q0, d, nq)
                    pss, pssb = pspool.next()
                    P.mm([kb[g], qb[g]], [pssb], pss[:, 0:nq], K_ap, Q_ap)
                    tt, tb = tmp_p.next()
                    P.dve("scalar_tensor_tensor", [pssb, bb], [tb], out=tt[:, 0:nq], in0=pss[:, 0:nq], scalar=SCALE,
                          in1=bias[:, g, boff:boff + nq], op0=ALU.mult, op1=ALU.add)
                    Et, Eb = E_p.next()
                    P.act([tb], [Eb], Et[:, 0:nq], tt[:, 0:nq], AF.Exp)
                    V_ap = vb[:, g, vi, :]
                    vi += 1
                    if m >= 0:
                        P.mm([vbb[g], Eb], [pob], po[:, 0:128], V_ap, Et[:, 0:128], start=False, stop=True)
                        P.mm([ob, Eb], [pdb], pd[:, 0:128], ones[:], Et[:, 0:128], start=False, stop=True)
                        cols = strided(accO, None, 0, 1, 1) if False else None
                        oc = 128 * m * d + c
                        aO = bass.AP(accO[:, oc:oc + 1].tensor, accO[:, oc:oc + 1].offset,
                                     [list(accO[:, oc:oc + 1].ap[0]), [d, 128]])
                        aD = bass.AP(accD[:, oc:oc + 1].tensor, accD[:, oc:oc + 1].offset,
                                     [list(accD[:, oc:oc + 1].ap[0]), [d, 128]])
                        if g == 0:
                            P.act([pob], [accb], aO, po[:, 0:128], AF.Copy)
                            P.dve("tensor_copy", [pdb], [accb], out=aD, in_=pd[:, 0:128])
                        else:
                            P.dve("tensor_tensor", [pob, accb], [accb], out=aO, in0=aO, in1=po[:, 0:128], op=ALU.add)
                            P.dve("tensor_tensor", [pdb, accb], [accb], out=aD, in0=aD, in1=pd[:, 0:128], op=ALU.add)
                    if m < nblk - 1:
                        po, pob = pspool.next()
                        pd, pdb = pspool.next()
                        e0 = 0 if m == -1 else 128
                        P.mm([vbb[g], Eb], [pob], po[:, 0:128], V_ap, Et[:, e0:e0 + 128], start=True, stop=False)
                        P.mm([ob, Eb], [pdb], pd[:, 0:128], ones[:], Et[:, e0:e0 + 128], start=True, stop=False)
        outs = []
        for h in range(HALF // 512):
            sl = slice(h * 512, (h + 1) * 512)
            P.dve("reciprocal", [accb], [accb], out=accD[:, sl], in_=accD[:, sl])
            P.dve("tensor_tensor", [accb], [accb], out=accO[:, sl], in0=accO[:, sl], in1=accD[:, sl], op=ALU.mult)
            outs.append(P.dma("sync", [accb], [], o_out[:, sl], accO[:, sl]))
        P.emit(final_wait=outs)
    return nc


def alibi_slopes():
    return np.exp2(-8.0 * np.arange(1, 13, dtype=np.float32) / 12).astype(np.float32)


def dil_bias(j, s):
    sl = alibi_slopes()
    kq = np.arange(256)[None, :] - np.arange(128)[:, None]
    valid = (kq >= 0) & (kq <= 128)
    out = np.full((128, 3, 384), -1e30, np.float32)
    for g, d in enumerate(DILS):
        b = np.where(valid, -sl[4 * g + j] * (kq * d).astype(np.float32), np.float32(-1e30)).astype(np.float32)
        out[:, g, 0:256] = b
        if s > 0:
            out[:, g, 256:384] = b[:, 128:256]
    return out


def dil_layout(q, k, v, j, s):
    t0 = s * HALF
    qs = np.stack([q[t0:t0 + HALF, 4 * g + j, :].T for g in range(3)])
    kext = np.zeros((3, 128, HALO + HALF), np.float32)
    vblk = np.zeros((128, 3, NVB, 128), np.float32)
    for g, d in enumerate(DILS):
        h = 4 * g + j
        lo = t0 - HALO
        src_lo = max(lo, 0)
        kext[g][:, src_lo - lo:] = k[src_lo:t0 + HALF, h, :].T
        nblk = HALF // (128 * d)
        vi = 0
        for c in range(d):
            for m in range(-1, nblk):
                tok = t0 + (128 * m + np.arange(128)) * d + c
                if tok[0] >= 0:
                    vblk[:, g, vi, :] = v[tok, h, :]
                vi += 1
    return {"q": np.ascontiguousarray(qs), "k": kext, "vblk": vblk, "bias": dil_bias(j, s)}


_NC_CACHE = {}


def _prog(key, fn):
    if key not in _NC_CACHE:
        _NC_CACHE[key] = fn()
    return _NC_CACHE[key]


def _run(nc, in_maps):
    res = run_bass_kernel_spmd(nc, in_maps, core_ids=list(range(len(in_maps))))
    return res.results


def kernel(x, mem, norm_gains, ffn_w_in, ffn_w_out, mem_norm_gain, w_mem_kv, dn_w_in, dn_conv, dn_a_log,
           dn_dt_bias, dn_o_norm, dn_w_out, kv_norm_gain, w_kv, dil_w_in, dil_w_out):
    f32 = np.float32
    x = np.asarray(x, f32)
    mem = np.asarray(mem, f32)
    A = lambda a: np.ascontiguousarray(np.asarray(a, f32))
    norm_gains, ffn_w_in, ffn_w_out = A(norm_gains), A(ffn_w_in), A(ffn_w_out)
    mem_norm_gain, w_mem_kv, dn_w_in, dn_conv = A(mem_norm_gain), A(w_mem_kv), A(dn_w_in), A(dn_conv)
    dn_a_log, dn_dt_bias, dn_o_norm, dn_w_out = A(dn_a_log), A(dn_dt_bias), A(dn_o_norm), A(dn_w_out)
    kv_norm_gain, w_kv, dil_w_in, dil_w_out = A(kv_norm_gain), A(w_kv), A(dil_w_in), A(dil_w_out)
    kinds = ["dn", "dn", "dil", "dil"]
    memT = np.ascontiguousarray(mem[0].T)
    xs = [np.ascontiguousarray(x[0, i * TOK:(i + 1) * TOK].T) for i in range(NCORES)]
    consts = dn_consts()
    o_in = None
    ksh = vsh = None
    for stage in range(5):
        prev = kinds[stage - 1] if stage > 0 else None
        nxt = kinds[stage] if stage < 4 else None
        has_kv = (stage == 2)
        nc = _prog(("ts", prev, nxt, has_kv), lambda: build_ts(prev, nxt, has_kv))
        g14 = np.zeros((14, D), f32)
        common = {}
        if prev is not None:
            lp = stage - 1
            g14[0:6] = norm_gains[lp]
            common.update(w_o=(dn_w_out[lp] if prev == "dn" else dil_w_out[lp - 2]),
                          f2_in=ffn_w_in[lp, 1], f2_out=ffn_w_out[lp, 1])
        if nxt is not None:
            ln = stage
            g14[6:12] = norm_gains[ln]
            g14[12] = mem_norm_gain[ln]
            g14[13] = kv_norm_gain
            common.update(f1_in=ffn_w_in[ln, 0], f1_out=ffn_w_out[ln, 0], memT=memT, w_mkv=w_mem_kv[ln])
            if nxt == "dn":
                common.update(w_p=dn_w_in[ln],
                              dnc=np.ascontiguousarray(np.tile(np.concatenate([dn_dt_bias[ln], dn_a_log[ln]])[None], (128, 1))))
            else:
                common.update(w_p=dil_w_in[ln - 2])
            if has_kv:
                common.update(w_kv=w_kv)
        common["gains"] = gains_layout(g14)
        ims = []
        for i in range(NCORES):
            im = dict(common)
            im["x_in"] = xs[i]
            if prev is not None:
                im["o_in"] = o_in[i]
            ims.append(im)
        res = _run(nc, ims)
        xs = [r["x_out"] for r in res]
        if nxt is None:
            break
        omem = [r["omemT"] for r in res]
        if nxt == "dn":
            ln = stage
            qkvT = np.concatenate([r["qkvT"] for r in res], axis=1)
            zf = np.concatenate([r["z"] for r in res], axis=0)
            gb = np.concatenate([r["gb"] for r in res], axis=0)
            NCH = SEQ // CH
            q4 = qkvT.reshape(3, 12, 128, SEQ)
            z4 = zf.reshape(NCH, CH, 12, 128)
            g3 = gb[:, 0:12].reshape(NCH, CH, 12)
            b3 = gb[:, 12:24].reshape(NCH, CH, 12)
            cw = dn_conv[ln].reshape(4, 3, 12, 128)
            ncd = _prog(("dn",), lambda: build_dn())
            ims = []
            for c in range(NCORES):
                h0 = 2 * (c % 6)
                ims.append({
                    "qkv": np.ascontiguousarray(q4[:, h0:h0 + 2].transpose(1, 0, 2, 3)),
                    "convw": np.ascontiguousarray(cw[:, :, h0:h0 + 2, :].transpose(3, 2, 1, 0).reshape(128, 24)),
                    "zc": np.ascontiguousarray(z4[:, :, h0:h0 + 2].transpose(1, 0, 2, 3)),
                    "gcol": np.ascontiguousarray(g3[:, :, h0:h0 + 2].transpose(1, 2, 0)),
                    "bcol": np.ascontiguousarray(b3[:, :, h0:h0 + 2].transpose(1, 2, 0)),
                    "onorm": np.ascontiguousarray(np.tile(dn_o_norm[ln][None], (CH, 1))),
                    "consts": consts,
                })
            res = _run(ncd, ims)
            of = np.zeros((SEQ, 12, 128), f32)
            for c in range(6):
                of[:, 2 * c:2 * c + 2] = res[c]["o"].transpose(1, 0, 2, 3).reshape(SEQ, 2, 128)
            of = of.reshape(SEQ, 1536)
        else:
            qT = np.concatenate([r["qT"] for r in res], axis=1)
            if has_kv:
                ksh = np.ascontiguousarray(np.concatenate([r["kT"] for r in res], axis=1).T).reshape(SEQ, 12, 128)
                vsh = np.concatenate([r["v"] for r in res], axis=0).reshape(SEQ, 12, 128)
            qf = np.ascontiguousarray(qT.T).reshape(SEQ, 12, 128)
            ncl = _prog(("dil",), lambda: build_dil())
            ims = [dil_layout(qf, ksh, vsh, c // 2, c % 2) for c in range(NCORES)]
            res = _run(ncl, ims)
            of = np.zeros((SEQ, 4, 128), f32)
            for c in range(NCORES):
                j, s = c // 2, c % 2
                of[s * HALF:(s + 1) * HALF, j] = res[c]["oT"].T
            of = of.reshape(SEQ, 512)
        o_in = [np.ascontiguousarray(np.concatenate([of[i * TOK:(i + 1) * TOK].T, omem[i]], axis=0)) for i in range(NCORES)]
    out = np.concatenate([xi.T for xi in xs], axis=0)[None]
    return np.ascontiguousarray(out.astype(f32))
```
